# Optimizing a Trainium2 kernel written in Bass

```python
import jax, jax.numpy as jnp
from jax import lax
import numpy as np

D_MODEL = 2048
BATCH = 16
SEQ = 2048
DEPTH = 4

N_META = 16
N_MIXERS = 2
D_FF = 5504
EPS = 1e-6
N_MLA_LAYERS = (DEPTH + 1) // 2
N_SSM_LAYERS = DEPTH // 2
MLA_HEADS = 16
Q_LORA = 512
KV_LORA = 512
QK_NOPE = 128
QK_ROPE = 64
V_HEAD = 128
ROPE_THETA = 10000.0
Q_BLOCK = 128
SSM_INNER = 2 * D_MODEL
SSM_HEAD_DIM = 64
SSM_HEADS = SSM_INNER // SSM_HEAD_DIM
SSM_GROUPS = 8
SSM_HEADS_PER_GROUP = SSM_HEADS // SSM_GROUPS
SSM_STATE = 128
SSM_CONV = 4
SSM_CHUNK = 128
SSM_CONV_DIM = SSM_INNER + 2 * SSM_GROUPS * SSM_STATE

kernel_name = 'hybrid_mla_ssd_macaron_meta'


def rms_norm(x, g):
    xf = x.astype(jnp.float32)
    y = xf * lax.rsqrt(jnp.mean(xf * xf, axis=-1, keepdims=True) + EPS)
    return (y * g.astype(jnp.float32)).astype(x.dtype)


def swiglu_ffn(h, w_in, w_out):
    gate, up = jnp.split(h @ w_in, 2, axis=-1)
    return (jax.nn.silu(gate) * up) @ w_out


def rope_tables(length):
    inv_freq = 1.0 / (ROPE_THETA ** (jnp.arange(0, QK_ROPE, 2, dtype=jnp.float32) / QK_ROPE))
    ang = jnp.arange(length, dtype=jnp.float32)[:, None] * inv_freq[None, :]
    return jnp.cos(ang), jnp.sin(ang)


def apply_rope(t, cos, sin):
    tf = t.astype(jnp.float32)
    t1, t2 = jnp.split(tf, 2, axis=-1)
    return jnp.concatenate([t1 * cos - t2 * sin, t2 * cos + t1 * sin], axis=-1).astype(t.dtype)


def query_blocks(length):
    return [(0, N_META)] + [(s, min(s + Q_BLOCK, length)) for s in range(N_META, length, Q_BLOCK)]


def mla_mixer(h, w_in, q_norm, w_uq, kv_norm, w_ukv, w_o):
    b, L, _ = h.shape
    c_q, c_kv, k_rope = jnp.split(h @ w_in, [Q_LORA, Q_LORA + KV_LORA], axis=-1)
    q = (rms_norm(c_q, q_norm) @ w_uq).reshape(b, L, MLA_HEADS, QK_NOPE + QK_ROPE)
    q_nope, q_rope = q[..., :QK_NOPE], q[..., QK_NOPE:]
    kv = (rms_norm(c_kv, kv_norm) @ w_ukv).reshape(b, L, MLA_HEADS, QK_NOPE + V_HEAD)
    k_nope, v = kv[..., :QK_NOPE], kv[..., QK_NOPE:]
    cos, sin = rope_tables(L)
    q_rope = apply_rope(q_rope, cos[:, None, :], sin[:, None, :])
    k_rope = apply_rope(k_rope, cos, sin)
    scale = (QK_NOPE + QK_ROPE) ** -0.5
    outs = []
    for start, end in query_blocks(L):
        s = (jnp.einsum('bqhd,bkhd->bhqk', q_nope[:, start:end], k_nope[:, :end])
             + jnp.einsum('bqhr,bkr->bhqk', q_rope[:, start:end], k_rope[:, :end])).astype(jnp.float32) * scale
        causal = jnp.arange(end)[None, :] <= jnp.arange(start, end)[:, None]
        p = jax.nn.softmax(jnp.where(causal, s, -jnp.inf), axis=-1)
        outs.append(jnp.einsum('bhqk,bkhd->bqhd', p.astype(v.dtype), v[:, :end]))
    o = jnp.concatenate(outs, axis=1).reshape(b, L, MLA_HEADS * V_HEAD)
    return o @ w_o


def causal_depthwise_conv(u, w, bias):
    out = lax.conv_general_dilated(u, w[:, None, :], window_strides=(1,), padding=[(SSM_CONV - 1, 0)],
                                   dimension_numbers=('NWC', 'WIO', 'NWC'), feature_group_count=u.shape[-1])
    return out + bias


def ssd_chunked_scan(xs, dt, a, bm, cm):
    b = xs.shape[0]
    pad_front = (-N_META) % SSM_CHUNK

    def chunks(t):
        t = t.astype(jnp.float32)
        t = jnp.pad(t, [(0, 0), (pad_front, 0)] + [(0, 0)] * (t.ndim - 2))
        return jnp.moveaxis(t.reshape(b, -1, SSM_CHUNK, *t.shape[2:]), 1, 0)

    causal = jnp.tril(jnp.ones((SSM_CHUNK, SSM_CHUNK), dtype=bool))[None, :, :, None, None]

    def step(state, inp):
        x_c, dt_c, b_c, c_c = inp
        a_cs = jnp.cumsum(dt_c * a, axis=1)
        seg = a_cs[:, :, None] - a_cs[:, None, :]
        decay = jnp.exp(jnp.where(causal, seg, -jnp.inf))
        xdt = x_c * dt_c[..., None]
        cb = jnp.einsum('blgn,bsgn->blsg', c_c, b_c)
        y_diag = jnp.einsum('blsgr,bsgrp->blgrp', cb[..., None] * decay, xdt)
        y_off = jnp.einsum('blgn,bgrpn->blgrp', c_c, state) * jnp.exp(a_cs)[..., None]
        to_end = jnp.exp(a_cs[:, -1:] - a_cs)
        new_state = (state * jnp.exp(a_cs[:, -1])[..., None, None]
                     + jnp.einsum('blgn,blgrp->bgrpn', b_c, xdt * to_end[..., None]))
        return new_state, y_diag + y_off

    state0 = jnp.zeros((b, SSM_GROUPS, SSM_HEADS_PER_GROUP, SSM_HEAD_DIM, SSM_STATE), jnp.float32)
    _, ys = lax.scan(step, state0, (chunks(xs), chunks(dt), chunks(bm), chunks(cm)))
    ys = jnp.moveaxis(ys, 0, 1).reshape(b, -1, SSM_GROUPS, SSM_HEADS_PER_GROUP, SSM_HEAD_DIM)
    return ys[:, pad_front:]


def mamba2_mixer(h, w_in, conv_w, conv_b, dt_bias, a_log, d_skip, norm_w, w_out):
    b, L, _ = h.shape
    G, R, P, N = SSM_GROUPS, SSM_HEADS_PER_GROUP, SSM_HEAD_DIM, SSM_STATE
    z, xbc, dt = jnp.split(h @ w_in, [SSM_INNER, SSM_INNER + SSM_CONV_DIM], axis=-1)
    xbc = jax.nn.silu(causal_depthwise_conv(xbc, conv_w, conv_b))
    xs, bm, cm = jnp.split(xbc, [SSM_INNER, SSM_INNER + G * N], axis=-1)
    xs = xs.reshape(b, L, G, R, P)
    bm = bm.reshape(b, L, G, N)
    cm = cm.reshape(b, L, G, N)
    dt = jax.nn.softplus((dt + dt_bias).astype(jnp.float32)).reshape(b, L, G, R)
    a = -jnp.exp(a_log.astype(jnp.float32)).reshape(G, R)
    y = ssd_chunked_scan(xs, dt, a, bm, cm) + d_skip.astype(jnp.float32).reshape(G, R)[..., None] * xs.astype(jnp.float32)
    y = y.reshape(b, L, SSM_INNER) * jax.nn.silu(z.astype(jnp.float32))
    y = rms_norm(y.reshape(b, L, G, SSM_INNER // G), norm_w.reshape(G, SSM_INNER // G))
    return y.reshape(b, L, SSM_INNER).astype(h.dtype) @ w_out


def setup_inputs(seed: int = 0) -> dict:
    key = jax.random.key(seed)
    ks = iter(jax.random.split(key, 32))
    f32 = jnp.float32

    def dense(shape, fan_in):
        return jax.random.normal(next(ks), shape, f32) * fan_in ** -0.5

    def gain(shape):
        return 1.0 + 0.02 * jax.random.normal(next(ks), shape, f32)

    nA, nB = N_MLA_LAYERS, N_SSM_LAYERS
    dt0 = jnp.exp(jax.random.uniform(next(ks), (nB, SSM_HEADS), f32, jnp.log(1e-3), jnp.log(1e-1)))
    return {
        'x': jax.random.normal(next(ks), (BATCH, SEQ, D_MODEL), f32),
        'meta_tokens': jax.random.normal(next(ks), (N_META, D_MODEL), f32),
        'norm_ffn1': gain((DEPTH, D_MODEL)),
        'ffn1_w_in': dense((DEPTH, D_MODEL, 2 * D_FF), D_MODEL),
        'ffn1_w_out': dense((DEPTH, D_FF, D_MODEL), D_FF),
        'norm_mix': gain((DEPTH, D_MODEL)),
        'norm_ffn2': gain((DEPTH, D_MODEL)),
        'ffn2_w_in': dense((DEPTH, D_MODEL, 2 * D_FF), D_MODEL),
        'ffn2_w_out': dense((DEPTH, D_FF, D_MODEL), D_FF),
        'mla_w_in': dense((nA, D_MODEL, Q_LORA + KV_LORA + QK_ROPE), D_MODEL),
        'mla_q_norm': gain((nA, Q_LORA)),
        'mla_w_uq': dense((nA, Q_LORA, MLA_HEADS * (QK_NOPE + QK_ROPE)), Q_LORA),
        'mla_kv_norm': gain((nA, KV_LORA)),
        'mla_w_ukv': dense((nA, KV_LORA, MLA_HEADS * (QK_NOPE + V_HEAD)), KV_LORA),
        'mla_w_o': dense((nA, MLA_HEADS * V_HEAD, D_MODEL), MLA_HEADS * V_HEAD),
        'ssm_w_in': dense((nB, D_MODEL, SSM_INNER + SSM_CONV_DIM + SSM_HEADS), D_MODEL),
        'ssm_conv_w': dense((nB, SSM_CONV, SSM_CONV_DIM), SSM_CONV),
        'ssm_conv_b': 0.02 * jax.random.normal(next(ks), (nB, SSM_CONV_DIM), f32),
        'ssm_dt_bias': dt0 + jnp.log(-jnp.expm1(-dt0)),
        'ssm_a_log': jnp.log(jax.random.uniform(next(ks), (nB, SSM_HEADS), f32, 1.0, 16.0)),
        'ssm_d': 1.0 + 0.1 * jax.random.normal(next(ks), (nB, SSM_HEADS), f32),
        'ssm_norm': gain((nB, SSM_INNER)),
        'ssm_w_out': dense((nB, SSM_INNER, D_MODEL), SSM_INNER),
        'final_norm': gain((D_MODEL,)),
    }


def reference(x, meta_tokens, norm_ffn1, ffn1_w_in, ffn1_w_out, norm_mix, norm_ffn2, ffn2_w_in, ffn2_w_out,
              mla_w_in, mla_q_norm, mla_w_uq, mla_kv_norm, mla_w_ukv, mla_w_o,
              ssm_w_in, ssm_conv_w, ssm_conv_b, ssm_dt_bias, ssm_a_log, ssm_d, ssm_norm, ssm_w_out,
              final_norm):
    b = x.shape[0]
    meta = jnp.broadcast_to(meta_tokens[None].astype(x.dtype), (b, N_META, x.shape[-1]))
    h = jnp.concatenate([meta, x], axis=1)
    for i in range(DEPTH):
        h = h + 0.5 * swiglu_ffn(rms_norm(h, norm_ffn1[i]), ffn1_w_in[i], ffn1_w_out[i])
        u = rms_norm(h, norm_mix[i])
        j = i // N_MIXERS
        if i % N_MIXERS == 0:
            h = h + mla_mixer(u, mla_w_in[j], mla_q_norm[j], mla_w_uq[j], mla_kv_norm[j], mla_w_ukv[j], mla_w_o[j])
        else:
            h = h + mamba2_mixer(u, ssm_w_in[j], ssm_conv_w[j], ssm_conv_b[j], ssm_dt_bias[j], ssm_a_log[j],
                                 ssm_d[j], ssm_norm[j], ssm_w_out[j])
        h = h + 0.5 * swiglu_ffn(rms_norm(h, norm_ffn2[i]), ffn2_w_in[i], ffn2_w_out[i])
    return rms_norm(h, final_norm)[:, N_META:]
```

```python
import numpy as np
from contextlib import ExitStack
import concourse.bass as bass
import concourse.mybir as mybir

F32 = mybir.dt.float32
BF16 = mybir.dt.bfloat16
AF = mybir.ActivationFunctionType
ALU = mybir.AluOpType
AX = mybir.AxisListType

ENGS = ['pe', 'act', 'dve', 'pool', 'sp']
NS_DMA = 12


class Ins:
    __slots__ = ('fn', 'waits', 'signal', 'dma', 'dsem', 'dval', 'fn_was_real')

    def __init__(self, fn, dma=False):
        self.fn = fn
        self.fn_was_real = fn is not None
        self.waits = []
        self.signal = False
        self.dma = dma
        self.dsem = None
        self.dval = 0


class Prog:
    def __init__(self, nc):
        self.nc = nc
        self.q = {e: [] for e in ENGS}
        self.lastw = {}
        self.readers = {}
        self.seen = {e: {} for e in ENGS}
        self.ndma = {e: 0 for e in ENGS}
        self.dma_tok = {e: {} for e in ENGS}

    def _need(self, eng, ins, tok):
        if tok is None:
            return
        kind = tok[0]
        if kind == 'c':
            _, f, i = tok
            if f == eng and eng == 'pe':
                return
            key = ('c', f)
            if self.seen[eng].get(key, -1) >= i:
                return
            self.seen[eng][key] = i
            self.q[f][i].signal = True
            ins.waits.append(tok)
        else:
            _, qn, j = tok
            s = j % NS_DMA
            v = 16 * (j // NS_DMA + 1)
            key = ('d', qn, s)
            if self.seen[eng].get(key, 0) >= v:
                return
            self.seen[eng][key] = v
            ins.waits.append(tok)

    def _add(self, eng, fn, r, w, dma):
        ins = Ins(fn, dma)
        idx = len(self.q[eng])
        if dma:
            j = self.ndma[eng]
            self.ndma[eng] += 1
            tok = ('d', eng, j)
            if j >= NS_DMA:
                self._need(eng, ins, ('d', eng, j - NS_DMA))
        else:
            tok = ('c', eng, idx)
        for res in r:
            self._need(eng, ins, self.lastw.get(res))
        for res in w:
            self._need(eng, ins, self.lastw.get(res))
            rd = self.readers.get(res)
            if rd is not None:
                for f, i in rd[0].items():
                    if (not dma) and f == eng:
                        continue
                    self._need(eng, ins, ('c', f, i))
                for t in rd[1]:
                    self._need(eng, ins, t)
        for res in r:
            rd = self.readers.setdefault(res, ({}, []))
            if dma:
                rd[1].append(tok)
            else:
                rd[0][eng] = idx
        for res in w:
            self.lastw[res] = tok
            self.readers.pop(res, None)
        self.q[eng].append(ins)
        return tok

    def op(self, eng, fn, r=(), w=()):
        return self._add(eng, fn, r, w, False)

    def dma(self, eng, fn, r=(), w=()):
        return self._add(eng, fn, r, w, True)

    def _setup(self):
        nc = self.nc
        self.csem = {e: nc.alloc_semaphore('c_' + e) for e in ENGS}
        self.dsem = {e: [nc.alloc_semaphore('d_%s_%d' % (e, i)) for i in range(NS_DMA)]
                     for e in ('sp', 'pool', 'act')}
        self.emitted = {e: 0 for e in ENGS}
        self.cnt = {e: [] for e in ENGS}
        self.sig = {e: 0 for e in ENGS}
        self.jd = {e: 0 for e in ENGS}
        self.engobj = {'pe': nc.tensor, 'act': nc.scalar, 'dve': nc.vector, 'pool': nc.gpsimd, 'sp': nc.sync}

    def flush(self):
        nc = self.nc
        if not hasattr(self, 'csem'):
            self._setup()
        ctoks = []
        for f in ENGS:
            n = len(self.q[f])
            if n > self.emitted[f]:
                ctoks.append(('c', f, n - 1))
        dtoks = []
        for qn in ('sp', 'pool', 'act'):
            n = self.ndma[qn]
            for j in range(max(0, n - NS_DMA), n):
                dtoks.append(('d', qn, j))
        bars = {}
        for e in ENGS:
            ins = Ins(None)
            for t in ctoks:
                if t[1] != e:
                    self._need(e, ins, t)
            for t in dtoks:
                self._need(e, ins, t)
            bars[e] = ins
        for e in ENGS:
            self.q[e].append(bars[e])
        for e in ENGS:
            c = self.sig[e]
            for ins in self.q[e][self.emitted[e]:]:
                if ins.signal and not ins.dma and ins.fn is not None:
                    c += 1
                self.cnt[e].append(c)
            self.sig[e] = c

        def do_wait(eobj, tok):
            if tok[0] == 'c':
                _, f, i = tok
                eobj.wait_ge(self.csem[f], self.cnt[f][i])
            else:
                _, qn, j = tok
                eobj.wait_ge(self.dsem[qn][j % NS_DMA], 16 * (j // NS_DMA + 1))

        def run(ename):
            lo = self.emitted[ename]
            hi = len(self.q[ename])

            def body(eobj):
                for ins in self.q[ename][lo:hi]:
                    for tok in ins.waits:
                        do_wait(eobj, tok)
                    if ins.fn is None:
                        continue
                    bi = ins.fn(eobj)
                    if ins.dma:
                        bi.then_inc(self.dsem[ename][self.jd[ename] % NS_DMA], 16)
                        self.jd[ename] += 1
                    elif ins.signal:
                        bi.then_inc(self.csem[ename], 1)
            return body

        with nc.Block() as block:
            block.tensor(run('pe'))
            block.scalar(run('act'))
            block.vector(run('dve'))
            block.gpsimd(run('pool'))
            block.sync(run('sp'))
        for e in ENGS:
            self.emitted[e] = len(self.q[e])
            for ins in self.q[e]:
                ins.fn = None if ins.fn is None else ins.fn
        self.lastw_phase_clear()

    def lastw_phase_clear(self):
        self.lastw = {k: v for k, v in self.lastw.items() if v[0] == 'd'}
        self.readers = {k: ({}, v[1]) for k, v in self.readers.items() if v[1]}

    def nextbank(self):
        bs = getattr(self, 'bankset', None)
        if bs:
            i = getattr(self, '_bsi', 0)
            self._bsi = i + 1
            return bs[i % len(bs)]
        b = getattr(self, '_bank', 0)
        self._bank = (b + 1) % 8
        return b


def _mm(P, out, lhsT, rhs, start, stop, r, w):
    P.op('pe', lambda e: e.matmul(out, lhsT=lhsT, rhs=rhs, start=start, stop=stop), r, w)


def _tr(P, out, in_, ident, r, w):
    P.op('pe', lambda e: e.transpose(out, in_, ident), r, w)


def _act(P, out, in_, func, r, w, bias=None, scale=None, accum_out=None):
    kw = {}
    if bias is not None:
        kw['bias'] = bias
    if scale is not None:
        kw['scale'] = scale
    if accum_out is not None:
        kw['accum_out'] = accum_out
    P.op('act', lambda e: e.activation(out=out, in_=in_, func=func, **kw), r, w)


def _tt(P, eng, out, in0, in1, op, r, w):
    P.op(eng, lambda e: e.tensor_tensor(out=out, in0=in0, in1=in1, op=op), r, w)


def _ts(P, eng, out, in0, s1, s2, op0, op1, r, w):
    if s2 is None:
        P.op(eng, lambda e: e.tensor_scalar(out=out, in0=in0, scalar1=s1, scalar2=None, op0=op0), r, w)
    else:
        P.op(eng, lambda e: e.tensor_scalar(out=out, in0=in0, scalar1=s1, scalar2=s2, op0=op0, op1=op1), r, w)


def _stt(P, out, in0, scalar, in1, op0, op1, r, w):
    P.op('dve', lambda e: e.scalar_tensor_tensor(out=out, in0=in0, scalar=scalar, in1=in1, op0=op0, op1=op1), r, w)


def _cp(P, eng, out, in_, r, w):
    if eng == 'act':
        P.op('act', lambda e: e.activation(out=out, in_=in_, func=AF.Copy), r, w)
    else:
        P.op(eng, lambda e: e.tensor_copy(out=out, in_=in_), r, w)


def _red(P, out, in_, op, r, w):
    P.op('dve', lambda e: e.tensor_reduce(out=out, in_=in_, axis=AX.X, op=op), r, w)


def _recip(P, out, in_, r, w):
    P.op('dve', lambda e: e.reciprocal(out=out, in_=in_), r, w)


def _dma(P, eng, out, in_, r, w):
    P.dma(eng, lambda e: e.dma_start(out=out, in_=in_), r, w)


def _memset(P, eng, ap, val, w):
    P.op(eng, lambda e: e.memset(ap, val), (), w)


EPS = 1e-6
NEG = -30000.0
N_META = 16


class Cfg:
    def __init__(self, SEQ=2048, NSEQ=2, DEPTH=4, DM=2048, DFF=5504):
        self.SEQ, self.NSEQ, self.DEPTH, self.DM, self.DFF = SEQ, NSEQ, DEPTH, DM, DFF
        self.L = SEQ + N_META
        self.NT = 1 + SEQ // 128
        self.KC = DM // 128
        self.NF = DFF // 128
        self.BT = 7

    def tile(self, t):
        if t == 0:
            return 0, N_META
        return N_META + (t - 1) * 128, 128

    def blocks(self, seqs=None):
        tl = []
        for s in (range(self.NSEQ) if seqs is None else seqs):
            for t in range(1, self.NT):
                tl.append((s, t))
            tl.append((s, 0))
        out = []
        for b0 in range(0, len(tl), self.BT):
            blk = []
            c = 0
            for (s, t) in tl[b0:b0 + self.BT]:
                r0, rows = self.tile(t)
                blk.append((s, t, r0, rows, c))
                c += rows
            out.append(blk)
        return out


def col_groups(ncols, w=512):
    return [(c, min(c + w, ncols)) for c in range(0, ncols, w)]


class Ctx:
    pass


_UID = [0]


def _uname(name):
    _UID[0] += 1
    return '%s_u%d' % (name, _UID[0])


def hres(s, t):
    return 'H_%d_%d' % (s, t)


def norm_T(C, src_ap, src_res, rows, gbc, AT, at_res, col0, i):
    P, cfg = C.P, C.cfg
    DM, KC = cfg.DM, cfg.KC
    k = i % 2
    hb, hn = C.hbuf[k], C.hn[k]
    hbr, hnr = 'hbuf%d' % k, 'hn%d' % k
    _dma(P, 'sp', hb[:rows, :], src_ap, [src_res], [hbr])
    _act(P, C.sq[:rows, :], hb[:rows, :], AF.Square, [hbr], ['sq', 'ss%d' % k], accum_out=C.ss[:rows, k:k + 1])
    _act(P, C.rs[:rows, k:k + 1], C.ss[:rows, k:k + 1], AF.Sqrt, ['ss%d' % k], ['rs%d' % k], scale=1.0 / DM, bias=C.epsb[:rows, :])
    _recip(P, C.rstd[:rows, k:k + 1], C.rs[:rows, k:k + 1], ['rs%d' % k], ['rstd%d' % k])
    _stt(P, hn[:rows, :], hb[:rows, :], C.rstd[:rows, k:k + 1], gbc[:rows, :], ALU.mult, ALU.mult,
         [hbr, 'rstd%d' % k, 'gbc'], [hnr])
    transpose_into(C, hn, hnr, rows, KC, AT, at_res, col0)


def transpose_into(C, src, src_res, rows, nch, AT, at_res, col0, chw=128):
    P = C.P
    for g4 in range(0, nch, 4):
        n4 = min(4, nch - g4)
        b = P.nextbank()
        pb = C.psb[b]
        for j in range(n4):
            ch = g4 + j
            _tr(P, pb[:chw, j * 128:j * 128 + rows], src[:rows, ch * chw:(ch + 1) * chw], C.idb[:rows, :rows],
                [src_res, 'idb'], ['ps%d' % b])
        src_v = pb[:chw, 0:n4 * 128].rearrange("p (j c) -> p j c", c=128)[:, :, :rows]
        eng = 'act' if (C.evac_ctr % 2 == 0) else 'dve'
        C.evac_ctr += 1
        _cp(P, eng, AT[:chw, g4:g4 + n4, col0:col0 + rows], src_v, ['ps%d' % b], [at_res])


def gemm_tm(C, AT, at_res, KC, blk, w_r, N, epi, wbufs, wres, kstep=8, gw=512):
    P = C.P
    for (c0, c1) in col_groups(N, gw):
        n = c0 // gw
        w = c1 - c0
        banks = [P.nextbank() for _ in blk]
        for k0 in range(0, KC, kstep):
            k1 = min(KC, k0 + kstep)
            wb = C.wctr % len(wbufs)
            C.wctr += 1
            wt, wr_ = wbufs[wb], '%s%d' % (wres, wb)
            _dma(P, 'pool', wt[:, 0:k1 - k0, 0:w], w_r[:, k0:k1, c0:c1], [], [wr_])
            for kc in range(k0, k1):
                for ti, (s, t, r0, rows, col0) in enumerate(blk):
                    _mm(P, C.ps[banks[ti]][:rows, 0:w], AT[:, kc, col0:col0 + rows], wt[:, kc - k0, 0:w],
                        kc == 0, kc == KC - 1, [at_res, wr_], ['ps%d' % banks[ti]])
        for ti, te in enumerate(blk):
            epi(n, (c0, c1), te, C.ps[banks[ti]][:te[3], 0:w], 'ps%d' % banks[ti])


def residual_epi(C, H, coef):
    P = C.P

    def epi(n, cc, te, psap, psres):
        s, t, r0, rows, col0 = te
        k = C.hres_ctr % len(C.hres)
        C.hres_ctr += 1
        hr, hrr = C.hres[k], 'hres%d' % k
        w = cc[1] - cc[0]
        _dma(P, 'sp', hr[:rows, 0:w], H[s, r0:r0 + rows, cc[0]:cc[1]], [hres(s, t)], [hrr])
        _stt(P, hr[:rows, 0:w], psap, coef, hr[:rows, 0:w], ALU.mult, ALU.add, [psres, hrr], [hrr])
        _dma(P, 'sp', H[s, r0:r0 + rows, cc[0]:cc[1]], hr[:rows, 0:w], [hrr], [hres(s, t)])
    return epi


def ffn_phase(C, H, gain_d, win_r, wout_r):
    nc, P, cfg = C.nc, C.P, C.cfg
    DM, KC, NF = cfg.DM, cfg.KC, cfg.NF
    BC = cfg.BT * 128
    with ExitStack() as es:
        def A(name, shape, dt):
            return es.enter_context(nc.sbuf_tensor(_uname(name), shape, dt))
        gbc = A("f_gbc", [128, DM], F32)
        xnT = A("f_xnT", [128, KC, BC], BF16)
        actT = A("f_actT", [128, NF, BC], BF16)
        hb0 = A("f_hb0", [128, DM], F32)
        hb1 = A("f_hb1", [128, DM], F32)
        hn0 = A("f_hn0", [128, DM], BF16)
        hn1 = A("f_hn1", [128, DM], BF16)
        sq = A("f_sq", [128, DM], BF16)
        win0 = A("f_win0", [128, KC, 256], BF16)
        win1 = A("f_win1", [128, KC, 256], BF16)
        wo0 = A("f_wo0", [128, 8, 512], BF16)
        wo1 = A("f_wo1", [128, 8, 512], BF16)
        wo2 = A("f_wo2", [128, 8, 512], BF16)
        sg0 = A("f_sg0", [128, 512], F32)
        sg1 = A("f_sg1", [128, 512], F32)
        hr0 = A("f_hr0", [128, 512], F32)
        hr1 = A("f_hr1", [128, 512], F32)
        hr2 = A("f_hr2", [128, 512], F32)
        C.hbuf, C.hn, C.sq = [hb0, hb1], [hn0, hn1], sq
        C.hres = [hr0, hr1, hr2]
        wins = [win0, win1]
        sgs = [sg0, sg1]
        _dma(P, 'sp', gbc[:, :], gain_d.partition_broadcast(128), [], ['gbc'])
        epi = residual_epi(C, H, 0.5)
        for blk in cfg.blocks():
            ncols = sum(te[3] for te in blk)
            for i, (s, t, r0, rows, col0) in enumerate(blk):
                norm_T(C, H[s, r0:r0 + rows, :], hres(s, t), rows, gbc, xnT, 'xnT', col0, i)
            groups = col_groups(ncols)
            for f in range(NF):
                wb = f % 2
                wt, wr_ = wins[wb], 'win%d' % wb
                _dma(P, 'pool', wt[:, :, :], win_r[f], [], [wr_])
                for (c0, c1) in groups:
                    w = c1 - c0
                    bg, bu = P.nextbank(), P.nextbank()
                    for kc in range(KC):
                        _mm(P, C.ps[bg][:, 0:w], wt[:, kc, 0:128], xnT[:, kc, c0:c1], kc == 0, kc == KC - 1,
                            [wr_, 'xnT'], ['ps%d' % bg])
                    for kc in range(KC):
                        _mm(P, C.ps[bu][:, 0:w], wt[:, kc, 128:256], xnT[:, kc, c0:c1], kc == 0, kc == KC - 1,
                            [wr_, 'xnT'], ['ps%d' % bu])
                    k = C.sg_ctr % 2
                    C.sg_ctr += 1
                    _act(P, sgs[k][:, 0:w], C.ps[bg][:, 0:w], AF.Silu, ['ps%d' % bg], ['sg%d' % k])
                    _tt(P, 'dve', actT[:, f, c0:c1], sgs[k][:, 0:w], C.ps[bu][:, 0:w], ALU.mult,
                        ['sg%d' % k, 'ps%d' % bu], ['actT'])
            gemm_tm(C, actT, 'actT', NF, blk, wout_r, DM, epi, [wo0, wo1, wo2], 'wo')
        P.flush()


def out_phase(C, H, gain_d, out_d, raw=False):
    nc, P, cfg = C.nc, C.P, C.cfg
    DM = cfg.DM
    with ExitStack() as es:
        def A(name, shape, dt):
            return es.enter_context(nc.sbuf_tensor(_uname(name), shape, dt))
        gbc = A("o_gbc", [128, DM], F32)
        hb = [A("o_hb0", [128, DM], F32), A("o_hb1", [128, DM], F32)]
        ob = [A("o_ob0", [128, DM], F32), A("o_ob1", [128, DM], F32)]
        sq = A("o_sq", [128, DM], BF16)
        _dma(P, 'sp', gbc[:, :], gain_d.partition_broadcast(128), [], ['gbc'])
        i = 0
        for s in range(cfg.NSEQ):
            for t in range(1, cfg.NT):
                r0, rows = cfg.tile(t)
                k = i % 2
                i += 1
                hbr, obr = 'ohb%d' % k, 'oob%d' % k
                _dma(P, 'sp', hb[k][:rows, :], H[s, r0:r0 + rows, :], [hres(s, t)], [hbr])
                if raw:
                    _dma(P, 'sp', out_d[s, r0 - N_META:r0 - N_META + rows, :], hb[k][:rows, :], [hbr], ['out_%d_%d' % (s, t)])
                    continue
                _act(P, sq[:rows, :], hb[k][:rows, :], AF.Square, [hbr], ['sq', 'ss%d' % k], accum_out=C.ss[:rows, k:k + 1])
                _act(P, C.rs[:rows, k:k + 1], C.ss[:rows, k:k + 1], AF.Sqrt, ['ss%d' % k], ['rs%d' % k], scale=1.0 / DM, bias=C.epsb[:rows, :])
                _recip(P, C.rstd[:rows, k:k + 1], C.rs[:rows, k:k + 1], ['rs%d' % k], ['rstd%d' % k])
                _stt(P, ob[k][:rows, :], hb[k][:rows, :], C.rstd[:rows, k:k + 1], gbc[:rows, :], ALU.mult, ALU.mult,
                     [hbr, 'rstd%d' % k, 'gbc'], [obr])
                _dma(P, 'sp', out_d[s, r0 - N_META:r0 - N_META + rows, :], ob[k][:rows, :], [obr], ['out_%d_%d' % (s, t)])
        P.flush()


def build(cfg, plan=None):
    nc = bass.Bass("TRN2", target_bir_lowering=False)
    DM, DFF, L, NSEQ, SEQ, DEPTH = cfg.DM, cfg.DFF, cfg.L, cfg.NSEQ, cfg.SEQ, cfg.DEPTH
    KC, NF = cfg.KC, cfg.NF
    nA, nB = (DEPTH + 1) // 2, DEPTH // 2

    def din(name, shape, dt=F32):
        return nc.dram_tensor(name, list(shape), dt, kind="ExternalInput").ap()
    D = {}
    D['x'] = din('x', [NSEQ, SEQ, DM])
    D['meta'] = din('meta_tokens', [N_META, DM])
    for nm in ('norm_ffn1', 'norm_mix', 'norm_ffn2'):
        D[nm] = din(nm, [DEPTH, 1, DM])
    D['final_norm'] = din('final_norm', [1, DM])
    for nm in ('ffn1', 'ffn2'):
        D[nm + '_win'] = din(nm + '_win_r', [DEPTH, NF, 128, KC, 256])
        D[nm + '_wout'] = din(nm + '_wout_r', [DEPTH, 128, NF, DM])
    D['c_ident'] = din('c_ident', [128, 128])
    D['c_tri'] = din('c_tri', [128, 128])
    D['c_maskA'] = din('c_maskA', [128, 128])
    D['c_cos'] = din('c_cos', [L, 32])
    D['c_sin'] = din('c_sin', [L, 32])
    mixer_decl(nc, cfg, D, din, nA, nB)
    out_d = nc.dram_tensor('out', [NSEQ, SEQ, DM], F32, kind="ExternalOutput").ap()
    H = nc.dram_tensor('Hres', [NSEQ, L, DM], F32, kind="Internal").ap()

    C = Ctx()
    C.nc, C.cfg, C.D = nc, cfg, D
    C.P = P = Prog(nc)
    C.evac_ctr = C.wctr = C.hres_ctr = C.sg_ctr = 0
    C.ps = [nc.alloc_psum_tensor('psb%d' % i, [128, 512], F32) for i in range(8)]
    C.psb = [p[:, :].bitcast(BF16) for p in C.ps]
    C.ps = [p[:, :] for p in C.ps]
    C.idb = nc.alloc_sbuf_tensor('c_idb', [128, 128], BF16)
    C.idf = nc.alloc_sbuf_tensor('c_idf', [128, 128], F32)
    C.tri = nc.alloc_sbuf_tensor('c_trif', [128, 128], F32)
    C.maskA = nc.alloc_sbuf_tensor('c_maskAf', [128, 128], F32)
    C.onesf = nc.alloc_sbuf_tensor('c_onesf', [128, 128], F32)
    C.epsb = nc.alloc_sbuf_tensor('c_epsb', [128, 1], F32)
    C.ss = nc.alloc_sbuf_tensor('c_ss', [128, 2], F32)
    C.rs = nc.alloc_sbuf_tensor('c_rs', [128, 2], F32)
    C.rstd = nc.alloc_sbuf_tensor('c_rstd', [128, 2], F32)
    _dma(P, 'pool', C.idb[:, :], D['c_ident'], [], ['idb'])
    _dma(P, 'sp', C.idf[:, :], D['c_ident'], [], ['idf'])
    _dma(P, 'sp', C.tri[:, :], D['c_tri'], [], ['tri'])
    _dma(P, 'sp', C.maskA[:, :], D['c_maskA'], [], ['maskA'])
    _memset(P, 'dve', C.onesf[:, :], 1.0, ['onesf'])
    _memset(P, 'dve', C.epsb[:, :], EPS, ['epsb'])
    for s in range(NSEQ):
        _dma(P, 'sp', H[s, 0:N_META, :], D['meta'], [], [hres(s, 0)])
        for t in range(1, cfg.NT):
            r0, rows = cfg.tile(t)
            _dma(P, 'sp', H[s, r0:r0 + rows, :], D['x'][s, r0 - N_META:r0 - N_META + rows, :], [], [hres(s, t)])
    P.flush()
    if plan is None:
        plan = []
        for i in range(DEPTH):
            plan += [('ffn1', i), ('mla' if i % 2 == 0 else 'ssm', i // 2, i), ('ffn2', i)]
        plan += [('out',)]
    for st in plan:
        if st[0] in ('ffn1', 'ffn2'):
            i = st[1]
            ffn_phase(C, H, D['norm_' + st[0]][i], D[st[0] + '_win'][i], D[st[0] + '_wout'][i])
        elif st[0] == 'mla':
            mla_phase(C, H, st[1], st[2])
        elif st[0] == 'ssm':
            ssm_phase(C, H, st[1], st[2])
        elif st[0] == 'ssm1':
            ssm_part1(C, H, st[1], st[2])
        elif st[0] == 'ssm2':
            ssm_part2(C, st[1])
        elif st[0] == 'ssm3':
            ssm_part3(C, st[1])
        elif st[0] == 'ssm4':
            ssm_part4(C, H, st[1])
        elif st[0] == 'out':
            out_phase(C, H, D['final_norm'], out_d)
        elif st[0] == 'raw':
            out_phase(C, H, D['final_norm'], out_d, raw=True)
    return nc


def mixer_decl(nc, cfg, D, din, nA, nB):
    mla_decl(nc, cfg, D, din, nA)
    ssm_decl(nc, cfg, D, din, nB)


def host_consts(cfg):
    L = cfg.L
    ident = np.eye(128, dtype=np.float32)
    k = np.arange(128)
    tri = (k[:, None] <= k[None, :]).astype(np.float32)
    maskA = np.where(k[None, :] <= k[:, None], 0.0, NEG).astype(np.float32)
    inv_freq = (1.0 / (10000.0 ** (np.arange(0, 64, 2, dtype=np.float32) / np.float32(64)))).astype(np.float32)
    ang = np.arange(L, dtype=np.float32)[:, None] * inv_freq[None, :]
    return dict(c_ident=ident, c_tri=tri, c_maskA=maskA, c_maskS=np.ascontiguousarray(maskA.T),
                c_cos=np.cos(ang).astype(np.float32), c_sin=np.sin(ang).astype(np.float32))


def host_ffn_layout(cfg, w_in, w_out):
    Dp = w_in.shape[0]
    KC, NF, DFF, DM = cfg.KC, cfg.NF, cfg.DFF, cfg.DM
    g = w_in[:, :, :DFF].reshape(Dp, KC, 128, NF, 128)
    u = w_in[:, :, DFF:].reshape(Dp, KC, 128, NF, 128)
    gu = np.concatenate([g, u], axis=-1)
    win_r = np.ascontiguousarray(gu.transpose(0, 3, 2, 1, 4))
    wout_r = np.ascontiguousarray(w_out.reshape(Dp, NF, 128, DM).transpose(0, 2, 1, 3))
    return win_r, wout_r


MLA_H = 16
ATT_SCALE = float(192 ** -0.5)


def rope(C, src, cs, sn, dst, rows, nh, rres, wres):
    P = C.P
    a, b = C.ropeA[:rows, 0:nh, :], C.ropeB[:rows, 0:nh, :]
    csb = cs[:rows, :].unsqueeze(1).to_broadcast([rows, nh, 32])
    snb = sn[:rows, :].unsqueeze(1).to_broadcast([rows, nh, 32])
    t1, t2 = src[:, :, 0:32], src[:, :, 32:64]
    _tt(P, 'dve', a, t1, csb, ALU.mult, rres, ['ropeA'])
    _tt(P, 'dve', b, t2, snb, ALU.mult, rres, ['ropeB'])
    _tt(P, 'dve', dst[:, :, 0:32], a, b, ALU.subtract, ['ropeA', 'ropeB'], wres)
    _tt(P, 'dve', a, t2, csb, ALU.mult, rres, ['ropeA'])
    _tt(P, 'dve', b, t1, snb, ALU.mult, rres, ['ropeB'])
    _tt(P, 'dve', dst[:, :, 32:64], a, b, ALU.add, ['ropeA', 'ropeB'], wres)


def mla_decl(nc, cfg, D, din, nA):
    L, NSEQ = cfg.L, cfg.NSEQ
    D['mla_win'] = din('mla_win_r', [nA, 128, cfg.KC, 1088])
    D['mla_wuq'] = din('mla_wuq_r', [nA, 128, 4, 3072])
    D['mla_wukv'] = din('mla_wukv_r', [nA, 128, 4, 4096])
    D['mla_wo'] = din('mla_wo_r', [nA, 128, 16, cfg.DM])
    D['mla_qn'] = din('mla_q_norm', [nA, 1, 512])
    D['mla_kvn'] = din('mla_kv_norm', [nA, 1, 512])

    def scr(name, shape):
        return nc.dram_tensor(name, shape, BF16, kind="Internal").ap()
    D['QN'] = scr('s_QN', [NSEQ, L, 2048])
    D['QR'] = scr('s_QR', [NSEQ, L, 1024])
    D['KN'] = scr('s_KN', [NSEQ, L, 2048])
    D['V'] = scr('s_V', [NSEQ, L, 2048])
    D['KR'] = scr('s_KR', [NSEQ, L, 64])
    D['OT'] = scr('s_OT', [NSEQ, 128, 16, L])


def mla_part1(C, H, j, i):
    nc, P, cfg, D = C.nc, C.P, C.cfg, C.D
    DM, KC = cfg.DM, cfg.KC
    BC = cfg.BT * 128
    with ExitStack() as es:
        def A(name, shape, dt):
            return es.enter_context(nc.sbuf_tensor(_uname(name), shape, dt))
        gbc = A("m_gbc", [128, DM], F32)
        uT = A("m_uT", [128, KC, BC], BF16)
        C.hbuf = [A("m_hb0", [128, DM], F32), A("m_hb1", [128, DM], F32)]
        C.hn = [A("m_hn0", [128, DM], BF16), A("m_hn1", [128, DM], BF16)]
        C.sq = A("m_sq", [128, DM], BF16)
        cqT = A("m_cqT", [128, 4, BC], BF16)
        ckvT = A("m_ckvT", [128, 4, BC], BF16)
        wb = [A("m_w%d" % k, [128, 8, 512], BF16) for k in range(3)]
        gq = A("m_gq", [128, 512], F32)
        gkv = A("m_gkv", [128, 512], F32)
        cn = [A("m_cn%d" % k, [128, 512], BF16) for k in range(2)]
        st = [A("m_st%d" % k, [128, 512], BF16) for k in range(3)]
        fr = [A("m_fr%d" % k, [128, 512], F32) for k in range(2)]
        cs = [A("m_cs%d" % k, [128, 32], F32) for k in range(2)]
        sn = [A("m_sn%d" % k, [128, 32], F32) for k in range(2)]
        C.ropeA = A("m_ropeA", [128, 8, 32], F32)
        C.ropeB = A("m_ropeB", [128, 8, 32], F32)
        ctr = {'cn': 0, 'st': 0, 'fr': 0, 'cs': 0}
        _dma(P, 'sp', gbc[:, :], D['norm_mix'][i].partition_broadcast(128), [], ['gbc'])
        _dma(P, 'sp', gq[:, :], D['mla_qn'][j].partition_broadcast(128), [], ['gq'])
        _dma(P, 'sp', gkv[:, :], D['mla_kvn'][j].partition_broadcast(128), [], ['gkv'])

        def load_cs(r0, rows):
            k = ctr['cs'] % 2
            ctr['cs'] += 1
            _dma(P, 'sp', cs[k][:rows, :], D['c_cos'][r0:r0 + rows, :], [], ['cs%d' % k])
            _dma(P, 'sp', sn[k][:rows, :], D['c_sin'][r0:r0 + rows, :], [], ['sn%d' % k])
            return k

        def store_bf(psap, psres, rows, w, dst_ap, dst_res):
            k = ctr['st'] % 3
            ctr['st'] += 1
            eng = 'act' if k % 2 == 0 else 'dve'
            _cp(P, eng, st[k][:rows, 0:w], psap, [psres], ['st%d' % k])
            _dma(P, 'sp', dst_ap, st[k][:rows, 0:w], ['st%d' % k], [dst_res])

        def rope_store(psap, psres, te, nh, dst_ap, dst_res):
            s, t, r0, rows, col0 = te
            kf = ctr['fr'] % 2
            ctr['fr'] += 1
            _cp(P, 'act', fr[kf][:rows, 0:nh * 64], psap, [psres], ['fr%d' % kf])
            kc_ = load_cs(r0, rows)
            k = ctr['st'] % 3
            ctr['st'] += 1
            src = fr[kf][:rows, 0:nh * 64].rearrange("p (h d) -> p h d", d=64)
            dst = st[k][:rows, 0:nh * 64].rearrange("p (h d) -> p h d", d=64)
            rope(C, src, cs[kc_], sn[kc_], dst, rows, nh, ['fr%d' % kf, 'cs%d' % kc_, 'sn%d' % kc_], ['st%d' % k])
            _dma(P, 'sp', dst_ap, st[k][:rows, 0:nh * 64], ['st%d' % k], [dst_res])

        for blk in cfg.blocks():
            for ii, (s, t, r0, rows, col0) in enumerate(blk):
                norm_T(C, H[s, r0:r0 + rows, :], hres(s, t), rows, gbc, uT, 'uT', col0, ii)

            def epi_c(n, cc, te, psap, psres):
                s, t, r0, rows, col0 = te
                if n < 2:
                    k = ctr['cn'] % 2
                    ctr['cn'] += 1
                    _act(P, C.sq[:rows, 0:512], psap, AF.Square, [psres], ['sq', 'ss%d' % k], accum_out=C.ss[:rows, k:k + 1])
                    _act(P, C.rs[:rows, k:k + 1], C.ss[:rows, k:k + 1], AF.Sqrt, ['ss%d' % k], ['rs%d' % k], scale=1.0 / 512, bias=C.epsb[:rows, :])
                    _recip(P, C.rstd[:rows, k:k + 1], C.rs[:rows, k:k + 1], ['rs%d' % k], ['rstd%d' % k])
                    g_, gr_ = (gq, 'gq') if n == 0 else (gkv, 'gkv')
                    _stt(P, cn[k][:rows, :], psap, C.rstd[:rows, k:k + 1], g_[:rows, :], ALU.mult, ALU.mult,
                         [psres, 'rstd%d' % k, gr_], ['cn%d' % k])
                    if n == 0:
                        transpose_into(C, cn[k], 'cn%d' % k, rows, 4, cqT, 'cqT', col0)
                    else:
                        transpose_into(C, cn[k], 'cn%d' % k, rows, 4, ckvT, 'ckvT', col0)
                else:
                    rope_store(psap, psres, te, 1, D['KR'][s, r0:r0 + rows, :], 'KR_%d_%d' % (s, t))
            gemm_tm(C, uT, 'uT', KC, blk, D['mla_win'][j], 1088, epi_c, wb, 'mw')

            def epi_q(n, cc, te, psap, psres):
                s, t, r0, rows, col0 = te
                if n < 4:
                    store_bf(psap, psres, rows, 512, D['QN'][s, r0:r0 + rows, cc[0]:cc[1]], 'QN_%d_%d' % (s, t))
                else:
                    rope_store(psap, psres, te, 8, D['QR'][s, r0:r0 + rows, cc[0] - 2048:cc[1] - 2048], 'QR_%d_%d' % (s, t))
            gemm_tm(C, cqT, 'cqT', 4, blk, D['mla_wuq'][j], 3072, epi_q, wb, 'mw', kstep=4)

            def epi_kv(n, cc, te, psap, psres):
                s, t, r0, rows, col0 = te
                if n < 4:
                    store_bf(psap, psres, rows, 512, D['KN'][s, r0:r0 + rows, cc[0]:cc[1]], 'KN_%d_%d' % (s, t))
                else:
                    store_bf(psap, psres, rows, 512, D['V'][s, r0:r0 + rows, cc[0] - 2048:cc[1] - 2048], 'V_%d_%d' % (s, t))
            gemm_tm(C, ckvT, 'ckvT', 4, blk, D['mla_wukv'][j], 4096, epi_kv, wb, 'mw', kstep=4)
        P.flush()


def load_tok(C, dst, dst_res, src, s, c0, c1, rres_fn):
    P, cfg = C.P, C.cfg
    rr = [rres_fn(s, t) for t in range(cfg.NT)]
    _dma(P, 'sp', dst[:N_META, 0, :], src[s, 0:N_META, c0:c1], rr, [dst_res])
    _dma(P, 'sp', dst[:, 1:cfg.NT, :], src[s, N_META:, c0:c1].rearrange("(t p) d -> p t d", p=128), rr, [dst_res])


def transpose_tok(C, tok, tok_res, nd, dstT, dst_res):
    P, cfg = C.P, C.cfg
    groups = [[0]] + [list(range(t, min(t + 4, cfg.NT))) for t in range(1, cfg.NT, 4)]
    for g in groups:
        b = P.nextbank()
        pb = C.psb[b]
        for jj, t in enumerate(g):
            r0, rows = cfg.tile(t)
            _tr(P, pb[:nd, jj * 128:jj * 128 + rows], tok[:rows, t, :], C.idb[:rows, :rows], [tok_res, 'idb'], ['ps%d' % b])
        r0, rows = cfg.tile(g[0])
        ncol = sum(cfg.tile(t)[1] for t in g)
        eng = 'act' if (C.evac_ctr % 2 == 0) else 'dve'
        C.evac_ctr += 1
        _cp(P, eng, dstT[:nd, r0:r0 + ncol], pb[:nd, 0:ncol], ['ps%d' % b], [dst_res])


def mla_part2(C, j):
    nc, P, cfg, D = C.nc, C.P, C.cfg, C.D
    L, NT = cfg.L, cfg.NT
    bounds = [0] + [cfg.tile(t)[0] for t in range(4, NT, 4)] + [L]
    with ExitStack() as es:
        def A(name, shape, dt):
            return es.enter_context(nc.sbuf_tensor(_uname(name), shape, dt))
        krtok = A("a_krtok", [128, NT, 64], BF16)
        KRT = A("a_KRT", [64, L], BF16)
        S2 = []
        for k in range(2):
            S2.append(dict(
                ktok=A("a_ktok%d" % k, [128, NT, 128], BF16), vtok=A("a_vtok%d" % k, [128, NT, 128], BF16),
                qntok=A("a_qntok%d" % k, [128, NT, 128], BF16), qrtok=A("a_qrtok%d" % k, [128, NT, 64], BF16),
                KT=A("a_KT%d" % k, [128, L], BF16), QNT=A("a_QNT%d" % k, [128, L], BF16),
                QRT=A("a_QRT%d" % k, [64, L], BF16), OT=A("a_OT%d" % k, [128, L], BF16)))
        Pb = [A("a_Pb%d" % k, [128, L], BF16) for k in range(2)]
        PT = [A("a_PT%d" % k, [128, NT, 128], BF16) for k in range(2)]
        mxs = [A("a_mxs%d" % k, [128, 8], F32) for k in range(2)]
        mx = [A("a_mx%d" % k, [128, 1], F32) for k in range(2)]
        nb = [A("a_nb%d" % k, [128, 1], F32) for k in range(2)]
        sums = [A("a_sums%d" % k, [128, 8], F32) for k in range(2)]
        rsum = [A("a_rsum%d" % k, [128, 1], F32) for k in range(2)]
        rinv = [A("a_rinv%d" % k, [128, 1], F32) for k in range(2)]
        KRT2 = [KRT, A("a_KRT1", [64, L], BF16)]
        krtok2 = [krtok, A("a_krtok1", [128, NT, 64], BF16)]
        P.bankset = [5, 6]
        units = [(s, h) for s in range(cfg.NSEQ) for h in range(MLA_H)]
        itc = [0]

        def prologue(u):
            s, h = units[u]
            B = S2[u % 2]
            sfx = str(u % 2)
            if h == 0:
                ss_ = str(s % 2)
                load_tok(C, krtok2[s % 2], 'krtok' + ss_, D['KR'], s, 0, 64, lambda s_, t_: 'KR_%d_%d' % (s_, t_))
                transpose_tok(C, krtok2[s % 2], 'krtok' + ss_, 64, KRT2[s % 2], 'KRT' + ss_)
            load_tok(C, B['ktok'], 'ktok' + sfx, D['KN'], s, h * 128, (h + 1) * 128, lambda s_, t_: 'KN_%d_%d' % (s_, t_))
            load_tok(C, B['vtok'], 'vtok' + sfx, D['V'], s, h * 128, (h + 1) * 128, lambda s_, t_: 'V_%d_%d' % (s_, t_))
            load_tok(C, B['qntok'], 'qntok' + sfx, D['QN'], s, h * 128, (h + 1) * 128, lambda s_, t_: 'QN_%d_%d' % (s_, t_))
            load_tok(C, B['qrtok'], 'qrtok' + sfx, D['QR'], s, h * 64, (h + 1) * 64, lambda s_, t_: 'QR_%d_%d' % (s_, t_))
            transpose_tok(C, B['ktok'], 'ktok' + sfx, 128, B['KT'], 'KT' + sfx)
            transpose_tok(C, B['qntok'], 'qntok' + sfx, 128, B['QNT'], 'QNT' + sfx)
            transpose_tok(C, B['qrtok'], 'qrtok' + sfx, 64, B['QRT'], 'QRT' + sfx)

        def stageA(u, i, k):
            s, h = units[u]
            B = S2[u % 2]
            sfx = str(u % 2)
            KRTs, krr = KRT2[s % 2], 'KRT' + str(s % 2)
            r0, rows = cfg.tile(i)
            nkeys = r0 + rows
            ks = str(k)
            grp = [(bounds[g], min(bounds[g + 1], nkeys)) for g in range(len(bounds) - 1) if bounds[g] < nkeys]
            for gi, (g0, g1) in enumerate(grp):
                w = g1 - g0
                pr = 'ps%d' % gi
                psg = C.ps[gi][:rows, 0:w]
                _mm(P, psg, B['QNT'][:, r0:r0 + rows], B['KT'][:, g0:g1], True, False, ['QNT' + sfx, 'KT' + sfx], [pr])
                _mm(P, psg, B['QRT'][:64, r0:r0 + rows], KRTs[:64, g0:g1], False, True, ['QRT' + sfx, krr], [pr])
                if g1 == nkeys:
                    lo = r0 - g0
                    dg = C.ps[gi][:rows, lo:lo + rows]
                    _tt(P, 'dve', dg, dg, C.maskA[:rows, :rows], ALU.add, [pr, 'maskA'], [pr])
                _red(P, mxs[k][:rows, gi:gi + 1], psg, ALU.max, [pr], ['mxs' + ks])
            ng = len(grp)
            _red(P, mx[k][:rows, :], mxs[k][:rows, 0:ng], ALU.max, ['mxs' + ks], ['mx' + ks])
            _ts(P, 'dve', nb[k][:rows, :], mx[k][:rows, :], -ATT_SCALE, None, ALU.mult, None, ['mx' + ks], ['nb' + ks])
            for gi, (g0, g1) in enumerate(grp):
                w = g1 - g0
                pr = 'ps%d' % gi
                _act(P, Pb[k][:rows, g0:g1], C.ps[gi][:rows, 0:w], AF.Exp, [pr, 'nb' + ks], ['Pb' + ks, 'sums' + ks],
                     bias=nb[k][:rows, :], scale=ATT_SCALE, accum_out=sums[k][:rows, gi:gi + 1])
            _red(P, rsum[k][:rows, :], sums[k][:rows, 0:ng], ALU.add, ['sums' + ks], ['rsum' + ks])
            _recip(P, rinv[k][:rows, :], rsum[k][:rows, :], ['rsum' + ks], ['rinv' + ks])
            _ts(P, 'dve', Pb[k][:rows, 0:nkeys], Pb[k][:rows, 0:nkeys], rinv[k][:rows, :], None, ALU.mult, None,
                ['Pb' + ks, 'rinv' + ks], ['Pb' + ks])

        def stageB(u, i, k):
            s, h = units[u]
            B = S2[u % 2]
            sfx = str(u % 2)
            r0, rows = cfg.tile(i)
            ks = str(k)
            kt = list(range(i + 1))
            for q0 in range(0, len(kt), 4):
                g = kt[q0:q0 + 4]
                b = P.nextbank()
                for jj, jt in enumerate(g):
                    kr0, kn = cfg.tile(jt)
                    _tr(P, C.psb[b][:kn, jj * 128:jj * 128 + rows], Pb[k][:rows, kr0:kr0 + kn], C.idb[:rows, :rows],
                        ['Pb' + ks, 'idb'], ['ps%d' % b])
                srcv = C.psb[b][:, 0:len(g) * 128].rearrange("p (j c) -> p j c", c=128)[:, :, :rows]
                eng = 'act' if (C.evac_ctr % 2 == 0) else 'dve'
                C.evac_ctr += 1
                _cp(P, eng, PT[k][:, g[0]:g[0] + len(g), :rows], srcv, ['ps%d' % b], ['PT' + ks])
            bo = 7
            for jt in kt:
                kr0, kn = cfg.tile(jt)
                _mm(P, C.ps[bo][:, 0:rows], B['vtok'][:kn, jt, :], PT[k][:kn, jt, :rows], jt == 0, jt == i,
                    ['vtok' + sfx, 'PT' + ks], ['ps%d' % bo])
            _cp(P, 'act', B['OT'][:, r0:r0 + rows], C.ps[bo][:, 0:rows], ['ps%d' % bo], ['OT' + sfx])

        prologue(0)
        for u, (s, h) in enumerate(units):
            if u + 1 < len(units):
                prologue(u + 1)
            kk = [(itc[0] + i) % 2 for i in range(NT)]
            itc[0] += NT
            stageA(u, 0, kk[0])
            for i in range(1, NT):
                stageA(u, i, kk[i])
                stageB(u, i - 1, kk[i - 1])
            stageB(u, NT - 1, kk[NT - 1])
            _dma(P, 'sp', D['OT'][s, :, h, :], S2[u % 2]['OT'][:, :], ['OT' + str(u % 2)], ['OTd_%d' % s])
        P.bankset = None
        P.flush()


def mla_part3(C, H, j):
    nc, P, cfg, D = C.nc, C.P, C.cfg, C.D
    BC = cfg.BT * 128
    with ExitStack() as es:
        def A(name, shape, dt):
            return es.enter_context(nc.sbuf_tensor(_uname(name), shape, dt))
        oT = [A("p_oT%d" % k, [128, 16, BC], BF16) for k in range(2)]
        wb = [A("p_w%d" % k, [128, 8, 512], BF16) for k in range(3)]
        C.hres = [A("p_hr%d" % k, [128, 512], F32) for k in range(3)]
        epi = residual_epi(C, H, 1.0)
        for bi, blk in enumerate(cfg.blocks()):
            k = bi % 2
            for (s, t, r0, rows, col0) in blk:
                _dma(P, 'sp', oT[k][:, :, col0:col0 + rows], D['OT'][s, :, :, r0:r0 + rows], ['OTd_%d' % s], ['oT%d' % k])
            gemm_tm(C, oT[k], 'oT%d' % k, 16, blk, D['mla_wo'][j], cfg.DM, epi, wb, 'pw')
        P.flush()


def mla_phase(C, H, j, i):
    mla_part1(C, H, j, i)
    mla_part2(C, j)
    mla_part3(C, H, j)


def host_mla_layout(cfg, w_in, w_uq, w_ukv, w_o):
    nA = w_in.shape[0]
    KC = cfg.KC
    win_r = np.ascontiguousarray(w_in.reshape(nA, KC, 128, 1088).transpose(0, 2, 1, 3))
    q = w_uq.reshape(nA, 512, 16, 192)
    wuq = np.concatenate([q[..., :128].reshape(nA, 512, 2048), q[..., 128:].reshape(nA, 512, 1024)], axis=-1)
    wuq_r = np.ascontiguousarray(wuq.reshape(nA, 4, 128, 3072).transpose(0, 2, 1, 3))
    kv = w_ukv.reshape(nA, 512, 16, 256)
    wukv = np.concatenate([kv[..., :128].reshape(nA, 512, 2048), kv[..., 128:].reshape(nA, 512, 2048)], axis=-1)
    wukv_r = np.ascontiguousarray(wukv.reshape(nA, 4, 128, 4096).transpose(0, 2, 1, 3))
    wo_r = np.ascontiguousarray(w_o.reshape(nA, 16, 128, cfg.DM).transpose(0, 2, 1, 3))
    return win_r, wuq_r, wukv_r, wo_r


S_IN = 4096
S_CONV = 6144
S_PROJ = 10304
S_H = 64
import os as _os
DBG3 = int(_os.environ.get('DBG3', '9'))
DBGX = int(_os.environ.get('DBGX', '99'))


def ssm_decl(nc, cfg, D, din, nB):
    L, NSEQ = cfg.L, cfg.NSEQ
    D['ssm_win'] = din('ssm_win_r', [nB, 128, cfg.KC, S_PROJ])
    D['ssm_wout'] = din('ssm_wout_r', [nB, 128, 32, cfg.DM])
    D['ssm_cw'] = din('ssm_conv_w', [nB, 4, S_CONV])
    D['ssm_cb'] = din('ssm_conv_b', [nB, 1, S_CONV])
    D['ssm_dtb'] = din('ssm_dt_bias', [nB, 1, S_H])
    D['ssm_alog'] = din('ssm_a_log', [nB, 1, S_H])
    D['ssm_d'] = din('ssm_d', [nB, 1, S_H])
    D['ssm_nw'] = din('ssm_norm', [nB, 1, S_IN])
    D['c_maskS'] = din('c_maskS', [128, 128])
    D['ZX'] = nc.dram_tensor('s_ZX', [NSEQ, L, S_PROJ], F32, kind="Internal").ap()
    D['XC'] = nc.dram_tensor('s_XC', [NSEQ, L, S_CONV], F32, kind="Internal").ap()
    D['YT'] = nc.dram_tensor('s_YT', [NSEQ, 128, 32, L], BF16, kind="Internal").ap()


def ssm_part1(C, H, j, i):
    nc, P, cfg, D = C.nc, C.P, C.cfg, C.D
    DM, KC = cfg.DM, cfg.KC
    BC = cfg.BT * 128
    with ExitStack() as es:
        def A(name, shape, dt):
            return es.enter_context(nc.sbuf_tensor(_uname(name), shape, dt))
        gbc = A("s1_gbc", [128, DM], F32)
        uT = A("s1_uT", [128, KC, BC], BF16)
        C.hbuf = [A("s1_hb0", [128, DM], F32), A("s1_hb1", [128, DM], F32)]
        C.hn = [A("s1_hn0", [128, DM], BF16), A("s1_hn1", [128, DM], BF16)]
        C.sq = A("s1_sq", [128, DM], BF16)
        wb = [A("s1_w%d" % k, [128, 8, 512], BF16) for k in range(3)]
        fr = [A("s1_fr%d" % k, [128, 512], F32) for k in range(4)]
        ctr = [0]
        _dma(P, 'sp', gbc[:, :], D['norm_mix'][i].partition_broadcast(128), [], ['gbc'])

        def epi(n, cc, te, psap, psres):
            s, t, r0, rows, col0 = te
            k = ctr[0] % 4
            ctr[0] += 1
            w = cc[1] - cc[0]
            _cp(P, 'act' if k % 2 == 0 else 'dve', fr[k][:rows, 0:w], psap, [psres], ['fr%d' % k])
            _dma(P, 'sp', D['ZX'][s, r0:r0 + rows, cc[0]:cc[1]], fr[k][:rows, 0:w], ['fr%d' % k], ['ZX_%d_%d' % (s, t)])
        for blk in cfg.blocks():
            for ii, (s, t, r0, rows, col0) in enumerate(blk):
                norm_T(C, H[s, r0:r0 + rows, :], hres(s, t), rows, gbc, uT, 'uT', col0, ii)
            gemm_tm(C, uT, 'uT', KC, blk, D['ssm_win'][j], S_PROJ, epi, wb, 'sw')
        P.flush()


def ssm_part2(C, j):
    nc, P, cfg, D = C.nc, C.P, C.cfg, C.D
    CW = 1024
    with ExitStack() as es:
        def A(name, shape, dt):
            return es.enter_context(nc.sbuf_tensor(_uname(name), shape, dt))
        wbc = A("s2_wbc", [128, 4, CW], F32)
        bbc = A("s2_bbc", [128, CW], F32)
        xk = [[A("s2_x%d_%d" % (b, k), [128, CW], F32) for k in range(4)] for b in range(2)]
        pk = [[A("s2_p%d_%d" % (b, k), [128, CW], F32) for k in range(4)] for b in range(2)]
        oc = [A("s2_o%d" % b, [128, CW], F32) for b in range(2)]
        it = 0
        for c0 in range(0, S_CONV, CW):
            c1 = c0 + CW
            for k in range(4):
                _dma(P, 'sp', wbc[:, k, :], D['ssm_cw'][j, k:k + 1, c0:c1].partition_broadcast(128), [], ['wbc'])
            _dma(P, 'sp', bbc[:, :], D['ssm_cb'][j, :, c0:c1].partition_broadcast(128), [], ['bbc'])
            for s in range(cfg.NSEQ):
                for t in range(cfg.NT):
                    r0, rows = cfg.tile(t)
                    b = it % 2
                    it += 1
                    zr = ['ZX_%d_%d' % (s, t)] + (['ZX_%d_%d' % (s, t - 1)] if t >= 2 else []) + (['ZX_%d_0' % s] if t == 1 else [])
                    for k in range(4):
                        sh = 3 - k
                        xr = 'x%d_%d' % (b, k)
                        if r0 - sh < 0:
                            _memset(P, 'pool', xk[b][k][:rows, :], 0.0, [xr])
                            _dma(P, 'sp', xk[b][k][sh:rows, :], D['ZX'][s, 0:rows - sh, S_IN + c0:S_IN + c1], zr, [xr])
                        else:
                            _dma(P, 'sp', xk[b][k][:rows, :], D['ZX'][s, r0 - sh:r0 - sh + rows, S_IN + c0:S_IN + c1], zr, [xr])
                        _tt(P, 'pool', pk[b][k][:rows, :], xk[b][k][:rows, :], wbc[:rows, k, :], ALU.mult, [xr, 'wbc'], ['p%d_%d' % (b, k)])
                    _tt(P, 'dve', pk[b][0][:rows, :], pk[b][0][:rows, :], pk[b][1][:rows, :], ALU.add, ['p%d_0' % b, 'p%d_1' % b], ['p%d_0' % b])
                    _tt(P, 'dve', pk[b][2][:rows, :], pk[b][2][:rows, :], pk[b][3][:rows, :], ALU.add, ['p%d_2' % b, 'p%d_3' % b], ['p%d_2' % b])
                    _tt(P, 'dve', pk[b][0][:rows, :], pk[b][0][:rows, :], pk[b][2][:rows, :], ALU.add, ['p%d_0' % b, 'p%d_2' % b], ['p%d_0' % b])
                    _tt(P, 'dve', pk[b][0][:rows, :], pk[b][0][:rows, :], bbc[:rows, :], ALU.add, ['p%d_0' % b, 'bbc'], ['p%d_0' % b])
                    _act(P, oc[b][:rows, :], pk[b][0][:rows, :], AF.Silu, ['p%d_0' % b], ['oc%d' % b])
                    _dma(P, 'sp', D['XC'][s, r0:r0 + rows, c0:c1], oc[b][:rows, :], ['oc%d' % b], ['XC_%d_%d' % (s, t)])
        P.flush()


def ssm_part3(C, j):
    nc, P, cfg, D = C.nc, C.P, C.cfg, C.D
    with ExitStack() as es:
        def A(name, shape, dt):
            return es.enter_context(nc.sbuf_tensor(_uname(name), shape, dt))
        ST = A("s3_ST", [128, S_IN], F32)
        Sbf = A("s3_Sbf", [128, S_IN], BF16)
        nwbc = A("s3_nw", [128, S_IN], F32)
        xs = A("s3_xs", [128, S_IN], F32)
        zt = A("s3_zt", [128, S_IN], F32)
        bcf = A("s3_bcf", [128, 2048], F32)
        bcb = A("s3_bcb", [128, 2048], BF16)
        BCT = A("s3_BCT", [128, 16, 128], BF16)
        xdt = A("s3_xdt", [128, S_IN], BF16)
        xw = A("s3_xw", [128, S_IN], BF16)
        ysb = A("s3_ysb", [128, S_IN], F32)
        ynb = A("s3_ynb", [128, S_IN], BF16)
        YTt = A("s3_YTt", [128, 32, 128], BF16)
        E = [A("s3_E%d" % k, [128, 8 * 128], BF16) for k in range(2)]
        MT = [A("s3_MT%d" % k, [128, 8 * 128], BF16) for k in range(2)]
        rhsg = [A("s3_rg%d" % k, [128, 8 * 128], F32) for k in range(2)]
        CBT = [A("s3_CBT%d" % k, [128, 128], BF16) for k in range(2)]
        t1 = [A("s3_t1%d" % k, [128, 512], F32) for k in range(2)]
        mrep = A("s3_mrep", [128, 8 * 128], BF16)
        mrep16 = A("s3_mrep16", [128, 8 * 16], BF16)
        maskS = A("s3_maskS", [128, 128], F32)
        sm = {}
        for nm in ('dtb', 'abc', 'dbc', 'dtr', 'dt', 'dta', 'acs', 'nacs', 'expA', 'decayc', 'wq', 'tmpw', 'ssg', 'rsg', 'rstdg'):
            sm[nm] = A("s3_" + nm, [128, 64], F32)
        acsT = A("s3_acsT", [128, 128], F32)
        dtaP = A("s3_dtaP", [128, 128], F32)
        _dma(P, 'sp', nwbc[:, :], D['ssm_nw'][j].partition_broadcast(128), [], ['nwbc'])
        _dma(P, 'sp', sm['dtb'][:, :], D['ssm_dtb'][j].partition_broadcast(128), [], ['dtb'])
        _dma(P, 'sp', sm['abc'][:, :], D['ssm_alog'][j].partition_broadcast(128), [], ['abc'])
        _dma(P, 'sp', sm['dbc'][:, :], D['ssm_d'][j].partition_broadcast(128), [], ['dbc'])
        _dma(P, 'sp', maskS[:, :], D['c_maskS'], [], ['maskS'])
        _act(P, sm['abc'][:, :], sm['abc'][:, :], AF.Exp, ['abc'], ['abc'])
        _ts(P, 'dve', sm['abc'][:, :], sm['abc'][:, :], -1.0, None, ALU.mult, None, ['abc'], ['abc'])
        _cp(P, 'dve', mrep[:, :].rearrange("p (r l) -> p r l", l=128), maskS[:, :].unsqueeze(1).to_broadcast([128, 8, 128]), ['maskS'], ['mrep'])
        _memset(P, 'dve', mrep16[:, :], 0.0, ['mrep16'])
        _cp(P, 'dve', mrep16[:16, :].rearrange("p (r l) -> p r l", l=16), maskS[:16, :16].unsqueeze(1).to_broadcast([16, 8, 16]), ['maskS'], ['mrep16'])
        it = 0
        P.bankset = [0, 1]
        for s in range(cfg.NSEQ):
            _memset(P, 'pool', ST[:, :], 0.0, ['ST%d' % g for g in range(8)])
            _memset(P, 'pool', Sbf[:, :], 0.0, ['Sbf%d' % g for g in range(8)])
            _memset(P, 'pool', dtaP[:, :], 0.0, ['dta'])
            for t in range(cfg.NT):
                r0, rows = cfg.tile(t)
                R = rows
                mr = mrep if R == 128 else mrep16
                mrr = 'mrep' if R == 128 else 'mrep16'
                xcr, zxr = 'XC_%d_%d' % (s, t), 'ZX_%d_%d' % (s, t)
                _dma(P, 'sp', xs[:R, :], D['XC'][s, r0:r0 + R, 0:S_IN], [xcr], ['xs'])
                _dma(P, 'sp', bcf[:R, :], D['XC'][s, r0:r0 + R, S_IN:S_CONV], [xcr], ['bcf'])
                _dma(P, 'sp', zt[:R, :], D['ZX'][s, r0:r0 + R, 0:S_IN], [zxr], ['zt'])
                _dma(P, 'sp', sm['dtr'][:R, :], D['ZX'][s, r0:r0 + R, S_IN + S_CONV:S_PROJ], [zxr], ['dtr'])
                _tt(P, 'dve', sm['dt'][:R, :], sm['dtr'][:R, :], sm['dtb'][:R, :], ALU.add, ['dtr', 'dtb'], ['dt'])
                _act(P, sm['dt'][:R, :], sm['dt'][:R, :], AF.Exp, ['dt'], ['dt'])
                _act(P, sm['dt'][:R, :], sm['dt'][:R, :], AF.Ln, ['dt'], ['dt'], bias=1.0)
                _tt(P, 'dve', dtaP[:R, 0:64], sm['dt'][:R, :], sm['abc'][:R, :], ALU.mult, ['dt', 'abc'], ['dta'])
                bq = P.nextbank()
                pq, pqr = C.ps[bq], 'ps%d' % bq
                _stmts = []
                _stmts.append(lambda: _mm(P, pq[:, 0:64], C.tri[:, :], dtaP[:, 0:64], True, True, ['tri', 'dta'], [pqr]))
                _stmts.append(lambda: _mm(P, pq[:, 64:64 + R], dtaP[:, :], C.tri[:, :R], True, True, ['tri', 'dta'], [pqr]))
                _stmts.append(lambda: _mm(P, pq[:, 256:320], C.onesf[:, :], dtaP[:, 0:64], True, True, ['onesf', 'dta'], [pqr]))
                _stmts.append(lambda: _mm(P, pq[:, 480:496], C.idb[:, :], C.idb[:, 0:16], True, True, ['idb'], [pqr]))
                _stmts.append(lambda: _cp(P, 'act', sm['acs'][:R, :], pq[:R, 0:64], [pqr], ['acs']))
                _stmts.append(lambda: (_ts(P, 'dve', sm['nacs'][:R, :], pq[:R, 0:64], -1.0, None, ALU.mult, None, [pqr], ['nacs']) if _os.environ.get('NACS_PSUM') else _ts(P, 'dve', sm['nacs'][:R, :], sm['acs'][:R, :], -1.0, None, ALU.mult, None, ['acs'], ['nacs'])))
                _stmts.append(lambda: _cp(P, 'act', acsT[:, :R], pq[:, 64:64 + R], [pqr], ['acsT']))
                _stmts.append(lambda: _act(P, sm['expA'][:R, :], pq[:R, 0:64], AF.Exp, [pqr], ['expA']))
                _stmts.append(lambda: _act(P, sm['decayc'][:, :], pq[:, 256:320], AF.Exp, [pqr], ['decayc']))
                _stmts.append(lambda: _cp(P, 'act', sm['ssg'][:R, :], pq[:R, 256:320], [pqr], ['ssg']))
                _stmts.append(lambda: _tt(P, 'dve', sm['tmpw'][:R, :], sm['ssg'][:R, :], sm['acs'][:R, :], ALU.subtract, ['ssg', 'acs'], ['tmpw']))
                _stmts.append(lambda: _act(P, sm['tmpw'][:R, :], sm['tmpw'][:R, :], AF.Exp, ['tmpw'], ['tmpw']))
                _stmts.append(lambda: _tt(P, 'dve', sm['wq'][:R, :], sm['tmpw'][:R, :], sm['dt'][:R, :], ALU.mult, ['tmpw', 'dt'], ['wq']))
                for _f in _stmts[:DBGX]:
                    _f()
                xs3 = xs[:R, :].rearrange("p (h d) -> p h d", d=64)
                _tt(P, 'dve', xdt[:R, :].rearrange("p (h d) -> p h d", d=64), xs3,
                    sm['dt'][:R, :].unsqueeze(2).to_broadcast([R, 64, 64]), ALU.mult, ['xs', 'dt'], ['xdt'])
                _tt(P, 'pool', xw[:R, :].rearrange("p (h d) -> p h d", d=64), xs3,
                    sm['wq'][:R, :].unsqueeze(2).to_broadcast([R, 64, 64]), ALU.mult, ['xs', 'wq'], ['xw'])
                _cp(P, 'act', bcb[:R, :], bcf[:R, :], ['bcf'], ['bcb'])
                transpose_into(C, bcb, 'bcb', R, 16, BCT, 'BCT', 0)
                def stageA(g, k):
                    ks = str(k)
                    _mm(P, C.ps[4][:R, 0:R], BCT[:, g, :R], BCT[:, 8 + g, :R], True, True, ['BCT'], ['ps4'])
                    _cp(P, 'act', CBT[k][:R, :R], C.ps[4][:R, 0:R], ['ps4'], ['CBT' + ks])
                    rg3 = rhsg[k][:, 0:8 * R].rearrange("p (r l) -> p r l", l=R)
                    _tt(P, 'pool', rg3, acsT[:, :R].unsqueeze(1).to_broadcast([128, 8, R]),
                        C.idf[:, 8 * g:8 * g + 8].unsqueeze(2).to_broadcast([128, 8, R]), ALU.mult, ['acsT', 'idf'], ['rhsg' + ks])
                    for half in range(2):
                        bb = 2 + half
                        pb_, pbr = C.ps[bb], 'ps%d' % bb
                        _mm(P, pb_[:, 0:4 * R], C.onesf[:, :], rhsg[k][:, half * 4 * R:(half + 1) * 4 * R], True, False,
                            ['onesf', 'rhsg' + ks], [pbr])
                        _mm(P, pb_[:, 0:4 * R], C.idb[:, :], mr[:, 0:4 * R], False, True, ['idb', mrr], [pbr])
                        for rr in range(4):
                            r_ = half * 4 + rr
                            hh = 8 * g + r_
                            _act(P, E[k][:R, r_ * R:(r_ + 1) * R], pb_[:R, rr * R:(rr + 1) * R], AF.Exp, [pbr, 'nacs'], ['E' + ks],
                                 bias=sm['nacs'][:R, hh:hh + 1], scale=1.0)
                    _tt(P, 'dve', MT[k][:R, 0:8 * R].rearrange("p (r l) -> p r l", l=R),
                        E[k][:R, 0:8 * R].rearrange("p (r l) -> p r l", l=R),
                        CBT[k][:R, :R].unsqueeze(1).to_broadcast([R, 8, R]), ALU.mult, ['E' + ks, 'CBT' + ks], ['MT' + ks])

                def stageB(g, k):
                    ks = str(k)
                    by, bo, bs = 5, 6, 7
                    for r_ in range(8):
                        hh = 8 * g + r_
                        _mm(P, C.ps[by][:R, r_ * 64:(r_ + 1) * 64], MT[k][:R, r_ * R:(r_ + 1) * R], xdt[:R, hh * 64:(hh + 1) * 64],
                            True, True, ['MT' + ks, 'xdt'], ['ps%d' % by])
                    _mm(P, C.ps[bo][:R, 0:512], BCT[:, 8 + g, :R], Sbf[:, g * 512:(g + 1) * 512], True, True,
                        ['BCT', 'Sbf%d' % g], ['ps%d' % bo])
                    _tt(P, 'dve', t1[k][:R, :].rearrange("p (r d) -> p r d", d=64),
                        C.ps[bo][:R, 0:512].rearrange("p (r d) -> p r d", d=64),
                        sm['expA'][:R, 8 * g:8 * g + 8].unsqueeze(2).to_broadcast([R, 8, 64]), ALU.mult,
                        ['ps%d' % bo, 'expA'], ['t1' + ks])
                    _tt(P, 'dve', ysb[:R, g * 512:(g + 1) * 512], C.ps[by][:R, 0:512], t1[k][:R, :], ALU.add,
                        ['ps%d' % by, 't1' + ks], ['ysb%d' % g])
                    _mm(P, C.ps[bs][:, 0:512], bcb[:R, g * 128:(g + 1) * 128], xw[:R, g * 512:(g + 1) * 512], True, True,
                        ['bcb', 'xw'], ['ps%d' % bs])
                    stg = ST[:, g * 512:(g + 1) * 512]
                    _tt(P, 'pool', stg.rearrange("p (r d) -> p r d", d=64), stg.rearrange("p (r d) -> p r d", d=64),
                        sm['decayc'][:, 8 * g:8 * g + 8].unsqueeze(2).to_broadcast([128, 8, 64]), ALU.mult,
                        ['ST%d' % g, 'decayc'], ['ST%d' % g])
                    _tt(P, 'dve', stg, stg, C.ps[bs][:, 0:512], ALU.add, ['ST%d' % g, 'ps%d' % bs], ['ST%d' % g])
                    _cp(P, 'pool', Sbf[:, g * 512:(g + 1) * 512], stg, ['ST%d' % g], ['Sbf%d' % g])

                kk = [(it + g) % 2 for g in range(8)]
                it += 8
                stageA(0, kk[0])
                for g in range(1, 8):
                    stageA(g, kk[g])
                    stageB(g - 1, kk[g - 1])
                stageB(7, kk[7])
                YS = ['ysb%d' % g for g in range(8)]
                _tt(P, 'pool', xs3, xs3, sm['dbc'][:R, :].unsqueeze(2).to_broadcast([R, 64, 64]), ALU.mult, ['xs', 'dbc'], ['xs'])
                _tt(P, 'dve', ysb[:R, :], ysb[:R, :], xs[:R, :], ALU.add, YS + ['xs'], YS)
                _act(P, zt[:R, :], zt[:R, :], AF.Silu, ['zt'], ['zt'])
                _tt(P, 'dve', ysb[:R, :], ysb[:R, :], zt[:R, :], ALU.mult, YS + ['zt'], YS)
                for g in range(8):
                    _act(P, ynb[:R, g * 512:(g + 1) * 512], ysb[:R, g * 512:(g + 1) * 512], AF.Square, YS, ['ynb', 'ssg'],
                         accum_out=sm['ssg'][:R, g:g + 1])
                _act(P, sm['rsg'][:R, 0:8], sm['ssg'][:R, 0:8], AF.Sqrt, ['ssg'], ['rsg'], scale=1.0 / 512, bias=C.epsb[:R, :])
                _recip(P, sm['rstdg'][:R, 0:8], sm['rsg'][:R, 0:8], ['rsg'], ['rstdg'])
                ys3 = ysb[:R, :].rearrange("p (g c) -> p g c", c=512)
                _tt(P, 'dve', ys3, ys3, sm['rstdg'][:R, 0:8].unsqueeze(2).to_broadcast([R, 8, 512]), ALU.mult, YS + ['rstdg'], YS)
                _tt(P, 'dve', ynb[:R, :], ysb[:R, :], nwbc[:R, :], ALU.mult, YS + ['nwbc'], ['ynb'])
                transpose_into(C, ynb, 'ynb', R, 32, YTt, 'YTt', 0)
                _dma(P, 'sp', D['YT'][s, :, :, r0:r0 + R], YTt[:, :, :R], ['YTt'], ['YTd_%d' % s])
        P.bankset = None
        P.flush()


def ssm_part4(C, H, j):
    nc, P, cfg, D = C.nc, C.P, C.cfg, C.D
    BC = cfg.BT * 128
    with ExitStack() as es:
        def A(name, shape, dt):
            return es.enter_context(nc.sbuf_tensor(_uname(name), shape, dt))
        yT = [A("s4_yT%d" % k, [128, 32, BC], BF16) for k in range(2)]
        wb = [A("s4_w%d" % k, [128, 8, 512], BF16) for k in range(3)]
        C.hres = [A("s4_hr%d" % k, [128, 512], F32) for k in range(3)]
        epi = residual_epi(C, H, 1.0)
        for bi, blk in enumerate(cfg.blocks()):
            k = bi % 2
            for (s, t, r0, rows, col0) in blk:
                _dma(P, 'sp', yT[k][:, :, col0:col0 + rows], D['YT'][s, :, :, r0:r0 + rows], ['YTd_%d' % s], ['yT%d' % k])
            gemm_tm(C, yT[k], 'yT%d' % k, 32, blk, D['ssm_wout'][j], cfg.DM, epi, wb, 'ow')
        P.flush()


def ssm_phase(C, H, j, i):
    ssm_part1(C, H, j, i)
    ssm_part2(C, j)
    ssm_part3(C, j)
    ssm_part4(C, H, j)


def host_ssm_layout(cfg, w_in, w_out):
    nB = w_in.shape[0]
    win_r = np.ascontiguousarray(w_in.reshape(nB, cfg.KC, 128, S_PROJ).transpose(0, 2, 1, 3))
    wout_r = np.ascontiguousarray(w_out.reshape(nB, 32, 128, cfg.DM).transpose(0, 2, 1, 3))
    return win_r, wout_r


_NC_CACHE = {}


def kernel(x, meta_tokens, norm_ffn1, ffn1_w_in, ffn1_w_out, norm_mix, norm_ffn2, ffn2_w_in, ffn2_w_out,
           mla_w_in, mla_q_norm, mla_w_uq, mla_kv_norm, mla_w_ukv, mla_w_o,
           ssm_w_in, ssm_conv_w, ssm_conv_b, ssm_dt_bias, ssm_a_log, ssm_d, ssm_norm, ssm_w_out,
           final_norm):
    from concourse.bass_utils import run_bass_kernel_spmd
    f = lambda a: np.ascontiguousarray(np.asarray(a, dtype=np.float32))
    x = f(x)
    B, SEQ, DM = x.shape
    NCORES = 8
    NSEQ = B // NCORES
    DEPTH = norm_ffn1.shape[0]
    cfg = Cfg(SEQ=SEQ, NSEQ=NSEQ, DEPTH=DEPTH, DM=DM, DFF=ffn1_w_out.shape[1])
    key = (SEQ, NSEQ, DEPTH, DM, cfg.DFF)
    if key not in _NC_CACHE:
        _NC_CACHE[key] = build(cfg)
    nc = _NC_CACHE[key]
    shared = {}
    shared['meta_tokens'] = f(meta_tokens)
    shared['norm_ffn1'] = f(norm_ffn1).reshape(DEPTH, 1, DM)
    shared['norm_mix'] = f(norm_mix).reshape(DEPTH, 1, DM)
    shared['norm_ffn2'] = f(norm_ffn2).reshape(DEPTH, 1, DM)
    shared['final_norm'] = f(final_norm).reshape(1, DM)
    shared['ffn1_win_r'], shared['ffn1_wout_r'] = host_ffn_layout(cfg, f(ffn1_w_in), f(ffn1_w_out))
    shared['ffn2_win_r'], shared['ffn2_wout_r'] = host_ffn_layout(cfg, f(ffn2_w_in), f(ffn2_w_out))
    (shared['mla_win_r'], shared['mla_wuq_r'], shared['mla_wukv_r'], shared['mla_wo_r']) = host_mla_layout(
        cfg, f(mla_w_in), f(mla_w_uq), f(mla_w_ukv), f(mla_w_o))
    nA = mla_w_in.shape[0]
    shared['mla_q_norm'] = f(mla_q_norm).reshape(nA, 1, 512)
    shared['mla_kv_norm'] = f(mla_kv_norm).reshape(nA, 1, 512)
    shared['ssm_win_r'], shared['ssm_wout_r'] = host_ssm_layout(cfg, f(ssm_w_in), f(ssm_w_out))
    nB = ssm_w_in.shape[0]
    shared['ssm_conv_w'] = f(ssm_conv_w)
    shared['ssm_conv_b'] = f(ssm_conv_b).reshape(nB, 1, S_CONV)
    shared['ssm_dt_bias'] = f(ssm_dt_bias).reshape(nB, 1, S_H)
    shared['ssm_a_log'] = f(ssm_a_log).reshape(nB, 1, S_H)
    shared['ssm_d'] = f(ssm_d).reshape(nB, 1, S_H)
    shared['ssm_norm'] = f(ssm_norm).reshape(nB, 1, S_IN)
    shared.update(host_consts(cfg))
    in_maps = []
    for c in range(NCORES):
        m = dict(shared)
        m['x'] = x[c * NSEQ:(c + 1) * NSEQ]
        in_maps.append(m)
    res = run_bass_kernel_spmd(nc, in_maps, core_ids=list(range(NCORES)))
    out = np.concatenate([res.results[c]['out'] for c in range(NCORES)], axis=0)
    return out.astype(np.float32)
```

```python
import numpy as np
from contextlib import ExitStack
import concourse.bass as bass
import concourse.mybir as mybir

F32 = mybir.dt.float32
BF16 = mybir.dt.bfloat16
AF = mybir.ActivationFunctionType
ALU = mybir.AluOpType
AX = mybir.AxisListType

ENGS = ['pe', 'act', 'dve', 'pool', 'sp']
NS_DMA = 12


class Ins:
    __slots__ = ('fn', 'waits', 'signal', 'dma', 'dsem', 'dval', 'fn_was_real')

    def __init__(self, fn, dma=False):
        self.fn = fn
        self.fn_was_real = fn is not None
        self.waits = []
        self.signal = False
        self.dma = dma
        self.dsem = None
        self.dval = 0


class Prog:
    def __init__(self, nc):
        self.nc = nc
        self.q = {e: [] for e in ENGS}
        self.lastw = {}
        self.readers = {}
        self.seen = {e: {} for e in ENGS}
        self.ndma = {e: 0 for e in ENGS}
        self.dma_tok = {e: {} for e in ENGS}

    def _need(self, eng, ins, tok):
        if tok is None:
            return
        kind = tok[0]
        if kind == 'c':
            _, f, i = tok
            if f == eng and eng == 'pe':
                return
            key = ('c', f)
            if self.seen[eng].get(key, -1) >= i:
                return
            self.seen[eng][key] = i
            self.q[f][i].signal = True
            ins.waits.append(tok)
        else:
            _, qn, j = tok
            s = j % NS_DMA
            v = 16 * (j // NS_DMA + 1)
            key = ('d', qn, s)
            if self.seen[eng].get(key, 0) >= v:
                return
            self.seen[eng][key] = v
            ins.waits.append(tok)

    def _add(self, eng, fn, r, w, dma):
        ins = Ins(fn, dma)
        idx = len(self.q[eng])
        if dma:
            j = self.ndma[eng]
            self.ndma[eng] += 1
            tok = ('d', eng, j)
            if j >= NS_DMA:
                self._need(eng, ins, ('d', eng, j - NS_DMA))
        else:
            tok = ('c', eng, idx)
        for res in r:
            self._need(eng, ins, self.lastw.get(res))
        for res in w:
            self._need(eng, ins, self.lastw.get(res))
            rd = self.readers.get(res)
            if rd is not None:
                for f, i in rd[0].items():
                    if (not dma) and f == eng:
                        continue
                    self._need(eng, ins, ('c', f, i))
                for t in rd[1]:
                    self._need(eng, ins, t)
        for res in r:
            rd = self.readers.setdefault(res, ({}, []))
            if dma:
                rd[1].append(tok)
            else:
                rd[0][eng] = idx
        for res in w:
            self.lastw[res] = tok
            self.readers.pop(res, None)
        self.q[eng].append(ins)
        return tok

    def op(self, eng, fn, r=(), w=()):
        return self._add(eng, fn, r, w, False)

    def dma(self, eng, fn, r=(), w=()):
        return self._add(eng, fn, r, w, True)

    def _setup(self):
        nc = self.nc
        self.csem = {e: nc.alloc_semaphore('c_' + e) for e in ENGS}
        self.dsem = {e: [nc.alloc_semaphore('d_%s_%d' % (e, i)) for i in range(NS_DMA)]
                     for e in ('sp', 'pool', 'act')}
        self.emitted = {e: 0 for e in ENGS}
        self.cnt = {e: [] for e in ENGS}
        self.sig = {e: 0 for e in ENGS}
        self.jd = {e: 0 for e in ENGS}
        self.engobj = {'pe': nc.tensor, 'act': nc.scalar, 'dve': nc.vector, 'pool': nc.gpsimd, 'sp': nc.sync}

    def flush(self):
        nc = self.nc
        if not hasattr(self, 'csem'):
            self._setup()
        ctoks = []
        for f in ENGS:
            n = len(self.q[f])
            if n > self.emitted[f]:
                ctoks.append(('c', f, n - 1))
        dtoks = []
        for qn in ('sp', 'pool', 'act'):
            n = self.ndma[qn]
            for j in range(max(0, n - NS_DMA), n):
                dtoks.append(('d', qn, j))
        bars = {}
        for e in ENGS:
            ins = Ins(None)
            for t in ctoks:
                if t[1] != e:
                    self._need(e, ins, t)
            for t in dtoks:
                self._need(e, ins, t)
            bars[e] = ins
        for e in ENGS:
            self.q[e].append(bars[e])
        for e in ENGS:
            c = self.sig[e]
            for ins in self.q[e][self.emitted[e]:]:
                if ins.signal and not ins.dma and ins.fn is not None:
                    c += 1
                self.cnt[e].append(c)
            self.sig[e] = c

        def do_wait(eobj, tok):
            if tok[0] == 'c':
                _, f, i = tok
                eobj.wait_ge(self.csem[f], self.cnt[f][i])
            else:
                _, qn, j = tok
                eobj.wait_ge(self.dsem[qn][j % NS_DMA], 16 * (j // NS_DMA + 1))

        def run(ename):
            lo = self.emitted[ename]
            hi = len(self.q[ename])

            def body(eobj):
                for ins in self.q[ename][lo:hi]:
                    for tok in ins.waits:
                        do_wait(eobj, tok)
                    if ins.fn is None:
                        continue
                    bi = ins.fn(eobj)
                    if ins.dma:
                        bi.then_inc(self.dsem[ename][self.jd[ename] % NS_DMA], 16)
                        self.jd[ename] += 1
                    elif ins.signal:
                        bi.then_inc(self.csem[ename], 1)
            return body

        with nc.Block() as block:
            block.tensor(run('pe'))
            block.scalar(run('act'))
            block.vector(run('dve'))
            block.gpsimd(run('pool'))
            block.sync(run('sp'))
        for e in ENGS:
            self.emitted[e] = len(self.q[e])
            for ins in self.q[e]:
                ins.fn = None if ins.fn is None else ins.fn
        self.lastw_phase_clear()

    def lastw_phase_clear(self):
        self.lastw = {k: v for k, v in self.lastw.items() if v[0] == 'd'}
        self.readers = {k: ({}, v[1]) for k, v in self.readers.items() if v[1]}

    def nextbank(self):
        bs = getattr(self, 'bankset', None)
        if bs:
            i = getattr(self, '_bsi', 0)
            self._bsi = i + 1
            return bs[i % len(bs)]
        b = getattr(self, '_bank', 0)
        self._bank = (b + 1) % 8
        return b


def _mm(P, out, lhsT, rhs, start, stop, r, w):
    P.op('pe', lambda e: e.matmul(out, lhsT=lhsT, rhs=rhs, start=start, stop=stop), r, w)


def _tr(P, out, in_, ident, r, w):
    P.op('pe', lambda e: e.transpose(out, in_, ident), r, w)


def _act(P, out, in_, func, r, w, bias=None, scale=None, accum_out=None):
    kw = {}
    if bias is not None:
        kw['bias'] = bias
    if scale is not None:
        kw['scale'] = scale
    if accum_out is not None:
        kw['accum_out'] = accum_out
    P.op('act', lambda e: e.activation(out=out, in_=in_, func=func, **kw), r, w)


def _tt(P, eng, out, in0, in1, op, r, w):
    P.op(eng, lambda e: e.tensor_tensor(out=out, in0=in0, in1=in1, op=op), r, w)


def _ts(P, eng, out, in0, s1, s2, op0, op1, r, w):
    if s2 is None:
        P.op(eng, lambda e: e.tensor_scalar(out=out, in0=in0, scalar1=s1, scalar2=None, op0=op0), r, w)
    else:
        P.op(eng, lambda e: e.tensor_scalar(out=out, in0=in0, scalar1=s1, scalar2=s2, op0=op0, op1=op1), r, w)


def _stt(P, out, in0, scalar, in1, op0, op1, r, w):
    P.op('dve', lambda e: e.scalar_tensor_tensor(out=out, in0=in0, scalar=scalar, in1=in1, op0=op0, op1=op1), r, w)


def _cp(P, eng, out, in_, r, w):
    if eng == 'act':
        P.op('act', lambda e: e.activation(out=out, in_=in_, func=AF.Copy), r, w)
    else:
        P.op(eng, lambda e: e.tensor_copy(out=out, in_=in_), r, w)


def _red(P, out, in_, op, r, w):
    P.op('dve', lambda e: e.tensor_reduce(out=out, in_=in_, axis=AX.X, op=op), r, w)


def _recip(P, out, in_, r, w):
    P.op('dve', lambda e: e.reciprocal(out=out, in_=in_), r, w)


def _dma(P, eng, out, in_, r, w):
    P.dma(eng, lambda e: e.dma_start(out=out, in_=in_), r, w)


def _memset(P, eng, ap, val, w):
    P.op(eng, lambda e: e.memset(ap, val), (), w)


EPS = 1e-6
NEG = -30000.0
N_META = 16


class Cfg:
    def __init__(self, SEQ=2048, NSEQ=2, DEPTH=4, DM=2048, DFF=5504):
        self.SEQ, self.NSEQ, self.DEPTH, self.DM, self.DFF = SEQ, NSEQ, DEPTH, DM, DFF
        self.L = SEQ + N_META
        self.NT = 1 + SEQ // 128
        self.KC = DM // 128
        self.NF = DFF // 128
        self.BT = 7

    def tile(self, t):
        if t == 0:
            return 0, N_META
        return N_META + (t - 1) * 128, 128

    def blocks(self, seqs=None):
        tl = []
        for s in (range(self.NSEQ) if seqs is None else seqs):
            for t in range(1, self.NT):
                tl.append((s, t))
            tl.append((s, 0))
        out = []
        for b0 in range(0, len(tl), self.BT):
            blk = []
            c = 0
            for (s, t) in tl[b0:b0 + self.BT]:
                r0, rows = self.tile(t)
                blk.append((s, t, r0, rows, c))
                c += rows
            out.append(blk)
        return out


def col_groups(ncols, w=512):
    return [(c, min(c + w, ncols)) for c in range(0, ncols, w)]


class Ctx:
    pass


_UID = [0]


def _uname(name):
    _UID[0] += 1
    return '%s_u%d' % (name, _UID[0])


def hres(s, t):
    return 'H_%d_%d' % (s, t)


def norm_T(C, src_ap, src_res, rows, gbc, AT, at_res, col0, i):
    P, cfg = C.P, C.cfg
    DM, KC = cfg.DM, cfg.KC
    k = i % 2
    hb, hn = C.hbuf[k], C.hn[k]
    hbr, hnr = 'hbuf%d' % k, 'hn%d' % k
    _dma(P, 'sp', hb[:rows, :], src_ap, [src_res], [hbr])
    _act(P, C.sq[:rows, :], hb[:rows, :], AF.Square, [hbr], ['sq', 'ss%d' % k], accum_out=C.ss[:rows, k:k + 1])
    _act(P, C.rs[:rows, k:k + 1], C.ss[:rows, k:k + 1], AF.Sqrt, ['ss%d' % k], ['rs%d' % k], scale=1.0 / DM, bias=C.epsb[:rows, :])
    _recip(P, C.rstd[:rows, k:k + 1], C.rs[:rows, k:k + 1], ['rs%d' % k], ['rstd%d' % k])
    _stt(P, hn[:rows, :], hb[:rows, :], C.rstd[:rows, k:k + 1], gbc[:rows, :], ALU.mult, ALU.mult,
         [hbr, 'rstd%d' % k, 'gbc'], [hnr])
    transpose_into(C, hn, hnr, rows, KC, AT, at_res, col0)


def transpose_into(C, src, src_res, rows, nch, AT, at_res, col0, chw=128):
    P = C.P
    for g4 in range(0, nch, 4):
        n4 = min(4, nch - g4)
        b = P.nextbank()
        pb = C.psb[b]
        for j in range(n4):
            ch = g4 + j
            _tr(P, pb[:chw, j * 128:j * 128 + rows], src[:rows, ch * chw:(ch + 1) * chw], C.idb[:rows, :rows],
                [src_res, 'idb'], ['ps%d' % b])
        src_v = pb[:chw, 0:n4 * 128].rearrange("p (j c) -> p j c", c=128)[:, :, :rows]
        eng = 'act' if (C.evac_ctr % 2 == 0) else 'dve'
        C.evac_ctr += 1
        _cp(P, eng, AT[:chw, g4:g4 + n4, col0:col0 + rows], src_v, ['ps%d' % b], [at_res])


def gemm_tm(C, AT, at_res, KC, blk, w_r, N, epi, wbufs, wres, kstep=8, gw=512):
    P = C.P
    for (c0, c1) in col_groups(N, gw):
        n = c0 // gw
        w = c1 - c0
        banks = [P.nextbank() for _ in blk]
        for k0 in range(0, KC, kstep):
            k1 = min(KC, k0 + kstep)
            wb = C.wctr % len(wbufs)
            C.wctr += 1
            wt, wr_ = wbufs[wb], '%s%d' % (wres, wb)
            _dma(P, 'pool', wt[:, 0:k1 - k0, 0:w], w_r[:, k0:k1, c0:c1], [], [wr_])
            for kc in range(k0, k1):
                for ti, (s, t, r0, rows, col0) in enumerate(blk):
                    _mm(P, C.ps[banks[ti]][:rows, 0:w], AT[:, kc, col0:col0 + rows], wt[:, kc - k0, 0:w],
                        kc == 0, kc == KC - 1, [at_res, wr_], ['ps%d' % banks[ti]])
        for ti, te in enumerate(blk):
            epi(n, (c0, c1), te, C.ps[banks[ti]][:te[3], 0:w], 'ps%d' % banks[ti])


def residual_epi(C, H, coef):
    P = C.P

    def epi(n, cc, te, psap, psres):
        s, t, r0, rows, col0 = te
        k = C.hres_ctr % len(C.hres)
        C.hres_ctr += 1
        hr, hrr = C.hres[k], 'hres%d' % k
        w = cc[1] - cc[0]
        _dma(P, 'sp', hr[:rows, 0:w], H[s, r0:r0 + rows, cc[0]:cc[1]], [hres(s, t)], [hrr])
        _stt(P, hr[:rows, 0:w], psap, coef, hr[:rows, 0:w], ALU.mult, ALU.add, [psres, hrr], [hrr])
        _dma(P, 'sp', H[s, r0:r0 + rows, cc[0]:cc[1]], hr[:rows, 0:w], [hrr], [hres(s, t)])
    return epi


def ffn_phase(C, H, gain_d, win_r, wout_r):
    nc, P, cfg = C.nc, C.P, C.cfg
    DM, KC, NF = cfg.DM, cfg.KC, cfg.NF
    BC = cfg.BT * 128
    with ExitStack() as es:
        def A(name, shape, dt):
            return es.enter_context(nc.sbuf_tensor(_uname(name), shape, dt))
        gbc = A("f_gbc", [128, DM], F32)
        xnT = A("f_xnT", [128, KC, BC], BF16)
        actT = A("f_actT", [128, NF, BC], BF16)
        hb0 = A("f_hb0", [128, DM], F32)
        hb1 = A("f_hb1", [128, DM], F32)
        hn0 = A("f_hn0", [128, DM], BF16)
        hn1 = A("f_hn1", [128, DM], BF16)
        sq = A("f_sq", [128, DM], BF16)
        win0 = A("f_win0", [128, KC, 256], BF16)
        win1 = A("f_win1", [128, KC, 256], BF16)
        wo0 = A("f_wo0", [128, 8, 512], BF16)
        wo1 = A("f_wo1", [128, 8, 512], BF16)
        wo2 = A("f_wo2", [128, 8, 512], BF16)
        sg0 = A("f_sg0", [128, 512], F32)
        sg1 = A("f_sg1", [128, 512], F32)
        hr0 = A("f_hr0", [128, 512], F32)
        hr1 = A("f_hr1", [128, 512], F32)
        hr2 = A("f_hr2", [128, 512], F32)
        C.hbuf, C.hn, C.sq = [hb0, hb1], [hn0, hn1], sq
        C.hres = [hr0, hr1, hr2]
        wins = [win0, win1]
        sgs = [sg0, sg1]
        _dma(P, 'sp', gbc[:, :], gain_d.partition_broadcast(128), [], ['gbc'])
        epi = residual_epi(C, H, 0.5)
        for blk in cfg.blocks():
            ncols = sum(te[3] for te in blk)
            for i, (s, t, r0, rows, col0) in enumerate(blk):
                norm_T(C, H[s, r0:r0 + rows, :], hres(s, t), rows, gbc, xnT, 'xnT', col0, i)
            groups = col_groups(ncols)
            for f in range(NF):
                wb = f % 2
                wt, wr_ = wins[wb], 'win%d' % wb
                _dma(P, 'pool', wt[:, :, :], win_r[f], [], [wr_])
                for (c0, c1) in groups:
                    w = c1 - c0
                    bg, bu = P.nextbank(), P.nextbank()
                    for kc in range(KC):
                        _mm(P, C.ps[bg][:, 0:w], wt[:, kc, 0:128], xnT[:, kc, c0:c1], kc == 0, kc == KC - 1,
                            [wr_, 'xnT'], ['ps%d' % bg])
                    for kc in range(KC):
                        _mm(P, C.ps[bu][:, 0:w], wt[:, kc, 128:256], xnT[:, kc, c0:c1], kc == 0, kc == KC - 1,
                            [wr_, 'xnT'], ['ps%d' % bu])
                    k = C.sg_ctr % 2
                    C.sg_ctr += 1
                    _act(P, sgs[k][:, 0:w], C.ps[bg][:, 0:w], AF.Silu, ['ps%d' % bg], ['sg%d' % k])
                    _tt(P, 'dve', actT[:, f, c0:c1], sgs[k][:, 0:w], C.ps[bu][:, 0:w], ALU.mult,
                        ['sg%d' % k, 'ps%d' % bu], ['actT'])
            gemm_tm(C, actT, 'actT', NF, blk, wout_r, DM, epi, [wo0, wo1, wo2], 'wo')
        P.flush()


def out_phase(C, H, gain_d, out_d, raw=False):
    nc, P, cfg = C.nc, C.P, C.cfg
    DM = cfg.DM
    with ExitStack() as es:
        def A(name, shape, dt):
            return es.enter_context(nc.sbuf_tensor(_uname(name), shape, dt))
        gbc = A("o_gbc", [128, DM], F32)
        hb = [A("o_hb0", [128, DM], F32), A("o_hb1", [128, DM], F32)]
        ob = [A("o_ob0", [128, DM], F32), A("o_ob1", [128, DM], F32)]
        sq = A("o_sq", [128, DM], BF16)
        _dma(P, 'sp', gbc[:, :], gain_d.partition_broadcast(128), [], ['gbc'])
        i = 0
        for s in range(cfg.NSEQ):
            for t in range(1, cfg.NT):
                r0, rows = cfg.tile(t)
                k = i % 2
                i += 1
                hbr, obr = 'ohb%d' % k, 'oob%d' % k
                _dma(P, 'sp', hb[k][:rows, :], H[s, r0:r0 + rows, :], [hres(s, t)], [hbr])
                if raw:
                    _dma(P, 'sp', out_d[s, r0 - N_META:r0 - N_META + rows, :], hb[k][:rows, :], [hbr], ['out_%d_%d' % (s, t)])
                    continue
                _act(P, sq[:rows, :], hb[k][:rows, :], AF.Square, [hbr], ['sq', 'ss%d' % k], accum_out=C.ss[:rows, k:k + 1])
                _act(P, C.rs[:rows, k:k + 1], C.ss[:rows, k:k + 1], AF.Sqrt, ['ss%d' % k], ['rs%d' % k], scale=1.0 / DM, bias=C.epsb[:rows, :])
                _recip(P, C.rstd[:rows, k:k + 1], C.rs[:rows, k:k + 1], ['rs%d' % k], ['rstd%d' % k])
                _stt(P, ob[k][:rows, :], hb[k][:rows, :], C.rstd[:rows, k:k + 1], gbc[:rows, :], ALU.mult, ALU.mult,
                     [hbr, 'rstd%d' % k, 'gbc'], [obr])
                _dma(P, 'sp', out_d[s, r0 - N_META:r0 - N_META + rows, :], ob[k][:rows, :], [obr], ['out_%d_%d' % (s, t)])
        P.flush()


def build(cfg, plan=None):
    nc = bass.Bass("TRN2", target_bir_lowering=False)
    DM, DFF, L, NSEQ, SEQ, DEPTH = cfg.DM, cfg.DFF, cfg.L, cfg.NSEQ, cfg.SEQ, cfg.DEPTH
    KC, NF = cfg.KC, cfg.NF
    nA, nB = (DEPTH + 1) // 2, DEPTH // 2

    def din(name, shape, dt=F32):
        return nc.dram_tensor(name, list(shape), dt, kind="ExternalInput").ap()
    D = {}
    D['x'] = din('x', [NSEQ, SEQ, DM])
    D['meta'] = din('meta_tokens', [N_META, DM])
    for nm in ('norm_ffn1', 'norm_mix', 'norm_ffn2'):
        D[nm] = din(nm, [DEPTH, 1, DM])
    D['final_norm'] = din('final_norm', [1, DM])
    for nm in ('ffn1', 'ffn2'):
        D[nm + '_win'] = din(nm + '_win_r', [DEPTH, NF, 128, KC, 256])
        D[nm + '_wout'] = din(nm + '_wout_r', [DEPTH, 128, NF, DM])
    D['c_ident'] = din('c_ident', [128, 128])
    D['c_tri'] = din('c_tri', [128, 128])
    D['c_maskA'] = din('c_maskA', [128, 128])
    D['c_cos'] = din('c_cos', [L, 32])
    D['c_sin'] = din('c_sin', [L, 32])
    mixer_decl(nc, cfg, D, din, nA, nB)
    out_d = nc.dram_tensor('out', [NSEQ, SEQ, DM], F32, kind="ExternalOutput").ap()
    H = nc.dram_tensor('Hres', [NSEQ, L, DM], F32, kind="Internal").ap()

    C = Ctx()
    C.nc, C.cfg, C.D = nc, cfg, D
    C.P = P = Prog(nc)
    C.evac_ctr = C.wctr = C.hres_ctr = C.sg_ctr = 0
    C.ps = [nc.alloc_psum_tensor('psb%d' % i, [128, 512], F32) for i in range(8)]
    C.psb = [p[:, :].bitcast(BF16) for p in C.ps]
    C.ps = [p[:, :] for p in C.ps]
    C.idb = nc.alloc_sbuf_tensor('c_idb', [128, 128], BF16)
    C.idf = nc.alloc_sbuf_tensor('c_idf', [128, 128], F32)
    C.tri = nc.alloc_sbuf_tensor('c_trif', [128, 128], F32)
    C.maskA = nc.alloc_sbuf_tensor('c_maskAf', [128, 128], F32)
    C.onesf = nc.alloc_sbuf_tensor('c_onesf', [128, 128], F32)
    C.epsb = nc.alloc_sbuf_tensor('c_epsb', [128, 1], F32)
    C.ss = nc.alloc_sbuf_tensor('c_ss', [128, 2], F32)
    C.rs = nc.alloc_sbuf_tensor('c_rs', [128, 2], F32)
    C.rstd = nc.alloc_sbuf_tensor('c_rstd', [128, 2], F32)
    _dma(P, 'pool', C.idb[:, :], D['c_ident'], [], ['idb'])
    _dma(P, 'sp', C.idf[:, :], D['c_ident'], [], ['idf'])
    _dma(P, 'sp', C.tri[:, :], D['c_tri'], [], ['tri'])
    _dma(P, 'sp', C.maskA[:, :], D['c_maskA'], [], ['maskA'])
    _memset(P, 'dve', C.onesf[:, :], 1.0, ['onesf'])
    _memset(P, 'dve', C.epsb[:, :], EPS, ['epsb'])
    for s in range(NSEQ):
        _dma(P, 'sp', H[s, 0:N_META, :], D['meta'], [], [hres(s, 0)])
        for t in range(1, cfg.NT):
            r0, rows = cfg.tile(t)
            _dma(P, 'sp', H[s, r0:r0 + rows, :], D['x'][s, r0 - N_META:r0 - N_META + rows, :], [], [hres(s, t)])
    P.flush()
    if plan is None:
        plan = []
        for i in range(DEPTH):
            plan += [('ffn1', i), ('mla' if i % 2 == 0 else 'ssm', i // 2, i), ('ffn2', i)]
        plan += [('out',)]
    for st in plan:
        if st[0] in ('ffn1', 'ffn2'):
            i = st[1]
            ffn_phase(C, H, D['norm_' + st[0]][i], D[st[0] + '_win'][i], D[st[0] + '_wout'][i])
        elif st[0] == 'mla':
            mla_phase(C, H, st[1], st[2])
        elif st[0] == 'ssm':
            ssm_phase(C, H, st[1], st[2])
        elif st[0] == 'ssm1':
            ssm_part1(C, H, st[1], st[2])
        elif st[0] == 'ssm2':
            ssm_part2(C, st[1])
        elif st[0] == 'ssm3':
            ssm_part3(C, st[1])
        elif st[0] == 'ssm4':
            ssm_part4(C, H, st[1])
        elif st[0] == 'out':
            out_phase(C, H, D['final_norm'], out_d)
        elif st[0] == 'raw':
            out_phase(C, H, D['final_norm'], out_d, raw=True)
    return nc


def mixer_decl(nc, cfg, D, din, nA, nB):
    mla_decl(nc, cfg, D, din, nA)
    ssm_decl(nc, cfg, D, din, nB)


def host_consts(cfg):
    L = cfg.L
    ident = np.eye(128, dtype=np.float32)
    k = np.arange(128)
    tri = (k[:, None] <= k[None, :]).astype(np.float32)
    maskA = np.where(k[None, :] <= k[:, None], 0.0, NEG).astype(np.float32)
    inv_freq = (1.0 / (10000.0 ** (np.arange(0, 64, 2, dtype=np.float32) / np.float32(64)))).astype(np.float32)
    ang = np.arange(L, dtype=np.float32)[:, None] * inv_freq[None, :]
    return dict(c_ident=ident, c_tri=tri, c_maskA=maskA, c_maskS=np.ascontiguousarray(maskA.T),
                c_cos=np.cos(ang).astype(np.float32), c_sin=np.sin(ang).astype(np.float32))


def host_ffn_layout(cfg, w_in, w_out):
    Dp = w_in.shape[0]
    KC, NF, DFF, DM = cfg.KC, cfg.NF, cfg.DFF, cfg.DM
    g = w_in[:, :, :DFF].reshape(Dp, KC, 128, NF, 128)
    u = w_in[:, :, DFF:].reshape(Dp, KC, 128, NF, 128)
    gu = np.concatenate([g, u], axis=-1)
    win_r = np.ascontiguousarray(gu.transpose(0, 3, 2, 1, 4))
    wout_r = np.ascontiguousarray(w_out.reshape(Dp, NF, 128, DM).transpose(0, 2, 1, 3))
    return win_r, wout_r


MLA_H = 16
ATT_SCALE = float(192 ** -0.5)


def rope(C, src, cs, sn, dst, rows, nh, rres, wres):
    P = C.P
    a, b = C.ropeA[:rows, 0:nh, :], C.ropeB[:rows, 0:nh, :]
    csb = cs[:rows, :].unsqueeze(1).to_broadcast([rows, nh, 32])
    snb = sn[:rows, :].unsqueeze(1).to_broadcast([rows, nh, 32])
    t1, t2 = src[:, :, 0:32], src[:, :, 32:64]
    _tt(P, 'dve', a, t1, csb, ALU.mult, rres, ['ropeA'])
    _tt(P, 'dve', b, t2, snb, ALU.mult, rres, ['ropeB'])
    _tt(P, 'dve', dst[:, :, 0:32], a, b, ALU.subtract, ['ropeA', 'ropeB'], wres)
    _tt(P, 'dve', a, t2, csb, ALU.mult, rres, ['ropeA'])
    _tt(P, 'dve', b, t1, snb, ALU.mult, rres, ['ropeB'])
    _tt(P, 'dve', dst[:, :, 32:64], a, b, ALU.add, ['ropeA', 'ropeB'], wres)


def mla_decl(nc, cfg, D, din, nA):
    L, NSEQ = cfg.L, cfg.NSEQ
    D['mla_win'] = din('mla_win_r', [nA, 128, cfg.KC, 1088])
    D['mla_wuq'] = din('mla_wuq_r', [nA, 128, 4, 3072])
    D['mla_wukv'] = din('mla_wukv_r', [nA, 128, 4, 4096])
    D['mla_wo'] = din('mla_wo_r', [nA, 128, 16, cfg.DM])
    D['mla_qn'] = din('mla_q_norm', [nA, 1, 512])
    D['mla_kvn'] = din('mla_kv_norm', [nA, 1, 512])

    def scr(name, shape):
        return nc.dram_tensor(name, shape, BF16, kind="Internal").ap()
    D['QN'] = scr('s_QN', [NSEQ, L, 2048])
    D['QR'] = scr('s_QR', [NSEQ, L, 1024])
    D['KN'] = scr('s_KN', [NSEQ, L, 2048])
    D['V'] = scr('s_V', [NSEQ, L, 2048])
    D['KR'] = scr('s_KR', [NSEQ, L, 64])
    D['OT'] = scr('s_OT', [NSEQ, 128, 16, L])


def mla_part1(C, H, j, i):
    nc, P, cfg, D = C.nc, C.P, C.cfg, C.D
    DM, KC = cfg.DM, cfg.KC
    BC = cfg.BT * 128
    with ExitStack() as es:
        def A(name, shape, dt):
            return es.enter_context(nc.sbuf_tensor(_uname(name), shape, dt))
        gbc = A("m_gbc", [128, DM], F32)
        uT = A("m_uT", [128, KC, BC], BF16)
        C.hbuf = [A("m_hb0", [128, DM], F32), A("m_hb1", [128, DM], F32)]
        C.hn = [A("m_hn0", [128, DM], BF16), A("m_hn1", [128, DM], BF16)]
        C.sq = A("m_sq", [128, DM], BF16)
        cqT = A("m_cqT", [128, 4, BC], BF16)
        ckvT = A("m_ckvT", [128, 4, BC], BF16)
        wb = [A("m_w%d" % k, [128, 8, 512], BF16) for k in range(3)]
        gq = A("m_gq", [128, 512], F32)
        gkv = A("m_gkv", [128, 512], F32)
        cn = [A("m_cn%d" % k, [128, 512], BF16) for k in range(2)]
        st = [A("m_st%d" % k, [128, 512], BF16) for k in range(3)]
        fr = [A("m_fr%d" % k, [128, 512], F32) for k in range(2)]
        cs = [A("m_cs%d" % k, [128, 32], F32) for k in range(2)]
        sn = [A("m_sn%d" % k, [128, 32], F32) for k in range(2)]
        C.ropeA = A("m_ropeA", [128, 8, 32], F32)
        C.ropeB = A("m_ropeB", [128, 8, 32], F32)
        ctr = {'cn': 0, 'st': 0, 'fr': 0, 'cs': 0}
        _dma(P, 'sp', gbc[:, :], D['norm_mix'][i].partition_broadcast(128), [], ['gbc'])
        _dma(P, 'sp', gq[:, :], D['mla_qn'][j].partition_broadcast(128), [], ['gq'])
        _dma(P, 'sp', gkv[:, :], D['mla_kvn'][j].partition_broadcast(128), [], ['gkv'])

        def load_cs(r0, rows):
            k = ctr['cs'] % 2
            ctr['cs'] += 1
            _dma(P, 'sp', cs[k][:rows, :], D['c_cos'][r0:r0 + rows, :], [], ['cs%d' % k])
            _dma(P, 'sp', sn[k][:rows, :], D['c_sin'][r0:r0 + rows, :], [], ['sn%d' % k])
            return k

        def store_bf(psap, psres, rows, w, dst_ap, dst_res):
            k = ctr['st'] % 3
            ctr['st'] += 1
            eng = 'act' if k % 2 == 0 else 'dve'
            _cp(P, eng, st[k][:rows, 0:w], psap, [psres], ['st%d' % k])
            _dma(P, 'sp', dst_ap, st[k][:rows, 0:w], ['st%d' % k], [dst_res])

        def rope_store(psap, psres, te, nh, dst_ap, dst_res):
            s, t, r0, rows, col0 = te
            kf = ctr['fr'] % 2
            ctr['fr'] += 1
            _cp(P, 'act', fr[kf][:rows, 0:nh * 64], psap, [psres], ['fr%d' % kf])
            kc_ = load_cs(r0, rows)
            k = ctr['st'] % 3
            ctr['st'] += 1
            src = fr[kf][:rows, 0:nh * 64].rearrange("p (h d) -> p h d", d=64)
            dst = st[k][:rows, 0:nh * 64].rearrange("p (h d) -> p h d", d=64)
            rope(C, src, cs[kc_], sn[kc_], dst, rows, nh, ['fr%d' % kf, 'cs%d' % kc_, 'sn%d' % kc_], ['st%d' % k])
            _dma(P, 'sp', dst_ap, st[k][:rows, 0:nh * 64], ['st%d' % k], [dst_res])

        for blk in cfg.blocks():
            for ii, (s, t, r0, rows, col0) in enumerate(blk):
                norm_T(C, H[s, r0:r0 + rows, :], hres(s, t), rows, gbc, uT, 'uT', col0, ii)

            def epi_c(n, cc, te, psap, psres):
                s, t, r0, rows, col0 = te
                if n < 2:
                    k = ctr['cn'] % 2
                    ctr['cn'] += 1
                    _act(P, C.sq[:rows, 0:512], psap, AF.Square, [psres], ['sq', 'ss%d' % k], accum_out=C.ss[:rows, k:k + 1])
                    _act(P, C.rs[:rows, k:k + 1], C.ss[:rows, k:k + 1], AF.Sqrt, ['ss%d' % k], ['rs%d' % k], scale=1.0 / 512, bias=C.epsb[:rows, :])
                    _recip(P, C.rstd[:rows, k:k + 1], C.rs[:rows, k:k + 1], ['rs%d' % k], ['rstd%d' % k])
                    g_, gr_ = (gq, 'gq') if n == 0 else (gkv, 'gkv')
                    _stt(P, cn[k][:rows, :], psap, C.rstd[:rows, k:k + 1], g_[:rows, :], ALU.mult, ALU.mult,
                         [psres, 'rstd%d' % k, gr_], ['cn%d' % k])
                    if n == 0:
                        transpose_into(C, cn[k], 'cn%d' % k, rows, 4, cqT, 'cqT', col0)
                    else:
                        transpose_into(C, cn[k], 'cn%d' % k, rows, 4, ckvT, 'ckvT', col0)
                else:
                    rope_store(psap, psres, te, 1, D['KR'][s, r0:r0 + rows, :], 'KR_%d_%d' % (s, t))
            gemm_tm(C, uT, 'uT', KC, blk, D['mla_win'][j], 1088, epi_c, wb, 'mw')

            def epi_q(n, cc, te, psap, psres):
                s, t, r0, rows, col0 = te
                if n < 4:
                    store_bf(psap, psres, rows, 512, D['QN'][s, r0:r0 + rows, cc[0]:cc[1]], 'QN_%d_%d' % (s, t))
                else:
                    rope_store(psap, psres, te, 8, D['QR'][s, r0:r0 + rows, cc[0] - 2048:cc[1] - 2048], 'QR_%d_%d' % (s, t))
            gemm_tm(C, cqT, 'cqT', 4, blk, D['mla_wuq'][j], 3072, epi_q, wb, 'mw', kstep=4)

            def epi_kv(n, cc, te, psap, psres):
                s, t, r0, rows, col0 = te
                if n < 4:
                    store_bf(psap, psres, rows, 512, D['KN'][s, r0:r0 + rows, cc[0]:cc[1]], 'KN_%d_%d' % (s, t))
                else:
                    store_bf(psap, psres, rows, 512, D['V'][s, r0:r0 + rows, cc[0] - 2048:cc[1] - 2048], 'V_%d_%d' % (s, t))
            gemm_tm(C, ckvT, 'ckvT', 4, blk, D['mla_wukv'][j], 4096, epi_kv, wb, 'mw', kstep=4)
        P.flush()


def load_tok(C, dst, dst_res, src, s, c0, c1, rres_fn):
    P, cfg = C.P, C.cfg
    rr = [rres_fn(s, t) for t in range(cfg.NT)]
    _dma(P, 'sp', dst[:N_META, 0, :], src[s, 0:N_META, c0:c1], rr, [dst_res])
    _dma(P, 'sp', dst[:, 1:cfg.NT, :], src[s, N_META:, c0:c1].rearrange("(t p) d -> p t d", p=128), rr, [dst_res])


def transpose_tok(C, tok, tok_res, nd, dstT, dst_res):
    P, cfg = C.P, C.cfg
    groups = [[0]] + [list(range(t, min(t + 4, cfg.NT))) for t in range(1, cfg.NT, 4)]
    for g in groups:
        b = P.nextbank()
        pb = C.psb[b]
        for jj, t in enumerate(g):
            r0, rows = cfg.tile(t)
            _tr(P, pb[:nd, jj * 128:jj * 128 + rows], tok[:rows, t, :], C.idb[:rows, :rows], [tok_res, 'idb'], ['ps%d' % b])
        r0, rows = cfg.tile(g[0])
        ncol = sum(cfg.tile(t)[1] for t in g)
        eng = 'act' if (C.evac_ctr % 2 == 0) else 'dve'
        C.evac_ctr += 1
        _cp(P, eng, dstT[:nd, r0:r0 + ncol], pb[:nd, 0:ncol], ['ps%d' % b], [dst_res])


def mla_part2(C, j):
    nc, P, cfg, D = C.nc, C.P, C.cfg, C.D
    L, NT = cfg.L, cfg.NT
    bounds = [0] + [cfg.tile(t)[0] for t in range(4, NT, 4)] + [L]
    with ExitStack() as es:
        def A(name, shape, dt):
            return es.enter_context(nc.sbuf_tensor(_uname(name), shape, dt))
        krtok = A("a_krtok", [128, NT, 64], BF16)
        KRT = A("a_KRT", [64, L], BF16)
        S2 = []
        for k in range(2):
            S2.append(dict(
                ktok=A("a_ktok%d" % k, [128, NT, 128], BF16), vtok=A("a_vtok%d" % k, [128, NT, 128], BF16),
                qntok=A("a_qntok%d" % k, [128, NT, 128], BF16), qrtok=A("a_qrtok%d" % k, [128, NT, 64], BF16),
                KT=A("a_KT%d" % k, [128, L], BF16), QNT=A("a_QNT%d" % k, [128, L], BF16),
                QRT=A("a_QRT%d" % k, [64, L], BF16), OT=A("a_OT%d" % k, [128, L], BF16)))
        Pb = [A("a_Pb%d" % k, [128, L], BF16) for k in range(2)]
        PT = [A("a_PT%d" % k, [128, NT, 128], BF16) for k in range(2)]
        mxs = [A("a_mxs%d" % k, [128, 8], F32) for k in range(2)]
        mx = [A("a_mx%d" % k, [128, 1], F32) for k in range(2)]
        nb = [A("a_nb%d" % k, [128, 1], F32) for k in range(2)]
        sums = [A("a_sums%d" % k, [128, 8], F32) for k in range(2)]
        rsum = [A("a_rsum%d" % k, [128, 1], F32) for k in range(2)]
        rinv = [A("a_rinv%d" % k, [128, 1], F32) for k in range(2)]
        KRT2 = [KRT, A("a_KRT1", [64, L], BF16)]
        krtok2 = [krtok, A("a_krtok1", [128, NT, 64], BF16)]
        P.bankset = [5, 6]
        units = [(s, h) for s in range(cfg.NSEQ) for h in range(MLA_H)]
        itc = [0]

        def prologue(u):
            s, h = units[u]
            B = S2[u % 2]
            sfx = str(u % 2)
            if h == 0:
                ss_ = str(s % 2)
                load_tok(C, krtok2[s % 2], 'krtok' + ss_, D['KR'], s, 0, 64, lambda s_, t_: 'KR_%d_%d' % (s_, t_))
                transpose_tok(C, krtok2[s % 2], 'krtok' + ss_, 64, KRT2[s % 2], 'KRT' + ss_)
            load_tok(C, B['ktok'], 'ktok' + sfx, D['KN'], s, h * 128, (h + 1) * 128, lambda s_, t_: 'KN_%d_%d' % (s_, t_))
            load_tok(C, B['vtok'], 'vtok' + sfx, D['V'], s, h * 128, (h + 1) * 128, lambda s_, t_: 'V_%d_%d' % (s_, t_))
            load_tok(C, B['qntok'], 'qntok' + sfx, D['QN'], s, h * 128, (h + 1) * 128, lambda s_, t_: 'QN_%d_%d' % (s_, t_))
            load_tok(C, B['qrtok'], 'qrtok' + sfx, D['QR'], s, h * 64, (h + 1) * 64, lambda s_, t_: 'QR_%d_%d' % (s_, t_))
            transpose_tok(C, B['ktok'], 'ktok' + sfx, 128, B['KT'], 'KT' + sfx)
            transpose_tok(C, B['qntok'], 'qntok' + sfx, 128, B['QNT'], 'QNT' + sfx)
            transpose_tok(C, B['qrtok'], 'qrtok' + sfx, 64, B['QRT'], 'QRT' + sfx)

        def stageA(u, i, k):
            s, h = units[u]
            B = S2[u % 2]
            sfx = str(u % 2)
            KRTs, krr = KRT2[s % 2], 'KRT' + str(s % 2)
            r0, rows = cfg.tile(i)
            nkeys = r0 + rows
            ks = str(k)
            grp = [(bounds[g], min(bounds[g + 1], nkeys)) for g in range(len(bounds) - 1) if bounds[g] < nkeys]
            for gi, (g0, g1) in enumerate(grp):
                w = g1 - g0
                pr = 'ps%d' % gi
                psg = C.ps[gi][:rows, 0:w]
                _mm(P, psg, B['QNT'][:, r0:r0 + rows], B['KT'][:, g0:g1], True, False, ['QNT' + sfx, 'KT' + sfx], [pr])
                _mm(P, psg, B['QRT'][:64, r0:r0 + rows], KRTs[:64, g0:g1], False, True, ['QRT' + sfx, krr], [pr])
                if g1 == nkeys:
                    lo = r0 - g0
                    dg = C.ps[gi][:rows, lo:lo + rows]
                    _tt(P, 'dve', dg, dg, C.maskA[:rows, :rows], ALU.add, [pr, 'maskA'], [pr])
                _red(P, mxs[k][:rows, gi:gi + 1], psg, ALU.max, [pr], ['mxs' + ks])
            ng = len(grp)
            _red(P, mx[k][:rows, :], mxs[k][:rows, 0:ng], ALU.max, ['mxs' + ks], ['mx' + ks])
            _ts(P, 'dve', nb[k][:rows, :], mx[k][:rows, :], -ATT_SCALE, None, ALU.mult, None, ['mx' + ks], ['nb' + ks])
            for gi, (g0, g1) in enumerate(grp):
                w = g1 - g0
                pr = 'ps%d' % gi
                _act(P, Pb[k][:rows, g0:g1], C.ps[gi][:rows, 0:w], AF.Exp, [pr, 'nb' + ks], ['Pb' + ks, 'sums' + ks],
                     bias=nb[k][:rows, :], scale=ATT_SCALE, accum_out=sums[k][:rows, gi:gi + 1])
            _red(P, rsum[k][:rows, :], sums[k][:rows, 0:ng], ALU.add, ['sums' + ks], ['rsum' + ks])
            _recip(P, rinv[k][:rows, :], rsum[k][:rows, :], ['rsum' + ks], ['rinv' + ks])
            _ts(P, 'dve', Pb[k][:rows, 0:nkeys], Pb[k][:rows, 0:nkeys], rinv[k][:rows, :], None, ALU.mult, None,
                ['Pb' + ks, 'rinv' + ks], ['Pb' + ks])

        def stageB(u, i, k):
            s, h = units[u]
            B = S2[u % 2]
            sfx = str(u % 2)
            r0, rows = cfg.tile(i)
            ks = str(k)
            kt = list(range(i + 1))
            for q0 in range(0, len(kt), 4):
                g = kt[q0:q0 + 4]
                b = P.nextbank()
                for jj, jt in enumerate(g):
                    kr0, kn = cfg.tile(jt)
                    _tr(P, C.psb[b][:kn, jj * 128:jj * 128 + rows], Pb[k][:rows, kr0:kr0 + kn], C.idb[:rows, :rows],
                        ['Pb' + ks, 'idb'], ['ps%d' % b])
                srcv = C.psb[b][:, 0:len(g) * 128].rearrange("p (j c) -> p j c", c=128)[:, :, :rows]
                eng = 'act' if (C.evac_ctr % 2 == 0) else 'dve'
                C.evac_ctr += 1
                _cp(P, eng, PT[k][:, g[0]:g[0] + len(g), :rows], srcv, ['ps%d' % b], ['PT' + ks])
            bo = 7
            for jt in kt:
                kr0, kn = cfg.tile(jt)
                _mm(P, C.ps[bo][:, 0:rows], B['vtok'][:kn, jt, :], PT[k][:kn, jt, :rows], jt == 0, jt == i,
                    ['vtok' + sfx, 'PT' + ks], ['ps%d' % bo])
            _cp(P, 'act', B['OT'][:, r0:r0 + rows], C.ps[bo][:, 0:rows], ['ps%d' % bo], ['OT' + sfx])

        prologue(0)
        for u, (s, h) in enumerate(units):
            if u + 1 < len(units):
                prologue(u + 1)
            kk = [(itc[0] + i) % 2 for i in range(NT)]
            itc[0] += NT
            stageA(u, 0, kk[0])
            for i in range(1, NT):
                stageA(u, i, kk[i])
                stageB(u, i - 1, kk[i - 1])
            stageB(u, NT - 1, kk[NT - 1])
            _dma(P, 'sp', D['OT'][s, :, h, :], S2[u % 2]['OT'][:, :], ['OT' + str(u % 2)], ['OTd_%d' % s])
        P.bankset = None
        P.flush()


def mla_part3(C, H, j):
    nc, P, cfg, D = C.nc, C.P, C.cfg, C.D
    BC = cfg.BT * 128
    with ExitStack() as es:
        def A(name, shape, dt):
            return es.enter_context(nc.sbuf_tensor(_uname(name), shape, dt))
        oT = [A("p_oT%d" % k, [128, 16, BC], BF16) for k in range(2)]
        wb = [A("p_w%d" % k, [128, 8, 512], BF16) for k in range(3)]
        C.hres = [A("p_hr%d" % k, [128, 512], F32) for k in range(3)]
        epi = residual_epi(C, H, 1.0)
        for bi, blk in enumerate(cfg.blocks()):
            k = bi % 2
            for (s, t, r0, rows, col0) in blk:
                _dma(P, 'sp', oT[k][:, :, col0:col0 + rows], D['OT'][s, :, :, r0:r0 + rows], ['OTd_%d' % s], ['oT%d' % k])
            gemm_tm(C, oT[k], 'oT%d' % k, 16, blk, D['mla_wo'][j], cfg.DM, epi, wb, 'pw')
        P.flush()


def mla_phase(C, H, j, i):
    mla_part1(C, H, j, i)
    mla_part2(C, j)
    mla_part3(C, H, j)


def host_mla_layout(cfg, w_in, w_uq, w_ukv, w_o):
    nA = w_in.shape[0]
    KC = cfg.KC
    win_r = np.ascontiguousarray(w_in.reshape(nA, KC, 128, 1088).transpose(0, 2, 1, 3))
    q = w_uq.reshape(nA, 512, 16, 192)
    wuq = np.concatenate([q[..., :128].reshape(nA, 512, 2048), q[..., 128:].reshape(nA, 512, 1024)], axis=-1)
    wuq_r = np.ascontiguousarray(wuq.reshape(nA, 4, 128, 3072).transpose(0, 2, 1, 3))
    kv = w_ukv.reshape(nA, 512, 16, 256)
    wukv = np.concatenate([kv[..., :128].reshape(nA, 512, 2048), kv[..., 128:].reshape(nA, 512, 2048)], axis=-1)
    wukv_r = np.ascontiguousarray(wukv.reshape(nA, 4, 128, 4096).transpose(0, 2, 1, 3))
    wo_r = np.ascontiguousarray(w_o.reshape(nA, 16, 128, cfg.DM).transpose(0, 2, 1, 3))
    return win_r, wuq_r, wukv_r, wo_r


S_IN = 4096
S_CONV = 6144
S_PROJ = 10304
S_H = 64
import os as _os
DBG3 = int(_os.environ.get('DBG3', '9'))
DBGX = int(_os.environ.get('DBGX', '99'))


def ssm_decl(nc, cfg, D, din, nB):
    L, NSEQ = cfg.L, cfg.NSEQ
    D['ssm_win'] = din('ssm_win_r', [nB, 128, cfg.KC, S_PROJ])
    D['ssm_wout'] = din('ssm_wout_r', [nB, 128, 32, cfg.DM])
    D['ssm_cw'] = din('ssm_conv_w', [nB, 4, S_CONV])
    D['ssm_cb'] = din('ssm_conv_b', [nB, 1, S_CONV])
    D['ssm_dtb'] = din('ssm_dt_bias', [nB, 1, S_H])
    D['ssm_alog'] = din('ssm_a_log', [nB, 1, S_H])
    D['ssm_d'] = din('ssm_d', [nB, 1, S_H])
    D['ssm_nw'] = din('ssm_norm', [nB, 1, S_IN])
    D['c_maskS'] = din('c_maskS', [128, 128])
    D['ZX'] = nc.dram_tensor('s_ZX', [NSEQ, L, S_PROJ], F32, kind="Internal").ap()
    D['XC'] = nc.dram_tensor('s_XC', [NSEQ, L, S_CONV], F32, kind="Internal").ap()
    D['YT'] = nc.dram_tensor('s_YT', [NSEQ, 128, 32, L], BF16, kind="Internal").ap()


def ssm_part1(C, H, j, i):
    nc, P, cfg, D = C.nc, C.P, C.cfg, C.D
    DM, KC = cfg.DM, cfg.KC
    BC = cfg.BT * 128
    with ExitStack() as es:
        def A(name, shape, dt):
            return es.enter_context(nc.sbuf_tensor(_uname(name), shape, dt))
        gbc = A("s1_gbc", [128, DM], F32)
        uT = A("s1_uT", [128, KC, BC], BF16)
        C.hbuf = [A("s1_hb0", [128, DM], F32), A("s1_hb1", [128, DM], F32)]
        C.hn = [A("s1_hn0", [128, DM], BF16), A("s1_hn1", [128, DM], BF16)]
        C.sq = A("s1_sq", [128, DM], BF16)
        wb = [A("s1_w%d" % k, [128, 8, 512], BF16) for k in range(3)]
        fr = [A("s1_fr%d" % k, [128, 512], F32) for k in range(4)]
        ctr = [0]
        _dma(P, 'sp', gbc[:, :], D['norm_mix'][i].partition_broadcast(128), [], ['gbc'])

        def epi(n, cc, te, psap, psres):
            s, t, r0, rows, col0 = te
            k = ctr[0] % 4
            ctr[0] += 1
            w = cc[1] - cc[0]
            _cp(P, 'act' if k % 2 == 0 else 'dve', fr[k][:rows, 0:w], psap, [psres], ['fr%d' % k])
            _dma(P, 'sp', D['ZX'][s, r0:r0 + rows, cc[0]:cc[1]], fr[k][:rows, 0:w], ['fr%d' % k], ['ZX_%d_%d' % (s, t)])
        for blk in cfg.blocks():
            for ii, (s, t, r0, rows, col0) in enumerate(blk):
                norm_T(C, H[s, r0:r0 + rows, :], hres(s, t), rows, gbc, uT, 'uT', col0, ii)
            gemm_tm(C, uT, 'uT', KC, blk, D['ssm_win'][j], S_PROJ, epi, wb, 'sw')
        P.flush()


def ssm_part2(C, j):
    nc, P, cfg, D = C.nc, C.P, C.cfg, C.D
    CW = 1024
    with ExitStack() as es:
        def A(name, shape, dt):
            return es.enter_context(nc.sbuf_tensor(_uname(name), shape, dt))
        wbc = A("s2_wbc", [128, 4, CW], F32)
        bbc = A("s2_bbc", [128, CW], F32)
        xk = [[A("s2_x%d_%d" % (b, k), [128, CW], F32) for k in range(4)] for b in range(2)]
        pk = [[A("s2_p%d_%d" % (b, k), [128, CW], F32) for k in range(4)] for b in range(2)]
        oc = [A("s2_o%d" % b, [128, CW], F32) for b in range(2)]
        it = 0
        for c0 in range(0, S_CONV, CW):
            c1 = c0 + CW
            for k in range(4):
                _dma(P, 'sp', wbc[:, k, :], D['ssm_cw'][j, k:k + 1, c0:c1].partition_broadcast(128), [], ['wbc'])
            _dma(P, 'sp', bbc[:, :], D['ssm_cb'][j, :, c0:c1].partition_broadcast(128), [], ['bbc'])
            for s in range(cfg.NSEQ):
                for t in range(cfg.NT):
                    r0, rows = cfg.tile(t)
                    b = it % 2
                    it += 1
                    zr = ['ZX_%d_%d' % (s, t)] + (['ZX_%d_%d' % (s, t - 1)] if t >= 2 else []) + (['ZX_%d_0' % s] if t == 1 else [])
                    for k in range(4):
                        sh = 3 - k
                        xr = 'x%d_%d' % (b, k)
                        if r0 - sh < 0:
                            _memset(P, 'pool', xk[b][k][:rows, :], 0.0, [xr])
                            _dma(P, 'sp', xk[b][k][sh:rows, :], D['ZX'][s, 0:rows - sh, S_IN + c0:S_IN + c1], zr, [xr])
                        else:
                            _dma(P, 'sp', xk[b][k][:rows, :], D['ZX'][s, r0 - sh:r0 - sh + rows, S_IN + c0:S_IN + c1], zr, [xr])
                        _tt(P, 'pool' if k == 1 else 'dve', pk[b][k][:rows, :], xk[b][k][:rows, :], wbc[:rows, k, :], ALU.mult, [xr, 'wbc'], ['p%d_%d' % (b, k)])
                    _tt(P, 'dve', pk[b][0][:rows, :], pk[b][0][:rows, :], pk[b][1][:rows, :], ALU.add, ['p%d_0' % b, 'p%d_1' % b], ['p%d_0' % b])
                    _tt(P, 'dve', pk[b][2][:rows, :], pk[b][2][:rows, :], pk[b][3][:rows, :], ALU.add, ['p%d_2' % b, 'p%d_3' % b], ['p%d_2' % b])
                    _tt(P, 'dve', pk[b][0][:rows, :], pk[b][0][:rows, :], pk[b][2][:rows, :], ALU.add, ['p%d_0' % b, 'p%d_2' % b], ['p%d_0' % b])
                    _tt(P, 'dve', pk[b][0][:rows, :], pk[b][0][:rows, :], bbc[:rows, :], ALU.add, ['p%d_0' % b, 'bbc'], ['p%d_0' % b])
                    _act(P, oc[b][:rows, :], pk[b][0][:rows, :], AF.Silu, ['p%d_0' % b], ['oc%d' % b])
                    _dma(P, 'act', D['XC'][s, r0:r0 + rows, c0:c1], oc[b][:rows, :], ['oc%d' % b], ['XC_%d_%d' % (s, t)])
        P.flush()


def ssm_part3(C, j):
    nc, P, cfg, D = C.nc, C.P, C.cfg, C.D
    with ExitStack() as es:
        def A(name, shape, dt):
            return es.enter_context(nc.sbuf_tensor(_uname(name), shape, dt))
        ST = A("s3_ST", [128, S_IN], F32)
        Sbf = A("s3_Sbf", [128, S_IN], BF16)
        nwbc = A("s3_nw", [128, S_IN], F32)
        xs_2 = [A("s3_xs0", [128, S_IN], F32), A("s3_xs1", [128, S_IN], F32)]
        zt_2 = [A("s3_zt0", [128, S_IN], F32), A("s3_zt1", [128, S_IN], F32)]
        bcf_2 = [A("s3_bcf0", [128, 2048], F32), A("s3_bcf1", [128, 2048], F32)]
        bcb = A("s3_bcb", [128, 2048], BF16)
        BCT = A("s3_BCT", [128, 16, 128], BF16)
        xdt = A("s3_xdt", [128, S_IN], BF16)
        xw = A("s3_xw", [128, S_IN], BF16)
        ysb = A("s3_ysb", [128, S_IN], F32)
        ynb = A("s3_ynb", [128, S_IN], BF16)
        YTt = A("s3_YTt", [128, 32, 128], BF16)
        E = [A("s3_E%d" % k, [128, 8 * 128], BF16) for k in range(2)]
        MT = [A("s3_MT%d" % k, [128, 8 * 128], BF16) for k in range(2)]
        rhsg = [A("s3_rg%d" % k, [128, 8 * 128], F32) for k in range(2)]
        CBT = [A("s3_CBT%d" % k, [128, 128], BF16) for k in range(2)]
        t1 = [A("s3_t1%d" % k, [128, 512], F32) for k in range(2)]
        mrep = A("s3_mrep", [128, 8 * 128], BF16)
        mrep16 = A("s3_mrep16", [128, 8 * 16], BF16)
        maskS = A("s3_maskS", [128, 128], F32)
        sm = {}
        for nm in ('dtb', 'abc', 'dbc', 'dtr', 'dt', 'dta', 'acs', 'nacs', 'expA', 'decayc', 'wq', 'tmpw', 'ssg', 'rsg', 'rstdg'):
            sm[nm] = A("s3_" + nm, [128, 64], F32)
        acsT = A("s3_acsT", [128, 128], F32)
        dtaP = A("s3_dtaP", [128, 128], F32)
        _dma(P, 'sp', nwbc[:, :], D['ssm_nw'][j].partition_broadcast(128), [], ['nwbc'])
        _dma(P, 'sp', sm['dtb'][:, :], D['ssm_dtb'][j].partition_broadcast(128), [], ['dtb'])
        _dma(P, 'sp', sm['abc'][:, :], D['ssm_alog'][j].partition_broadcast(128), [], ['abc'])
        _dma(P, 'sp', sm['dbc'][:, :], D['ssm_d'][j].partition_broadcast(128), [], ['dbc'])
        _dma(P, 'sp', maskS[:, :], D['c_maskS'], [], ['maskS'])
        _act(P, sm['abc'][:, :], sm['abc'][:, :], AF.Exp, ['abc'], ['abc'])
        _ts(P, 'dve', sm['abc'][:, :], sm['abc'][:, :], -1.0, None, ALU.mult, None, ['abc'], ['abc'])
        _cp(P, 'dve', mrep[:, :].rearrange("p (r l) -> p r l", l=128), maskS[:, :].unsqueeze(1).to_broadcast([128, 8, 128]), ['maskS'], ['mrep'])
        _memset(P, 'dve', mrep16[:, :], 0.0, ['mrep16'])
        _cp(P, 'dve', mrep16[:16, :].rearrange("p (r l) -> p r l", l=16), maskS[:16, :16].unsqueeze(1).to_broadcast([16, 8, 16]), ['maskS'], ['mrep16'])
        it = 0
        P.bankset = [0, 1]
        for s in range(cfg.NSEQ):
            _memset(P, 'pool', ST[:, :], 0.0, ['ST%d' % g for g in range(8)])
            _memset(P, 'pool', Sbf[:, :], 0.0, ['Sbf%d' % g for g in range(8)])
            _memset(P, 'pool', dtaP[:, :], 0.0, ['dta'])
            for t in range(cfg.NT):
                r0, rows = cfg.tile(t)
                R = rows
                mr = mrep if R == 128 else mrep16
                mrr = 'mrep' if R == 128 else 'mrep16'
                tp = (s * cfg.NT + t) % 2
                xs, zt, bcf = xs_2[tp], zt_2[tp], bcf_2[tp]
                XS, ZT, BCF = 'xs%d' % tp, 'zt%d' % tp, 'bcf%d' % tp
                xcr, zxr = 'XC_%d_%d' % (s, t), 'ZX_%d_%d' % (s, t)
                _dma(P, 'sp', xs[:R, :], D['XC'][s, r0:r0 + R, 0:S_IN], [xcr], [XS])
                _dma(P, 'sp', bcf[:R, :], D['XC'][s, r0:r0 + R, S_IN:S_CONV], [xcr], [BCF])
                _dma(P, 'sp', zt[:R, :], D['ZX'][s, r0:r0 + R, 0:S_IN], [zxr], [ZT])
                _dma(P, 'sp', sm['dtr'][:R, :], D['ZX'][s, r0:r0 + R, S_IN + S_CONV:S_PROJ], [zxr], ['dtr'])
                _tt(P, 'dve', sm['dt'][:R, :], sm['dtr'][:R, :], sm['dtb'][:R, :], ALU.add, ['dtr', 'dtb'], ['dt'])
                _act(P, sm['dt'][:R, :], sm['dt'][:R, :], AF.Exp, ['dt'], ['dt'])
                _act(P, sm['dt'][:R, :], sm['dt'][:R, :], AF.Ln, ['dt'], ['dt'], bias=1.0)
                _tt(P, 'dve', dtaP[:R, 0:64], sm['dt'][:R, :], sm['abc'][:R, :], ALU.mult, ['dt', 'abc'], ['dta'])
                bq = P.nextbank()
                pq, pqr = C.ps[bq], 'ps%d' % bq
                _stmts = []
                _stmts.append(lambda: _mm(P, pq[:, 0:64], C.tri[:, :], dtaP[:, 0:64], True, True, ['tri', 'dta'], [pqr]))
                _stmts.append(lambda: _mm(P, pq[:, 64:64 + R], dtaP[:, :], C.tri[:, :R], True, True, ['tri', 'dta'], [pqr]))
                _stmts.append(lambda: _mm(P, pq[:, 256:320], C.onesf[:, :], dtaP[:, 0:64], True, True, ['onesf', 'dta'], [pqr]))
                _stmts.append(lambda: _mm(P, pq[:, 480:496], C.idb[:, :], C.idb[:, 0:16], True, True, ['idb'], [pqr]))
                _stmts.append(lambda: _cp(P, 'act', sm['acs'][:R, :], pq[:R, 0:64], [pqr], ['acs']))
                _stmts.append(lambda: (_ts(P, 'dve', sm['nacs'][:R, :], pq[:R, 0:64], -1.0, None, ALU.mult, None, [pqr], ['nacs']) if _os.environ.get('NACS_PSUM') else _ts(P, 'dve', sm['nacs'][:R, :], sm['acs'][:R, :], -1.0, None, ALU.mult, None, ['acs'], ['nacs'])))
                _stmts.append(lambda: _cp(P, 'act', acsT[:, :R], pq[:, 64:64 + R], [pqr], ['acsT']))
                _stmts.append(lambda: _act(P, sm['expA'][:R, :], pq[:R, 0:64], AF.Exp, [pqr], ['expA']))
                _stmts.append(lambda: _act(P, sm['decayc'][:, :], pq[:, 256:320], AF.Exp, [pqr], ['decayc']))
                _stmts.append(lambda: _cp(P, 'act', sm['ssg'][:R, :], pq[:R, 256:320], [pqr], ['ssg']))
                _stmts.append(lambda: _tt(P, 'dve', sm['tmpw'][:R, :], sm['ssg'][:R, :], sm['acs'][:R, :], ALU.subtract, ['ssg', 'acs'], ['tmpw']))
                _stmts.append(lambda: _act(P, sm['tmpw'][:R, :], sm['tmpw'][:R, :], AF.Exp, ['tmpw'], ['tmpw']))
                _stmts.append(lambda: _tt(P, 'dve', sm['wq'][:R, :], sm['tmpw'][:R, :], sm['dt'][:R, :], ALU.mult, ['tmpw', 'dt'], ['wq']))
                for _f in _stmts[:DBGX]:
                    _f()
                xs3 = xs[:R, :].rearrange("p (h d) -> p h d", d=64)
                _tt(P, 'dve', xdt[:R, :].rearrange("p (h d) -> p h d", d=64), xs3,
                    sm['dt'][:R, :].unsqueeze(2).to_broadcast([R, 64, 64]), ALU.mult, [XS, 'dt'], ['xdt'])
                _tt(P, 'dve', xw[:R, :].rearrange("p (h d) -> p h d", d=64), xs3,
                    sm['wq'][:R, :].unsqueeze(2).to_broadcast([R, 64, 64]), ALU.mult, [XS, 'wq'], ['xw'])
                _cp(P, 'act', bcb[:R, :], bcf[:R, :], [BCF], ['bcb'])
                transpose_into(C, bcb, 'bcb', R, 16, BCT, 'BCT', 0)
                def stageA(g, k):
                    ks = str(k)
                    _mm(P, C.ps[4][:R, 0:R], BCT[:, g, :R], BCT[:, 8 + g, :R], True, True, ['BCT'], ['ps4'])
                    _cp(P, 'act', CBT[k][:R, :R], C.ps[4][:R, 0:R], ['ps4'], ['CBT' + ks])
                    rg3 = rhsg[k][:, 0:8 * R].rearrange("p (r l) -> p r l", l=R)
                    _tt(P, 'pool', rg3, acsT[:, :R].unsqueeze(1).to_broadcast([128, 8, R]),
                        C.idf[:, 8 * g:8 * g + 8].unsqueeze(2).to_broadcast([128, 8, R]), ALU.mult, ['acsT', 'idf'], ['rhsg' + ks])
                    for half in range(2):
                        bb = 2 + half
                        pb_, pbr = C.ps[bb], 'ps%d' % bb
                        _mm(P, pb_[:, 0:4 * R], C.onesf[:, :], rhsg[k][:, half * 4 * R:(half + 1) * 4 * R], True, False,
                            ['onesf', 'rhsg' + ks], [pbr])
                        _mm(P, pb_[:, 0:4 * R], C.idb[:, :], mr[:, 0:4 * R], False, True, ['idb', mrr], [pbr])
                        for rr in range(4):
                            r_ = half * 4 + rr
                            hh = 8 * g + r_
                            _act(P, E[k][:R, r_ * R:(r_ + 1) * R], pb_[:R, rr * R:(rr + 1) * R], AF.Exp, [pbr, 'nacs'], ['E' + ks],
                                 bias=sm['nacs'][:R, hh:hh + 1], scale=1.0)
                    _tt(P, 'dve', MT[k][:R, 0:8 * R].rearrange("p (r l) -> p r l", l=R),
                        E[k][:R, 0:8 * R].rearrange("p (r l) -> p r l", l=R),
                        CBT[k][:R, :R].unsqueeze(1).to_broadcast([R, 8, R]), ALU.mult, ['E' + ks, 'CBT' + ks], ['MT' + ks])

                def stageB(g, k):
                    ks = str(k)
                    by, bo, bs = 5, 6, 7
                    for r_ in range(8):
                        hh = 8 * g + r_
                        _mm(P, C.ps[by][:R, r_ * 64:(r_ + 1) * 64], MT[k][:R, r_ * R:(r_ + 1) * R], xdt[:R, hh * 64:(hh + 1) * 64],
                            True, True, ['MT' + ks, 'xdt'], ['ps%d' % by])
                    _mm(P, C.ps[bo][:R, 0:512], BCT[:, 8 + g, :R], Sbf[:, g * 512:(g + 1) * 512], True, True,
                        ['BCT', 'Sbf%d' % g], ['ps%d' % bo])
                    _tt(P, 'dve', t1[k][:R, :].rearrange("p (r d) -> p r d", d=64),
                        C.ps[bo][:R, 0:512].rearrange("p (r d) -> p r d", d=64),
                        sm['expA'][:R, 8 * g:8 * g + 8].unsqueeze(2).to_broadcast([R, 8, 64]), ALU.mult,
                        ['ps%d' % bo, 'expA'], ['t1' + ks])
                    _tt(P, 'dve', ysb[:R, g * 512:(g + 1) * 512], C.ps[by][:R, 0:512], t1[k][:R, :], ALU.add,
                        ['ps%d' % by, 't1' + ks], ['ysb%d' % g])
                    _mm(P, C.ps[bs][:, 0:512], bcb[:R, g * 128:(g + 1) * 128], xw[:R, g * 512:(g + 1) * 512], True, True,
                        ['bcb', 'xw'], ['ps%d' % bs])
                    stg = ST[:, g * 512:(g + 1) * 512]
                    _tt(P, 'pool', stg.rearrange("p (r d) -> p r d", d=64), stg.rearrange("p (r d) -> p r d", d=64),
                        sm['decayc'][:, 8 * g:8 * g + 8].unsqueeze(2).to_broadcast([128, 8, 64]), ALU.mult,
                        ['ST%d' % g, 'decayc'], ['ST%d' % g])
                    _tt(P, 'dve', stg, stg, C.ps[bs][:, 0:512], ALU.add, ['ST%d' % g, 'ps%d' % bs], ['ST%d' % g])
                    _cp(P, 'pool', Sbf[:, g * 512:(g + 1) * 512], stg, ['ST%d' % g], ['Sbf%d' % g])

                kk = [(it + g) % 2 for g in range(8)]
                it += 8
                stageA(0, kk[0])
                for g in range(1, 8):
                    stageA(g, kk[g])
                    stageB(g - 1, kk[g - 1])
                stageB(7, kk[7])
                YS = ['ysb%d' % g for g in range(8)]
                _tt(P, 'dve', xs3, xs3, sm['dbc'][:R, :].unsqueeze(2).to_broadcast([R, 64, 64]), ALU.mult, [XS, 'dbc'], [XS])
                _tt(P, 'dve', ysb[:R, :], ysb[:R, :], xs[:R, :], ALU.add, YS + [XS], YS)
                _act(P, zt[:R, :], zt[:R, :], AF.Silu, [ZT], [ZT])
                _tt(P, 'dve', ysb[:R, :], ysb[:R, :], zt[:R, :], ALU.mult, YS + [ZT], YS)
                for g in range(8):
                    _act(P, ynb[:R, g * 512:(g + 1) * 512], ysb[:R, g * 512:(g + 1) * 512], AF.Square, YS, ['ynb', 'ssg'],
                         accum_out=sm['ssg'][:R, g:g + 1])
                _act(P, sm['rsg'][:R, 0:8], sm['ssg'][:R, 0:8], AF.Sqrt, ['ssg'], ['rsg'], scale=1.0 / 512, bias=C.epsb[:R, :])
                _recip(P, sm['rstdg'][:R, 0:8], sm['rsg'][:R, 0:8], ['rsg'], ['rstdg'])
                ys3 = ysb[:R, :].rearrange("p (g c) -> p g c", c=512)
                _tt(P, 'dve', ys3, ys3, sm['rstdg'][:R, 0:8].unsqueeze(2).to_broadcast([R, 8, 512]), ALU.mult, YS + ['rstdg'], YS)
                _tt(P, 'dve', ynb[:R, :], ysb[:R, :], nwbc[:R, :], ALU.mult, YS + ['nwbc'], ['ynb'])
                transpose_into(C, ynb, 'ynb', R, 32, YTt, 'YTt', 0)
                _dma(P, 'sp', D['YT'][s, :, :, r0:r0 + R], YTt[:, :, :R], ['YTt'], ['YTd_%d' % s])
        P.bankset = None
        P.flush()


def ssm_part4(C, H, j):
    nc, P, cfg, D = C.nc, C.P, C.cfg, C.D
    BC = cfg.BT * 128
    with ExitStack() as es:
        def A(name, shape, dt):
            return es.enter_context(nc.sbuf_tensor(_uname(name), shape, dt))
        yT = [A("s4_yT%d" % k, [128, 32, BC], BF16) for k in range(2)]
        wb = [A("s4_w%d" % k, [128, 8, 512], BF16) for k in range(3)]
        C.hres = [A("s4_hr%d" % k, [128, 512], F32) for k in range(3)]
        epi = residual_epi(C, H, 1.0)
        for bi, blk in enumerate(cfg.blocks()):
            k = bi % 2
            for (s, t, r0, rows, col0) in blk:
                _dma(P, 'sp', yT[k][:, :, col0:col0 + rows], D['YT'][s, :, :, r0:r0 + rows], ['YTd_%d' % s], ['yT%d' % k])
            gemm_tm(C, yT[k], 'yT%d' % k, 32, blk, D['ssm_wout'][j], cfg.DM, epi, wb, 'ow')
        P.flush()


def ssm_phase(C, H, j, i):
    ssm_part1(C, H, j, i)
    ssm_part2(C, j)
    ssm_part3(C, j)
    ssm_part4(C, H, j)


def host_ssm_layout(cfg, w_in, w_out):
    nB = w_in.shape[0]
    win_r = np.ascontiguousarray(w_in.reshape(nB, cfg.KC, 128, S_PROJ).transpose(0, 2, 1, 3))
    wout_r = np.ascontiguousarray(w_out.reshape(nB, 32, 128, cfg.DM).transpose(0, 2, 1, 3))
    return win_r, wout_r


_NC_CACHE = {}


def kernel(x, meta_tokens, norm_ffn1, ffn1_w_in, ffn1_w_out, norm_mix, norm_ffn2, ffn2_w_in, ffn2_w_out,
           mla_w_in, mla_q_norm, mla_w_uq, mla_kv_norm, mla_w_ukv, mla_w_o,
           ssm_w_in, ssm_conv_w, ssm_conv_b, ssm_dt_bias, ssm_a_log, ssm_d, ssm_norm, ssm_w_out,
           final_norm):
    from concourse.bass_utils import run_bass_kernel_spmd
    f = lambda a: np.ascontiguousarray(np.asarray(a, dtype=np.float32))
    x = f(x)
    B, SEQ, DM = x.shape
    NCORES = 8
    NSEQ = B // NCORES
    DEPTH = norm_ffn1.shape[0]
    cfg = Cfg(SEQ=SEQ, NSEQ=NSEQ, DEPTH=DEPTH, DM=DM, DFF=ffn1_w_out.shape[1])
    key = (SEQ, NSEQ, DEPTH, DM, cfg.DFF)
    if key not in _NC_CACHE:
        _NC_CACHE[key] = build(cfg)
    nc = _NC_CACHE[key]
    shared = {}
    shared['meta_tokens'] = f(meta_tokens)
    shared['norm_ffn1'] = f(norm_ffn1).reshape(DEPTH, 1, DM)
    shared['norm_mix'] = f(norm_mix).reshape(DEPTH, 1, DM)
    shared['norm_ffn2'] = f(norm_ffn2).reshape(DEPTH, 1, DM)
    shared['final_norm'] = f(final_norm).reshape(1, DM)
    shared['ffn1_win_r'], shared['ffn1_wout_r'] = host_ffn_layout(cfg, f(ffn1_w_in), f(ffn1_w_out))
    shared['ffn2_win_r'], shared['ffn2_wout_r'] = host_ffn_layout(cfg, f(ffn2_w_in), f(ffn2_w_out))
    (shared['mla_win_r'], shared['mla_wuq_r'], shared['mla_wukv_r'], shared['mla_wo_r']) = host_mla_layout(
        cfg, f(mla_w_in), f(mla_w_uq), f(mla_w_ukv), f(mla_w_o))
    nA = mla_w_in.shape[0]
    shared['mla_q_norm'] = f(mla_q_norm).reshape(nA, 1, 512)
    shared['mla_kv_norm'] = f(mla_kv_norm).reshape(nA, 1, 512)
    shared['ssm_win_r'], shared['ssm_wout_r'] = host_ssm_layout(cfg, f(ssm_w_in), f(ssm_w_out))
    nB = ssm_w_in.shape[0]
    shared['ssm_conv_w'] = f(ssm_conv_w)
    shared['ssm_conv_b'] = f(ssm_conv_b).reshape(nB, 1, S_CONV)
    shared['ssm_dt_bias'] = f(ssm_dt_bias).reshape(nB, 1, S_H)
    shared['ssm_a_log'] = f(ssm_a_log).reshape(nB, 1, S_H)
    shared['ssm_d'] = f(ssm_d).reshape(nB, 1, S_H)
    shared['ssm_norm'] = f(ssm_norm).reshape(nB, 1, S_IN)
    shared.update(host_consts(cfg))
    in_maps = []
    for c in range(NCORES):
        m = dict(shared)
        m['x'] = x[c * NSEQ:(c + 1) * NSEQ]
        in_maps.append(m)
    res = run_bass_kernel_spmd(nc, in_maps, core_ids=list(range(NCORES)))
    out = np.concatenate([res.results[c]['out'] for c in range(NCORES)], axis=0)
    return out.astype(np.float32)
```

```python
import numpy as np
from contextlib import ExitStack
import concourse.bass as bass
import concourse.mybir as mybir

F32 = mybir.dt.float32
BF16 = mybir.dt.bfloat16
AF = mybir.ActivationFunctionType
ALU = mybir.AluOpType
AX = mybir.AxisListType

ENGS = ['pe', 'act', 'dve', 'pool', 'sp']
NS_DMA = 12


class Ins:
    __slots__ = ('fn', 'waits', 'signal', 'dma', 'dsem', 'dval', 'fn_was_real')

    def __init__(self, fn, dma=False):
        self.fn = fn
        self.fn_was_real = fn is not None
        self.waits = []
        self.signal = False
        self.dma = dma
        self.dsem = None
        self.dval = 0


class Prog:
    def __init__(self, nc):
        self.nc = nc
        self.q = {e: [] for e in ENGS}
        self.lastw = {}
        self.readers = {}
        self.seen = {e: {} for e in ENGS}
        self.ndma = {e: 0 for e in ENGS}
        self.dma_tok = {e: {} for e in ENGS}

    def _need(self, eng, ins, tok):
        if tok is None:
            return
        kind = tok[0]
        if kind == 'c':
            _, f, i = tok
            if f == eng and eng == 'pe':
                return
            key = ('c', f)
            if self.seen[eng].get(key, -1) >= i:
                return
            self.seen[eng][key] = i
            self.q[f][i].signal = True
            ins.waits.append(tok)
        else:
            _, qn, j = tok
            s = j % NS_DMA
            v = 16 * (j // NS_DMA + 1)
            key = ('d', qn, s)
            if self.seen[eng].get(key, 0) >= v:
                return
            self.seen[eng][key] = v
            ins.waits.append(tok)

    def _add(self, eng, fn, r, w, dma):
        ins = Ins(fn, dma)
        idx = len(self.q[eng])
        if dma:
            j = self.ndma[eng]
            self.ndma[eng] += 1
            tok = ('d', eng, j)
            if j >= NS_DMA:
                self._need(eng, ins, ('d', eng, j - NS_DMA))
        else:
            tok = ('c', eng, idx)
        for res in r:
            self._need(eng, ins, self.lastw.get(res))
        for res in w:
            self._need(eng, ins, self.lastw.get(res))
            rd = self.readers.get(res)
            if rd is not None:
                for f, i in rd[0].items():
                    if (not dma) and f == eng:
                        continue
                    self._need(eng, ins, ('c', f, i))
                for t in rd[1]:
                    self._need(eng, ins, t)
        for res in r:
            rd = self.readers.setdefault(res, ({}, []))
            if dma:
                rd[1].append(tok)
            else:
                rd[0][eng] = idx
        for res in w:
            self.lastw[res] = tok
            self.readers.pop(res, None)
        self.q[eng].append(ins)
        return tok

    def op(self, eng, fn, r=(), w=()):
        return self._add(eng, fn, r, w, False)

    def dma(self, eng, fn, r=(), w=()):
        return self._add(eng, fn, r, w, True)

    def _setup(self):
        nc = self.nc
        self.csem = {e: nc.alloc_semaphore('c_' + e) for e in ENGS}
        self.dsem = {e: [nc.alloc_semaphore('d_%s_%d' % (e, i)) for i in range(NS_DMA)]
                     for e in ('sp', 'pool', 'act')}
        self.emitted = {e: 0 for e in ENGS}
        self.cnt = {e: [] for e in ENGS}
        self.sig = {e: 0 for e in ENGS}
        self.jd = {e: 0 for e in ENGS}
        self.engobj = {'pe': nc.tensor, 'act': nc.scalar, 'dve': nc.vector, 'pool': nc.gpsimd, 'sp': nc.sync}

    def flush(self):
        nc = self.nc
        if not hasattr(self, 'csem'):
            self._setup()
        ctoks = []
        for f in ENGS:
            n = len(self.q[f])
            if n > self.emitted[f]:
                ctoks.append(('c', f, n - 1))
        dtoks = []
        for qn in ('sp', 'pool', 'act'):
            n = self.ndma[qn]
            for j in range(max(0, n - NS_DMA), n):
                dtoks.append(('d', qn, j))
        bars = {}
        for e in ENGS:
            ins = Ins(None)
            for t in ctoks:
                if t[1] != e:
                    self._need(e, ins, t)
            for t in dtoks:
                self._need(e, ins, t)
            bars[e] = ins
        for e in ENGS:
            self.q[e].append(bars[e])
        for e in ENGS:
            c = self.sig[e]
            for ins in self.q[e][self.emitted[e]:]:
                if ins.signal and not ins.dma and ins.fn is not None:
                    c += 1
                self.cnt[e].append(c)
            self.sig[e] = c

        def do_wait(eobj, tok):
            if tok[0] == 'c':
                _, f, i = tok
                eobj.wait_ge(self.csem[f], self.cnt[f][i])
            else:
                _, qn, j = tok
                eobj.wait_ge(self.dsem[qn][j % NS_DMA], 16 * (j // NS_DMA + 1))

        def run(ename):
            lo = self.emitted[ename]
            hi = len(self.q[ename])

            def body(eobj):
                for ins in self.q[ename][lo:hi]:
                    for tok in ins.waits:
                        do_wait(eobj, tok)
                    if ins.fn is None:
                        continue
                    bi = ins.fn(eobj)
                    if ins.dma:
                        bi.then_inc(self.dsem[ename][self.jd[ename] % NS_DMA], 16)
                        self.jd[ename] += 1
                    elif ins.signal:
                        bi.then_inc(self.csem[ename], 1)
            return body

        with nc.Block() as block:
            block.tensor(run('pe'))
            block.scalar(run('act'))
            block.vector(run('dve'))
            block.gpsimd(run('pool'))
            block.sync(run('sp'))
        for e in ENGS:
            self.emitted[e] = len(self.q[e])
            for ins in self.q[e]:
                ins.fn = None if ins.fn is None else ins.fn
        self.lastw_phase_clear()

    def lastw_phase_clear(self):
        self.lastw = {k: v for k, v in self.lastw.items() if v[0] == 'd'}
        self.readers = {k: ({}, v[1]) for k, v in self.readers.items() if v[1]}

    def nextbank(self):
        bs = getattr(self, 'bankset', None)
        if bs:
            i = getattr(self, '_bsi', 0)
            self._bsi = i + 1
            return bs[i % len(bs)]
        b = getattr(self, '_bank', 0)
        self._bank = (b + 1) % 8
        return b


def _mm(P, out, lhsT, rhs, start, stop, r, w):
    P.op('pe', lambda e: e.matmul(out, lhsT=lhsT, rhs=rhs, start=start, stop=stop), r, w)


def _tr(P, out, in_, ident, r, w):
    P.op('pe', lambda e: e.transpose(out, in_, ident), r, w)


def _act(P, out, in_, func, r, w, bias=None, scale=None, accum_out=None):
    kw = {}
    if bias is not None:
        kw['bias'] = bias
    if scale is not None:
        kw['scale'] = scale
    if accum_out is not None:
        kw['accum_out'] = accum_out
    P.op('act', lambda e: e.activation(out=out, in_=in_, func=func, **kw), r, w)


def _tt(P, eng, out, in0, in1, op, r, w):
    P.op(eng, lambda e: e.tensor_tensor(out=out, in0=in0, in1=in1, op=op), r, w)


def _ts(P, eng, out, in0, s1, s2, op0, op1, r, w):
    if s2 is None:
        P.op(eng, lambda e: e.tensor_scalar(out=out, in0=in0, scalar1=s1, scalar2=None, op0=op0), r, w)
    else:
        P.op(eng, lambda e: e.tensor_scalar(out=out, in0=in0, scalar1=s1, scalar2=s2, op0=op0, op1=op1), r, w)


def _stt(P, out, in0, scalar, in1, op0, op1, r, w):
    P.op('dve', lambda e: e.scalar_tensor_tensor(out=out, in0=in0, scalar=scalar, in1=in1, op0=op0, op1=op1), r, w)


def _cp(P, eng, out, in_, r, w):
    if eng == 'act':
        P.op('act', lambda e: e.activation(out=out, in_=in_, func=AF.Copy), r, w)
    else:
        P.op(eng, lambda e: e.tensor_copy(out=out, in_=in_), r, w)


def _red(P, out, in_, op, r, w):
    P.op('dve', lambda e: e.tensor_reduce(out=out, in_=in_, axis=AX.X, op=op), r, w)


def _recip(P, out, in_, r, w):
    P.op('dve', lambda e: e.reciprocal(out=out, in_=in_), r, w)


def _dma(P, eng, out, in_, r, w):
    P.dma(eng, lambda e: e.dma_start(out=out, in_=in_), r, w)


def _memset(P, eng, ap, val, w):
    P.op(eng, lambda e: e.memset(ap, val), (), w)


EPS = 1e-6
NEG = -30000.0
N_META = 16


class Cfg:
    def __init__(self, SEQ=2048, NSEQ=2, DEPTH=4, DM=2048, DFF=5504):
        self.SEQ, self.NSEQ, self.DEPTH, self.DM, self.DFF = SEQ, NSEQ, DEPTH, DM, DFF
        self.L = SEQ + N_META
        self.NT = 1 + SEQ // 128
        self.KC = DM // 128
        self.NF = DFF // 128
        self.BT = 7

    def tile(self, t):
        if t == 0:
            return 0, N_META
        return N_META + (t - 1) * 128, 128

    def blocks(self, seqs=None):
        tl = []
        for s in (range(self.NSEQ) if seqs is None else seqs):
            for t in range(1, self.NT):
                tl.append((s, t))
            tl.append((s, 0))
        out = []
        for b0 in range(0, len(tl), self.BT):
            blk = []
            c = 0
            for (s, t) in tl[b0:b0 + self.BT]:
                r0, rows = self.tile(t)
                blk.append((s, t, r0, rows, c))
                c += rows
            out.append(blk)
        return out


def col_groups(ncols, w=512):
    return [(c, min(c + w, ncols)) for c in range(0, ncols, w)]


class Ctx:
    pass


_UID = [0]


def _uname(name):
    _UID[0] += 1
    return '%s_u%d' % (name, _UID[0])


def hres(s, t):
    return 'H_%d_%d' % (s, t)


def norm_T(C, src_ap, src_res, rows, gbc, AT, at_res, col0, i):
    P, cfg = C.P, C.cfg
    DM, KC = cfg.DM, cfg.KC
    k = i % 2
    hb, hn = C.hbuf[k], C.hn[k]
    hbr, hnr = 'hbuf%d' % k, 'hn%d' % k
    _dma(P, 'sp', hb[:rows, :], src_ap, [src_res], [hbr])
    _act(P, C.sq[:rows, :], hb[:rows, :], AF.Square, [hbr], ['sq', 'ss%d' % k], accum_out=C.ss[:rows, k:k + 1])
    _act(P, C.rs[:rows, k:k + 1], C.ss[:rows, k:k + 1], AF.Sqrt, ['ss%d' % k], ['rs%d' % k], scale=1.0 / DM, bias=C.epsb[:rows, :])
    _recip(P, C.rstd[:rows, k:k + 1], C.rs[:rows, k:k + 1], ['rs%d' % k], ['rstd%d' % k])
    _stt(P, hn[:rows, :], hb[:rows, :], C.rstd[:rows, k:k + 1], gbc[:rows, :], ALU.mult, ALU.mult,
         [hbr, 'rstd%d' % k, 'gbc'], [hnr])
    transpose_into(C, hn, hnr, rows, KC, AT, at_res, col0)


def transpose_into(C, src, src_res, rows, nch, AT, at_res, col0, chw=128):
    P = C.P
    for g4 in range(0, nch, 4):
        n4 = min(4, nch - g4)
        b = P.nextbank()
        pb = C.psb[b]
        for j in range(n4):
            ch = g4 + j
            _tr(P, pb[:chw, j * 128:j * 128 + rows], src[:rows, ch * chw:(ch + 1) * chw], C.idb[:rows, :rows],
                [src_res, 'idb'], ['ps%d' % b])
        src_v = pb[:chw, 0:n4 * 128].rearrange("p (j c) -> p j c", c=128)[:, :, :rows]
        eng = 'act' if (C.evac_ctr % 2 == 0) else 'dve'
        C.evac_ctr += 1
        _cp(P, eng, AT[:chw, g4:g4 + n4, col0:col0 + rows], src_v, ['ps%d' % b], [at_res])


def gemm_tm(C, AT, at_res, KC, blk, w_r, N, epi, wbufs, wres, kstep=8, gw=512, alt=None):
    P = C.P
    for (c0, c1) in col_groups(N, gw):
        n = c0 // gw
        w = c1 - c0
        banks = [P.nextbank() for _ in blk]
        for k0 in range(0, KC, kstep):
            k1 = min(KC, k0 + kstep)
            wb = C.wctr % len(wbufs)
            C.wctr += 1
            wt, wr_ = wbufs[wb], '%s%d' % (wres, wb)
            if alt is not None and (C.wctr % 2 == 0):
                stg, stg_res = alt
                _dma(P, 'sp', stg[:, 0:k1 - k0, 0:w], w_r[:, k0:k1, c0:c1], [], [stg_res])
                _act(P, wt[:, 0:k1 - k0, 0:w], stg[:, 0:k1 - k0, 0:w], AF.Copy, [stg_res], [wr_])
            else:
                _dma(P, 'pool', wt[:, 0:k1 - k0, 0:w], w_r[:, k0:k1, c0:c1], [], [wr_])
            for kc in range(k0, k1):
                for ti, (s, t, r0, rows, col0) in enumerate(blk):
                    _mm(P, C.ps[banks[ti]][:rows, 0:w], AT[:, kc, col0:col0 + rows], wt[:, kc - k0, 0:w],
                        kc == 0, kc == KC - 1, [at_res, wr_], ['ps%d' % banks[ti]])
        for ti, te in enumerate(blk):
            epi(n, (c0, c1), te, C.ps[banks[ti]][:te[3], 0:w], 'ps%d' % banks[ti])


def residual_epi(C, H, coef):
    P = C.P

    def epi(n, cc, te, psap, psres):
        s, t, r0, rows, col0 = te
        k = C.hres_ctr % len(C.hres)
        C.hres_ctr += 1
        hr, hrr = C.hres[k], 'hres%d' % k
        w = cc[1] - cc[0]
        _dma(P, 'sp', hr[:rows, 0:w], H[s, r0:r0 + rows, cc[0]:cc[1]], [hres(s, t)], [hrr])
        _stt(P, hr[:rows, 0:w], psap, coef, hr[:rows, 0:w], ALU.mult, ALU.add, [psres, hrr], [hrr])
        _dma(P, 'sp', H[s, r0:r0 + rows, cc[0]:cc[1]], hr[:rows, 0:w], [hrr], [hres(s, t)])
    return epi


def ffn_phase(C, H, gain_d, win_r, wout_r):
    nc, P, cfg = C.nc, C.P, C.cfg
    DM, KC, NF = cfg.DM, cfg.KC, cfg.NF
    BC = cfg.BT * 128
    with ExitStack() as es:
        def A(name, shape, dt):
            return es.enter_context(nc.sbuf_tensor(_uname(name), shape, dt))
        gbc = A("f_gbc", [128, DM], F32)
        xnT = A("f_xnT", [128, KC, BC], BF16)
        actT = A("f_actT", [128, NF, BC], BF16)
        hb0 = A("f_hb0", [128, DM], F32)
        hb1 = A("f_hb1", [128, DM], F32)
        hn0 = A("f_hn0", [128, DM], BF16)
        hn1 = A("f_hn1", [128, DM], BF16)
        sq = A("f_sq", [128, DM], BF16)
        win0 = A("f_win0", [128, KC, 256], BF16)
        win1 = A("f_win1", [128, KC, 256], BF16)
        wo0 = A("f_wo0", [128, 8, 512], BF16)
        wo1 = A("f_wo1", [128, 8, 512], BF16)
        wo2 = A("f_wo2", [128, 8, 512], BF16)
        sg0 = A("f_sg0", [128, 512], F32)
        sg1 = A("f_sg1", [128, 512], F32)
        hr0 = A("f_hr0", [128, 512], F32)
        hr1 = A("f_hr1", [128, 512], F32)
        hr2 = A("f_hr2", [128, 512], F32)
        C.hbuf, C.hn, C.sq = [hb0, hb1], [hn0, hn1], sq
        C.hres = [hr0, hr1, hr2]
        wins = [win0, win1]
        sgs = [sg0, sg1]
        _dma(P, 'sp', gbc[:, :], gain_d.partition_broadcast(128), [], ['gbc'])
        epi = residual_epi(C, H, 0.5)
        for blk in cfg.blocks():
            ncols = sum(te[3] for te in blk)
            for i, (s, t, r0, rows, col0) in enumerate(blk):
                norm_T(C, H[s, r0:r0 + rows, :], hres(s, t), rows, gbc, xnT, 'xnT', col0, i)
            groups = col_groups(ncols)
            for f in range(NF):
                wb = f % 2
                wt, wr_ = wins[wb], 'win%d' % wb
                _dma(P, 'pool', wt[:, :, :], win_r[f], [], [wr_])
                for (c0, c1) in groups:
                    w = c1 - c0
                    bg, bu = P.nextbank(), P.nextbank()
                    for kc in range(KC):
                        _mm(P, C.ps[bg][:, 0:w], wt[:, kc, 0:128], xnT[:, kc, c0:c1], kc == 0, kc == KC - 1,
                            [wr_, 'xnT'], ['ps%d' % bg])
                    for kc in range(KC):
                        _mm(P, C.ps[bu][:, 0:w], wt[:, kc, 128:256], xnT[:, kc, c0:c1], kc == 0, kc == KC - 1,
                            [wr_, 'xnT'], ['ps%d' % bu])
                    k = C.sg_ctr % 2
                    C.sg_ctr += 1
                    _act(P, sgs[k][:, 0:w], C.ps[bg][:, 0:w], AF.Silu, ['ps%d' % bg], ['sg%d' % k])
                    _tt(P, 'dve', actT[:, f, c0:c1], sgs[k][:, 0:w], C.ps[bu][:, 0:w], ALU.mult,
                        ['sg%d' % k, 'ps%d' % bu], ['actT'])
            stg = xnT[:, 0:10, :].rearrange("p a b -> p (a b)").bitcast(F32)[:, 0:4096].rearrange("p (k c) -> p k c", c=512)
            gemm_tm(C, actT, 'actT', NF, blk, wout_r, DM, epi, [wo0, wo1, wo2], 'wo', alt=(stg, 'xnT'))
        P.flush()


def out_phase(C, H, gain_d, out_d, raw=False):
    nc, P, cfg = C.nc, C.P, C.cfg
    DM = cfg.DM
    with ExitStack() as es:
        def A(name, shape, dt):
            return es.enter_context(nc.sbuf_tensor(_uname(name), shape, dt))
        gbc = A("o_gbc", [128, DM], F32)
        hb = [A("o_hb0", [128, DM], F32), A("o_hb1", [128, DM], F32)]
        ob = [A("o_ob0", [128, DM], F32), A("o_ob1", [128, DM], F32)]
        sq = A("o_sq", [128, DM], BF16)
        _dma(P, 'sp', gbc[:, :], gain_d.partition_broadcast(128), [], ['gbc'])
        i = 0
        for s in range(cfg.NSEQ):
            for t in range(1, cfg.NT):
                r0, rows = cfg.tile(t)
                k = i % 2
                i += 1
                hbr, obr = 'ohb%d' % k, 'oob%d' % k
                _dma(P, 'sp', hb[k][:rows, :], H[s, r0:r0 + rows, :], [hres(s, t)], [hbr])
                if raw:
                    _dma(P, 'sp', out_d[s, r0 - N_META:r0 - N_META + rows, :], hb[k][:rows, :], [hbr], ['out_%d_%d' % (s, t)])
                    continue
                _act(P, sq[:rows, :], hb[k][:rows, :], AF.Square, [hbr], ['sq', 'ss%d' % k], accum_out=C.ss[:rows, k:k + 1])
                _act(P, C.rs[:rows, k:k + 1], C.ss[:rows, k:k + 1], AF.Sqrt, ['ss%d' % k], ['rs%d' % k], scale=1.0 / DM, bias=C.epsb[:rows, :])
                _recip(P, C.rstd[:rows, k:k + 1], C.rs[:rows, k:k + 1], ['rs%d' % k], ['rstd%d' % k])
                _stt(P, ob[k][:rows, :], hb[k][:rows, :], C.rstd[:rows, k:k + 1], gbc[:rows, :], ALU.mult, ALU.mult,
                     [hbr, 'rstd%d' % k, 'gbc'], [obr])
                _dma(P, 'sp', out_d[s, r0 - N_META:r0 - N_META + rows, :], ob[k][:rows, :], [obr], ['out_%d_%d' % (s, t)])
        P.flush()


def build(cfg, plan=None):
    nc = bass.Bass("TRN2", target_bir_lowering=False)
    DM, DFF, L, NSEQ, SEQ, DEPTH = cfg.DM, cfg.DFF, cfg.L, cfg.NSEQ, cfg.SEQ, cfg.DEPTH
    KC, NF = cfg.KC, cfg.NF
    nA, nB = (DEPTH + 1) // 2, DEPTH // 2

    def din(name, shape, dt=F32):
        return nc.dram_tensor(name, list(shape), dt, kind="ExternalInput").ap()
    D = {}
    D['x'] = din('x', [NSEQ, SEQ, DM])
    D['meta'] = din('meta_tokens', [N_META, DM])
    for nm in ('norm_ffn1', 'norm_mix', 'norm_ffn2'):
        D[nm] = din(nm, [DEPTH, 1, DM])
    D['final_norm'] = din('final_norm', [1, DM])
    for nm in ('ffn1', 'ffn2'):
        D[nm + '_win'] = din(nm + '_win_r', [DEPTH, NF, 128, KC, 256])
        D[nm + '_wout'] = din(nm + '_wout_r', [DEPTH, 128, NF, DM])
    D['c_ident'] = din('c_ident', [128, 128])
    D['c_tri'] = din('c_tri', [128, 128])
    D['c_maskA'] = din('c_maskA', [128, 128])
    D['c_cos'] = din('c_cos', [L, 32])
    D['c_sin'] = din('c_sin', [L, 32])
    mixer_decl(nc, cfg, D, din, nA, nB)
    out_d = nc.dram_tensor('out', [NSEQ, SEQ, DM], F32, kind="ExternalOutput").ap()
    H = nc.dram_tensor('Hres', [NSEQ, L, DM], F32, kind="Internal").ap()

    C = Ctx()
    C.nc, C.cfg, C.D = nc, cfg, D
    C.P = P = Prog(nc)
    C.evac_ctr = C.wctr = C.hres_ctr = C.sg_ctr = 0
    C.ps = [nc.alloc_psum_tensor('psb%d' % i, [128, 512], F32) for i in range(8)]
    C.psb = [p[:, :].bitcast(BF16) for p in C.ps]
    C.ps = [p[:, :] for p in C.ps]
    C.idb = nc.alloc_sbuf_tensor('c_idb', [128, 128], BF16)
    C.idf = nc.alloc_sbuf_tensor('c_idf', [128, 128], F32)
    C.tri = nc.alloc_sbuf_tensor('c_trif', [128, 128], F32)
    C.maskA = nc.alloc_sbuf_tensor('c_maskAf', [128, 128], F32)
    C.onesf = nc.alloc_sbuf_tensor('c_onesf', [128, 128], F32)
    C.epsb = nc.alloc_sbuf_tensor('c_epsb', [128, 1], F32)
    C.ss = nc.alloc_sbuf_tensor('c_ss', [128, 2], F32)
    C.rs = nc.alloc_sbuf_tensor('c_rs', [128, 2], F32)
    C.rstd = nc.alloc_sbuf_tensor('c_rstd', [128, 2], F32)
    _dma(P, 'pool', C.idb[:, :], D['c_ident'], [], ['idb'])
    _dma(P, 'sp', C.idf[:, :], D['c_ident'], [], ['idf'])
    _dma(P, 'sp', C.tri[:, :], D['c_tri'], [], ['tri'])
    _dma(P, 'sp', C.maskA[:, :], D['c_maskA'], [], ['maskA'])
    _memset(P, 'dve', C.onesf[:, :], 1.0, ['onesf'])
    _memset(P, 'dve', C.epsb[:, :], EPS, ['epsb'])
    for s in range(NSEQ):
        _dma(P, 'sp', H[s, 0:N_META, :], D['meta'], [], [hres(s, 0)])
        for t in range(1, cfg.NT):
            r0, rows = cfg.tile(t)
            _dma(P, 'sp', H[s, r0:r0 + rows, :], D['x'][s, r0 - N_META:r0 - N_META + rows, :], [], [hres(s, t)])
    P.flush()
    if plan is None:
        plan = []
        for i in range(DEPTH):
            plan += [('ffn1', i), ('mla' if i % 2 == 0 else 'ssm', i // 2, i), ('ffn2', i)]
        plan += [('out',)]
    for st in plan:
        if st[0] in ('ffn1', 'ffn2'):
            i = st[1]
            ffn_phase(C, H, D['norm_' + st[0]][i], D[st[0] + '_win'][i], D[st[0] + '_wout'][i])
        elif st[0] == 'mla':
            mla_phase(C, H, st[1], st[2])
        elif st[0] == 'ssm':
            ssm_phase(C, H, st[1], st[2])
        elif st[0] == 'ssm1':
            ssm_part1(C, H, st[1], st[2])
        elif st[0] == 'ssm2':
            ssm_part2(C, st[1])
        elif st[0] == 'ssm3':
            ssm_part3(C, st[1])
        elif st[0] == 'ssm4':
            ssm_part4(C, H, st[1])
        elif st[0] == 'out':
            out_phase(C, H, D['final_norm'], out_d)
        elif st[0] == 'raw':
            out_phase(C, H, D['final_norm'], out_d, raw=True)
    return nc


def mixer_decl(nc, cfg, D, din, nA, nB):
    mla_decl(nc, cfg, D, din, nA)
    ssm_decl(nc, cfg, D, din, nB)


def host_consts(cfg):
    L = cfg.L
    ident = np.eye(128, dtype=np.float32)
    k = np.arange(128)
    tri = (k[:, None] <= k[None, :]).astype(np.float32)
    maskA = np.where(k[None, :] <= k[:, None], 0.0, NEG).astype(np.float32)
    inv_freq = (1.0 / (10000.0 ** (np.arange(0, 64, 2, dtype=np.float32) / np.float32(64)))).astype(np.float32)
    ang = np.arange(L, dtype=np.float32)[:, None] * inv_freq[None, :]
    return dict(c_ident=ident, c_tri=tri, c_maskA=maskA, c_maskS=np.ascontiguousarray(maskA.T),
                c_cos=np.cos(ang).astype(np.float32), c_sin=np.sin(ang).astype(np.float32))


def host_ffn_layout(cfg, w_in, w_out):
    Dp = w_in.shape[0]
    KC, NF, DFF, DM = cfg.KC, cfg.NF, cfg.DFF, cfg.DM
    g = w_in[:, :, :DFF].reshape(Dp, KC, 128, NF, 128)
    u = w_in[:, :, DFF:].reshape(Dp, KC, 128, NF, 128)
    gu = np.concatenate([g, u], axis=-1)
    win_r = np.ascontiguousarray(gu.transpose(0, 3, 2, 1, 4))
    wout_r = np.ascontiguousarray(w_out.reshape(Dp, NF, 128, DM).transpose(0, 2, 1, 3))
    return win_r, wout_r


MLA_H = 16
ATT_SCALE = float(192 ** -0.5)


def rope(C, src, cs, sn, dst, rows, nh, rres, wres):
    P = C.P
    a, b = C.ropeA[:rows, 0:nh, :], C.ropeB[:rows, 0:nh, :]
    csb = cs[:rows, :].unsqueeze(1).to_broadcast([rows, nh, 32])
    snb = sn[:rows, :].unsqueeze(1).to_broadcast([rows, nh, 32])
    t1, t2 = src[:, :, 0:32], src[:, :, 32:64]
    _tt(P, 'dve', a, t1, csb, ALU.mult, rres, ['ropeA'])
    _tt(P, 'dve', b, t2, snb, ALU.mult, rres, ['ropeB'])
    _tt(P, 'dve', dst[:, :, 0:32], a, b, ALU.subtract, ['ropeA', 'ropeB'], wres)
    _tt(P, 'dve', a, t2, csb, ALU.mult, rres, ['ropeA'])
    _tt(P, 'dve', b, t1, snb, ALU.mult, rres, ['ropeB'])
    _tt(P, 'dve', dst[:, :, 32:64], a, b, ALU.add, ['ropeA', 'ropeB'], wres)


def mla_decl(nc, cfg, D, din, nA):
    L, NSEQ = cfg.L, cfg.NSEQ
    D['mla_win'] = din('mla_win_r', [nA, 128, cfg.KC, 1088])
    D['mla_wuq'] = din('mla_wuq_r', [nA, 128, 4, 3072])
    D['mla_wukv'] = din('mla_wukv_r', [nA, 128, 4, 4096])
    D['mla_wo'] = din('mla_wo_r', [nA, 128, 16, cfg.DM])
    D['mla_qn'] = din('mla_q_norm', [nA, 1, 512])
    D['mla_kvn'] = din('mla_kv_norm', [nA, 1, 512])

    def scr(name, shape):
        return nc.dram_tensor(name, shape, BF16, kind="Internal").ap()
    D['QN'] = scr('s_QN', [NSEQ, L, 2048])
    D['QR'] = scr('s_QR', [NSEQ, L, 1024])
    D['KN'] = scr('s_KN', [NSEQ, L, 2048])
    D['V'] = scr('s_V', [NSEQ, L, 2048])
    D['KR'] = scr('s_KR', [NSEQ, L, 64])
    D['OT'] = scr('s_OT', [NSEQ, 128, 16, L])


def mla_part1(C, H, j, i):
    nc, P, cfg, D = C.nc, C.P, C.cfg, C.D
    DM, KC = cfg.DM, cfg.KC
    BC = cfg.BT * 128
    with ExitStack() as es:
        def A(name, shape, dt):
            return es.enter_context(nc.sbuf_tensor(_uname(name), shape, dt))
        gbc = A("m_gbc", [128, DM], F32)
        uT = A("m_uT", [128, KC, BC], BF16)
        C.hbuf = [A("m_hb0", [128, DM], F32), A("m_hb1", [128, DM], F32)]
        C.hn = [A("m_hn0", [128, DM], BF16), A("m_hn1", [128, DM], BF16)]
        C.sq = A("m_sq", [128, DM], BF16)
        cqT = A("m_cqT", [128, 4, BC], BF16)
        ckvT = A("m_ckvT", [128, 4, BC], BF16)
        wb = [A("m_w%d" % k, [128, 8, 512], BF16) for k in range(3)]
        gq = A("m_gq", [128, 512], F32)
        gkv = A("m_gkv", [128, 512], F32)
        cn = [A("m_cn%d" % k, [128, 512], BF16) for k in range(2)]
        st = [A("m_st%d" % k, [128, 512], BF16) for k in range(3)]
        fr = [A("m_fr%d" % k, [128, 512], F32) for k in range(2)]
        cs = [A("m_cs%d" % k, [128, 32], F32) for k in range(2)]
        sn = [A("m_sn%d" % k, [128, 32], F32) for k in range(2)]
        C.ropeA = A("m_ropeA", [128, 8, 32], F32)
        C.ropeB = A("m_ropeB", [128, 8, 32], F32)
        ctr = {'cn': 0, 'st': 0, 'fr': 0, 'cs': 0}
        _dma(P, 'sp', gbc[:, :], D['norm_mix'][i].partition_broadcast(128), [], ['gbc'])
        _dma(P, 'sp', gq[:, :], D['mla_qn'][j].partition_broadcast(128), [], ['gq'])
        _dma(P, 'sp', gkv[:, :], D['mla_kvn'][j].partition_broadcast(128), [], ['gkv'])

        def load_cs(r0, rows):
            k = ctr['cs'] % 2
            ctr['cs'] += 1
            _dma(P, 'sp', cs[k][:rows, :], D['c_cos'][r0:r0 + rows, :], [], ['cs%d' % k])
            _dma(P, 'sp', sn[k][:rows, :], D['c_sin'][r0:r0 + rows, :], [], ['sn%d' % k])
            return k

        def store_bf(psap, psres, rows, w, dst_ap, dst_res):
            k = ctr['st'] % 3
            ctr['st'] += 1
            eng = 'act' if k % 2 == 0 else 'dve'
            _cp(P, eng, st[k][:rows, 0:w], psap, [psres], ['st%d' % k])
            _dma(P, 'sp', dst_ap, st[k][:rows, 0:w], ['st%d' % k], [dst_res])

        def rope_store(psap, psres, te, nh, dst_ap, dst_res):
            s, t, r0, rows, col0 = te
            kf = ctr['fr'] % 2
            ctr['fr'] += 1
            _cp(P, 'act', fr[kf][:rows, 0:nh * 64], psap, [psres], ['fr%d' % kf])
            kc_ = load_cs(r0, rows)
            k = ctr['st'] % 3
            ctr['st'] += 1
            src = fr[kf][:rows, 0:nh * 64].rearrange("p (h d) -> p h d", d=64)
            dst = st[k][:rows, 0:nh * 64].rearrange("p (h d) -> p h d", d=64)
            rope(C, src, cs[kc_], sn[kc_], dst, rows, nh, ['fr%d' % kf, 'cs%d' % kc_, 'sn%d' % kc_], ['st%d' % k])
            _dma(P, 'sp', dst_ap, st[k][:rows, 0:nh * 64], ['st%d' % k], [dst_res])

        for blk in cfg.blocks():
            for ii, (s, t, r0, rows, col0) in enumerate(blk):
                norm_T(C, H[s, r0:r0 + rows, :], hres(s, t), rows, gbc, uT, 'uT', col0, ii)

            def epi_c(n, cc, te, psap, psres):
                s, t, r0, rows, col0 = te
                if n < 2:
                    k = ctr['cn'] % 2
                    ctr['cn'] += 1
                    _act(P, C.sq[:rows, 0:512], psap, AF.Square, [psres], ['sq', 'ss%d' % k], accum_out=C.ss[:rows, k:k + 1])
                    _act(P, C.rs[:rows, k:k + 1], C.ss[:rows, k:k + 1], AF.Sqrt, ['ss%d' % k], ['rs%d' % k], scale=1.0 / 512, bias=C.epsb[:rows, :])
                    _recip(P, C.rstd[:rows, k:k + 1], C.rs[:rows, k:k + 1], ['rs%d' % k], ['rstd%d' % k])
                    g_, gr_ = (gq, 'gq') if n == 0 else (gkv, 'gkv')
                    _stt(P, cn[k][:rows, :], psap, C.rstd[:rows, k:k + 1], g_[:rows, :], ALU.mult, ALU.mult,
                         [psres, 'rstd%d' % k, gr_], ['cn%d' % k])
                    if n == 0:
                        transpose_into(C, cn[k], 'cn%d' % k, rows, 4, cqT, 'cqT', col0)
                    else:
                        transpose_into(C, cn[k], 'cn%d' % k, rows, 4, ckvT, 'ckvT', col0)
                else:
                    rope_store(psap, psres, te, 1, D['KR'][s, r0:r0 + rows, :], 'KR_%d_%d' % (s, t))
            gemm_tm(C, uT, 'uT', KC, blk, D['mla_win'][j], 1088, epi_c, wb, 'mw')

            def epi_q(n, cc, te, psap, psres):
                s, t, r0, rows, col0 = te
                if n < 4:
                    store_bf(psap, psres, rows, 512, D['QN'][s, r0:r0 + rows, cc[0]:cc[1]], 'QN_%d_%d' % (s, t))
                else:
                    rope_store(psap, psres, te, 8, D['QR'][s, r0:r0 + rows, cc[0] - 2048:cc[1] - 2048], 'QR_%d_%d' % (s, t))
            gemm_tm(C, cqT, 'cqT', 4, blk, D['mla_wuq'][j], 3072, epi_q, wb, 'mw', kstep=4)

            def epi_kv(n, cc, te, psap, psres):
                s, t, r0, rows, col0 = te
                if n < 4:
                    store_bf(psap, psres, rows, 512, D['KN'][s, r0:r0 + rows, cc[0]:cc[1]], 'KN_%d_%d' % (s, t))
                else:
                    store_bf(psap, psres, rows, 512, D['V'][s, r0:r0 + rows, cc[0] - 2048:cc[1] - 2048], 'V_%d_%d' % (s, t))
            gemm_tm(C, ckvT, 'ckvT', 4, blk, D['mla_wukv'][j], 4096, epi_kv, wb, 'mw', kstep=4)
        P.flush()


def load_tok(C, dst, dst_res, src, s, c0, c1, rres_fn):
    P, cfg = C.P, C.cfg
    rr = [rres_fn(s, t) for t in range(cfg.NT)]
    _dma(P, 'sp', dst[:N_META, 0, :], src[s, 0:N_META, c0:c1], rr, [dst_res])
    _dma(P, 'sp', dst[:, 1:cfg.NT, :], src[s, N_META:, c0:c1].rearrange("(t p) d -> p t d", p=128), rr, [dst_res])


def transpose_tok(C, tok, tok_res, nd, dstT, dst_res):
    P, cfg = C.P, C.cfg
    groups = [[0]] + [list(range(t, min(t + 4, cfg.NT))) for t in range(1, cfg.NT, 4)]
    for g in groups:
        b = P.nextbank()
        pb = C.psb[b]
        for jj, t in enumerate(g):
            r0, rows = cfg.tile(t)
            _tr(P, pb[:nd, jj * 128:jj * 128 + rows], tok[:rows, t, :], C.idb[:rows, :rows], [tok_res, 'idb'], ['ps%d' % b])
        r0, rows = cfg.tile(g[0])
        ncol = sum(cfg.tile(t)[1] for t in g)
        eng = 'act' if (C.evac_ctr % 2 == 0) else 'dve'
        C.evac_ctr += 1
        _cp(P, eng, dstT[:nd, r0:r0 + ncol], pb[:nd, 0:ncol], ['ps%d' % b], [dst_res])


def mla_part2(C, j):
    nc, P, cfg, D = C.nc, C.P, C.cfg, C.D
    L, NT = cfg.L, cfg.NT
    bounds = [0] + [cfg.tile(t)[0] for t in range(4, NT, 4)] + [L]
    with ExitStack() as es:
        def A(name, shape, dt):
            return es.enter_context(nc.sbuf_tensor(_uname(name), shape, dt))
        krtok = A("a_krtok", [128, NT, 64], BF16)
        KRT = A("a_KRT", [64, L], BF16)
        S2 = []
        for k in range(2):
            S2.append(dict(
                ktok=A("a_ktok%d" % k, [128, NT, 128], BF16), vtok=A("a_vtok%d" % k, [128, NT, 128], BF16),
                qntok=A("a_qntok%d" % k, [128, NT, 128], BF16), qrtok=A("a_qrtok%d" % k, [128, NT, 64], BF16),
                KT=A("a_KT%d" % k, [128, L], BF16), QNT=A("a_QNT%d" % k, [128, L], BF16),
                QRT=A("a_QRT%d" % k, [64, L], BF16), OT=A("a_OT%d" % k, [128, L], BF16)))
        Pb = [A("a_Pb%d" % k, [128, L], BF16) for k in range(2)]
        PT = [A("a_PT%d" % k, [128, NT, 128], BF16) for k in range(2)]
        mxs = [A("a_mxs%d" % k, [128, 8], F32) for k in range(2)]
        mx = [A("a_mx%d" % k, [128, 1], F32) for k in range(2)]
        nb = [A("a_nb%d" % k, [128, 1], F32) for k in range(2)]
        sums = [A("a_sums%d" % k, [128, 8], F32) for k in range(2)]
        rsum = [A("a_rsum%d" % k, [128, 1], F32) for k in range(2)]
        rinv = [A("a_rinv%d" % k, [128, 1], F32) for k in range(2)]
        KRT2 = [KRT, A("a_KRT1", [64, L], BF16)]
        krtok2 = [krtok, A("a_krtok1", [128, NT, 64], BF16)]
        P.bankset = [5, 6]
        units = [(s, h) for s in range(cfg.NSEQ) for h in range(MLA_H)]
        itc = [0]

        def prologue(u):
            s, h = units[u]
            B = S2[u % 2]
            sfx = str(u % 2)
            if h == 0:
                ss_ = str(s % 2)
                load_tok(C, krtok2[s % 2], 'krtok' + ss_, D['KR'], s, 0, 64, lambda s_, t_: 'KR_%d_%d' % (s_, t_))
                transpose_tok(C, krtok2[s % 2], 'krtok' + ss_, 64, KRT2[s % 2], 'KRT' + ss_)
            load_tok(C, B['ktok'], 'ktok' + sfx, D['KN'], s, h * 128, (h + 1) * 128, lambda s_, t_: 'KN_%d_%d' % (s_, t_))
            load_tok(C, B['vtok'], 'vtok' + sfx, D['V'], s, h * 128, (h + 1) * 128, lambda s_, t_: 'V_%d_%d' % (s_, t_))
            load_tok(C, B['qntok'], 'qntok' + sfx, D['QN'], s, h * 128, (h + 1) * 128, lambda s_, t_: 'QN_%d_%d' % (s_, t_))
            load_tok(C, B['qrtok'], 'qrtok' + sfx, D['QR'], s, h * 64, (h + 1) * 64, lambda s_, t_: 'QR_%d_%d' % (s_, t_))
            transpose_tok(C, B['ktok'], 'ktok' + sfx, 128, B['KT'], 'KT' + sfx)
            transpose_tok(C, B['qntok'], 'qntok' + sfx, 128, B['QNT'], 'QNT' + sfx)
            transpose_tok(C, B['qrtok'], 'qrtok' + sfx, 64, B['QRT'], 'QRT' + sfx)

        def stageA(u, i, k):
            s, h = units[u]
            B = S2[u % 2]
            sfx = str(u % 2)
            KRTs, krr = KRT2[s % 2], 'KRT' + str(s % 2)
            r0, rows = cfg.tile(i)
            nkeys = r0 + rows
            ks = str(k)
            grp = [(bounds[g], min(bounds[g + 1], nkeys)) for g in range(len(bounds) - 1) if bounds[g] < nkeys]
            for gi, (g0, g1) in enumerate(grp):
                w = g1 - g0
                pr = 'ps%d' % gi
                psg = C.ps[gi][:rows, 0:w]
                _mm(P, psg, B['QNT'][:, r0:r0 + rows], B['KT'][:, g0:g1], True, False, ['QNT' + sfx, 'KT' + sfx], [pr])
                _mm(P, psg, B['QRT'][:64, r0:r0 + rows], KRTs[:64, g0:g1], False, True, ['QRT' + sfx, krr], [pr])
                if g1 == nkeys:
                    lo = r0 - g0
                    dg = C.ps[gi][:rows, lo:lo + rows]
                    _tt(P, 'dve', dg, dg, C.maskA[:rows, :rows], ALU.add, [pr, 'maskA'], [pr])
                _red(P, mxs[k][:rows, gi:gi + 1], psg, ALU.max, [pr], ['mxs' + ks])
            ng = len(grp)
            _red(P, mx[k][:rows, :], mxs[k][:rows, 0:ng], ALU.max, ['mxs' + ks], ['mx' + ks])
            _ts(P, 'dve', nb[k][:rows, :], mx[k][:rows, :], -ATT_SCALE, None, ALU.mult, None, ['mx' + ks], ['nb' + ks])
            for gi, (g0, g1) in enumerate(grp):
                w = g1 - g0
                pr = 'ps%d' % gi
                _act(P, Pb[k][:rows, g0:g1], C.ps[gi][:rows, 0:w], AF.Exp, [pr, 'nb' + ks], ['Pb' + ks, 'sums' + ks],
                     bias=nb[k][:rows, :], scale=ATT_SCALE, accum_out=sums[k][:rows, gi:gi + 1])
            _red(P, rsum[k][:rows, :], sums[k][:rows, 0:ng], ALU.add, ['sums' + ks], ['rsum' + ks])
            _recip(P, rinv[k][:rows, :], rsum[k][:rows, :], ['rsum' + ks], ['rinv' + ks])
            _ts(P, 'dve', Pb[k][:rows, 0:nkeys], Pb[k][:rows, 0:nkeys], rinv[k][:rows, :], None, ALU.mult, None,
                ['Pb' + ks, 'rinv' + ks], ['Pb' + ks])

        def stageB(u, i, k):
            s, h = units[u]
            B = S2[u % 2]
            sfx = str(u % 2)
            r0, rows = cfg.tile(i)
            ks = str(k)
            kt = list(range(i + 1))
            for q0 in range(0, len(kt), 4):
                g = kt[q0:q0 + 4]
                b = P.nextbank()
                for jj, jt in enumerate(g):
                    kr0, kn = cfg.tile(jt)
                    _tr(P, C.psb[b][:kn, jj * 128:jj * 128 + rows], Pb[k][:rows, kr0:kr0 + kn], C.idb[:rows, :rows],
                        ['Pb' + ks, 'idb'], ['ps%d' % b])
                srcv = C.psb[b][:, 0:len(g) * 128].rearrange("p (j c) -> p j c", c=128)[:, :, :rows]
                eng = 'act' if (C.evac_ctr % 2 == 0) else 'dve'
                C.evac_ctr += 1
                _cp(P, eng, PT[k][:, g[0]:g[0] + len(g), :rows], srcv, ['ps%d' % b], ['PT' + ks])
            bo = 7
            for jt in kt:
                kr0, kn = cfg.tile(jt)
                _mm(P, C.ps[bo][:, 0:rows], B['vtok'][:kn, jt, :], PT[k][:kn, jt, :rows], jt == 0, jt == i,
                    ['vtok' + sfx, 'PT' + ks], ['ps%d' % bo])
            _cp(P, 'act', B['OT'][:, r0:r0 + rows], C.ps[bo][:, 0:rows], ['ps%d' % bo], ['OT' + sfx])

        prologue(0)
        for u, (s, h) in enumerate(units):
            if u + 1 < len(units):
                prologue(u + 1)
            kk = [(itc[0] + i) % 2 for i in range(NT)]
            itc[0] += NT
            stageA(u, 0, kk[0])
            for i in range(1, NT):
                stageA(u, i, kk[i])
                stageB(u, i - 1, kk[i - 1])
            stageB(u, NT - 1, kk[NT - 1])
            _dma(P, 'sp', D['OT'][s, :, h, :], S2[u % 2]['OT'][:, :], ['OT' + str(u % 2)], ['OTd_%d' % s])
        P.bankset = None
        P.flush()


def mla_part3(C, H, j):
    nc, P, cfg, D = C.nc, C.P, C.cfg, C.D
    BC = cfg.BT * 128
    with ExitStack() as es:
        def A(name, shape, dt):
            return es.enter_context(nc.sbuf_tensor(_uname(name), shape, dt))
        oT = [A("p_oT%d" % k, [128, 16, BC], BF16) for k in range(2)]
        wb = [A("p_w%d" % k, [128, 8, 512], BF16) for k in range(3)]
        C.hres = [A("p_hr%d" % k, [128, 512], F32) for k in range(3)]
        epi = residual_epi(C, H, 1.0)
        for bi, blk in enumerate(cfg.blocks()):
            k = bi % 2
            for (s, t, r0, rows, col0) in blk:
                _dma(P, 'sp', oT[k][:, :, col0:col0 + rows], D['OT'][s, :, :, r0:r0 + rows], ['OTd_%d' % s], ['oT%d' % k])
            gemm_tm(C, oT[k], 'oT%d' % k, 16, blk, D['mla_wo'][j], cfg.DM, epi, wb, 'pw')
        P.flush()


def mla_phase(C, H, j, i):
    mla_part1(C, H, j, i)
    mla_part2(C, j)
    mla_part3(C, H, j)


def host_mla_layout(cfg, w_in, w_uq, w_ukv, w_o):
    nA = w_in.shape[0]
    KC = cfg.KC
    win_r = np.ascontiguousarray(w_in.reshape(nA, KC, 128, 1088).transpose(0, 2, 1, 3))
    q = w_uq.reshape(nA, 512, 16, 192)
    wuq = np.concatenate([q[..., :128].reshape(nA, 512, 2048), q[..., 128:].reshape(nA, 512, 1024)], axis=-1)
    wuq_r = np.ascontiguousarray(wuq.reshape(nA, 4, 128, 3072).transpose(0, 2, 1, 3))
    kv = w_ukv.reshape(nA, 512, 16, 256)
    wukv = np.concatenate([kv[..., :128].reshape(nA, 512, 2048), kv[..., 128:].reshape(nA, 512, 2048)], axis=-1)
    wukv_r = np.ascontiguousarray(wukv.reshape(nA, 4, 128, 4096).transpose(0, 2, 1, 3))
    wo_r = np.ascontiguousarray(w_o.reshape(nA, 16, 128, cfg.DM).transpose(0, 2, 1, 3))
    return win_r, wuq_r, wukv_r, wo_r


S_IN = 4096
S_CONV = 6144
S_PROJ = 10304
S_H = 64
import os as _os
DBG3 = int(_os.environ.get('DBG3', '9'))
DBGX = int(_os.environ.get('DBGX', '99'))


def ssm_decl(nc, cfg, D, din, nB):
    L, NSEQ = cfg.L, cfg.NSEQ
    D['ssm_win'] = din('ssm_win_r', [nB, 128, cfg.KC, S_PROJ])
    D['ssm_wout'] = din('ssm_wout_r', [nB, 128, 32, cfg.DM])
    D['ssm_cw'] = din('ssm_conv_w', [nB, 4, S_CONV])
    D['ssm_cb'] = din('ssm_conv_b', [nB, 1, S_CONV])
    D['ssm_dtb'] = din('ssm_dt_bias', [nB, 1, S_H])
    D['ssm_alog'] = din('ssm_a_log', [nB, 1, S_H])
    D['ssm_d'] = din('ssm_d', [nB, 1, S_H])
    D['ssm_nw'] = din('ssm_norm', [nB, 1, S_IN])
    D['c_maskS'] = din('c_maskS', [128, 128])
    D['ZX'] = nc.dram_tensor('s_ZX', [NSEQ, L, S_PROJ], F32, kind="Internal").ap()
    D['XC'] = nc.dram_tensor('s_XC', [NSEQ, L, S_CONV], F32, kind="Internal").ap()
    D['YT'] = nc.dram_tensor('s_YT', [NSEQ, 128, 32, L], BF16, kind="Internal").ap()


def ssm_part1(C, H, j, i):
    nc, P, cfg, D = C.nc, C.P, C.cfg, C.D
    DM, KC = cfg.DM, cfg.KC
    BC = cfg.BT * 128
    with ExitStack() as es:
        def A(name, shape, dt):
            return es.enter_context(nc.sbuf_tensor(_uname(name), shape, dt))
        gbc = A("s1_gbc", [128, DM], F32)
        uT = A("s1_uT", [128, KC, BC], BF16)
        C.hbuf = [A("s1_hb0", [128, DM], F32), A("s1_hb1", [128, DM], F32)]
        C.hn = [A("s1_hn0", [128, DM], BF16), A("s1_hn1", [128, DM], BF16)]
        C.sq = A("s1_sq", [128, DM], BF16)
        wb = [A("s1_w%d" % k, [128, 8, 512], BF16) for k in range(3)]
        fr = [A("s1_fr%d" % k, [128, 512], F32) for k in range(4)]
        ctr = [0]
        _dma(P, 'sp', gbc[:, :], D['norm_mix'][i].partition_broadcast(128), [], ['gbc'])

        def epi(n, cc, te, psap, psres):
            s, t, r0, rows, col0 = te
            k = ctr[0] % 4
            ctr[0] += 1
            w = cc[1] - cc[0]
            _cp(P, 'act' if k % 2 == 0 else 'dve', fr[k][:rows, 0:w], psap, [psres], ['fr%d' % k])
            _dma(P, 'sp', D['ZX'][s, r0:r0 + rows, cc[0]:cc[1]], fr[k][:rows, 0:w], ['fr%d' % k], ['ZX_%d_%d' % (s, t)])
        for blk in cfg.blocks():
            for ii, (s, t, r0, rows, col0) in enumerate(blk):
                norm_T(C, H[s, r0:r0 + rows, :], hres(s, t), rows, gbc, uT, 'uT', col0, ii)
            gemm_tm(C, uT, 'uT', KC, blk, D['ssm_win'][j], S_PROJ, epi, wb, 'sw')
        P.flush()


def ssm_part2(C, j):
    nc, P, cfg, D = C.nc, C.P, C.cfg, C.D
    CW = 1024
    with ExitStack() as es:
        def A(name, shape, dt):
            return es.enter_context(nc.sbuf_tensor(_uname(name), shape, dt))
        wbc = A("s2_wbc", [128, 4, CW], F32)
        bbc = A("s2_bbc", [128, CW], F32)
        xk = [[A("s2_x%d_%d" % (b, k), [128, CW], F32) for k in range(4)] for b in range(2)]
        pk = [[A("s2_p%d_%d" % (b, k), [128, CW], F32) for k in range(4)] for b in range(2)]
        oc = [A("s2_o%d" % b, [128, CW], F32) for b in range(2)]
        it = 0
        for c0 in range(0, S_CONV, CW):
            c1 = c0 + CW
            for k in range(4):
                _dma(P, 'sp', wbc[:, k, :], D['ssm_cw'][j, k:k + 1, c0:c1].partition_broadcast(128), [], ['wbc'])
            _dma(P, 'sp', bbc[:, :], D['ssm_cb'][j, :, c0:c1].partition_broadcast(128), [], ['bbc'])
            for s in range(cfg.NSEQ):
                for t in range(cfg.NT):
                    r0, rows = cfg.tile(t)
                    b = it % 2
                    it += 1
                    zr = ['ZX_%d_%d' % (s, t)] + (['ZX_%d_%d' % (s, t - 1)] if t >= 2 else []) + (['ZX_%d_0' % s] if t == 1 else [])
                    for k in range(4):
                        sh = 3 - k
                        xr = 'x%d_%d' % (b, k)
                        if r0 - sh < 0:
                            _memset(P, 'pool', xk[b][k][:rows, :], 0.0, [xr])
                            _dma(P, 'sp', xk[b][k][sh:rows, :], D['ZX'][s, 0:rows - sh, S_IN + c0:S_IN + c1], zr, [xr])
                        else:
                            _dma(P, 'sp', xk[b][k][:rows, :], D['ZX'][s, r0 - sh:r0 - sh + rows, S_IN + c0:S_IN + c1], zr, [xr])
                        _tt(P, 'pool' if k == 1 else 'dve', pk[b][k][:rows, :], xk[b][k][:rows, :], wbc[:rows, k, :], ALU.mult, [xr, 'wbc'], ['p%d_%d' % (b, k)])
                    _tt(P, 'dve', pk[b][0][:rows, :], pk[b][0][:rows, :], pk[b][1][:rows, :], ALU.add, ['p%d_0' % b, 'p%d_1' % b], ['p%d_0' % b])
                    _tt(P, 'dve', pk[b][2][:rows, :], pk[b][2][:rows, :], pk[b][3][:rows, :], ALU.add, ['p%d_2' % b, 'p%d_3' % b], ['p%d_2' % b])
                    _tt(P, 'dve', pk[b][0][:rows, :], pk[b][0][:rows, :], pk[b][2][:rows, :], ALU.add, ['p%d_0' % b, 'p%d_2' % b], ['p%d_0' % b])
                    _tt(P, 'dve', pk[b][0][:rows, :], pk[b][0][:rows, :], bbc[:rows, :], ALU.add, ['p%d_0' % b, 'bbc'], ['p%d_0' % b])
                    _act(P, oc[b][:rows, :], pk[b][0][:rows, :], AF.Silu, ['p%d_0' % b], ['oc%d' % b])
                    _dma(P, 'act', D['XC'][s, r0:r0 + rows, c0:c1], oc[b][:rows, :], ['oc%d' % b], ['XC_%d_%d' % (s, t)])
        P.flush()


def ssm_part3(C, j):
    nc, P, cfg, D = C.nc, C.P, C.cfg, C.D
    with ExitStack() as es:
        def A(name, shape, dt):
            return es.enter_context(nc.sbuf_tensor(_uname(name), shape, dt))
        ST = A("s3_ST", [128, S_IN], F32)
        Sbf = A("s3_Sbf", [128, S_IN], BF16)
        nwbc = A("s3_nw", [128, S_IN], F32)
        xs_2 = [A("s3_xs0", [128, S_IN], F32), A("s3_xs1", [128, S_IN], F32)]
        zt_2 = [A("s3_zt0", [128, S_IN], F32), A("s3_zt1", [128, S_IN], F32)]
        bcf_2 = [A("s3_bcf0", [128, 2048], F32), A("s3_bcf1", [128, 2048], F32)]
        bcb = A("s3_bcb", [128, 2048], BF16)
        BCT = A("s3_BCT", [128, 16, 128], BF16)
        xdt = A("s3_xdt", [128, S_IN], BF16)
        xw = A("s3_xw", [128, S_IN], BF16)
        ysb = A("s3_ysb", [128, S_IN], F32)
        ynb = A("s3_ynb", [128, S_IN], BF16)
        YTt = A("s3_YTt", [128, 32, 128], BF16)
        E = [A("s3_E%d" % k, [128, 8 * 128], BF16) for k in range(2)]
        MT = [A("s3_MT%d" % k, [128, 8 * 128], BF16) for k in range(2)]
        rhsg = [A("s3_rg%d" % k, [128, 8 * 128], F32) for k in range(2)]
        CBT = [A("s3_CBT%d" % k, [128, 128], BF16) for k in range(2)]
        t1 = [A("s3_t1%d" % k, [128, 512], F32) for k in range(2)]
        mrep = A("s3_mrep", [128, 8 * 128], BF16)
        mrep16 = A("s3_mrep16", [128, 8 * 16], BF16)
        maskS = A("s3_maskS", [128, 128], F32)
        sm = {}
        for nm in ('dtb', 'abc', 'dbc', 'dtr', 'dt', 'dta', 'acs', 'nacs', 'expA', 'decayc', 'wq', 'tmpw', 'ssg', 'rsg', 'rstdg'):
            sm[nm] = A("s3_" + nm, [128, 64], F32)
        acsT = A("s3_acsT", [128, 128], F32)
        dtaP = A("s3_dtaP", [128, 128], F32)
        _dma(P, 'sp', nwbc[:, :], D['ssm_nw'][j].partition_broadcast(128), [], ['nwbc'])
        _dma(P, 'sp', sm['dtb'][:, :], D['ssm_dtb'][j].partition_broadcast(128), [], ['dtb'])
        _dma(P, 'sp', sm['abc'][:, :], D['ssm_alog'][j].partition_broadcast(128), [], ['abc'])
        _dma(P, 'sp', sm['dbc'][:, :], D['ssm_d'][j].partition_broadcast(128), [], ['dbc'])
        _dma(P, 'sp', maskS[:, :], D['c_maskS'], [], ['maskS'])
        _act(P, sm['abc'][:, :], sm['abc'][:, :], AF.Exp, ['abc'], ['abc'])
        _ts(P, 'dve', sm['abc'][:, :], sm['abc'][:, :], -1.0, None, ALU.mult, None, ['abc'], ['abc'])
        _cp(P, 'dve', mrep[:, :].rearrange("p (r l) -> p r l", l=128), maskS[:, :].unsqueeze(1).to_broadcast([128, 8, 128]), ['maskS'], ['mrep'])
        _memset(P, 'dve', mrep16[:, :], 0.0, ['mrep16'])
        _cp(P, 'dve', mrep16[:16, :].rearrange("p (r l) -> p r l", l=16), maskS[:16, :16].unsqueeze(1).to_broadcast([16, 8, 16]), ['maskS'], ['mrep16'])
        it = 0
        P.bankset = [0, 1]
        for s in range(cfg.NSEQ):
            _memset(P, 'pool', ST[:, :], 0.0, ['ST%d' % g for g in range(8)])
            _memset(P, 'pool', Sbf[:, :], 0.0, ['Sbf%d' % g for g in range(8)])
            _memset(P, 'pool', dtaP[:, :], 0.0, ['dta'])
            for t in range(cfg.NT):
                r0, rows = cfg.tile(t)
                R = rows
                mr = mrep if R == 128 else mrep16
                mrr = 'mrep' if R == 128 else 'mrep16'
                tp = (s * cfg.NT + t) % 2
                xs, zt, bcf = xs_2[tp], zt_2[tp], bcf_2[tp]
                XS, ZT, BCF = 'xs%d' % tp, 'zt%d' % tp, 'bcf%d' % tp
                xcr, zxr = 'XC_%d_%d' % (s, t), 'ZX_%d_%d' % (s, t)
                _dma(P, 'sp', xs[:R, :], D['XC'][s, r0:r0 + R, 0:S_IN], [xcr], [XS])
                _dma(P, 'sp', bcf[:R, :], D['XC'][s, r0:r0 + R, S_IN:S_CONV], [xcr], [BCF])
                _dma(P, 'sp', zt[:R, :], D['ZX'][s, r0:r0 + R, 0:S_IN], [zxr], [ZT])
                _dma(P, 'sp', sm['dtr'][:R, :], D['ZX'][s, r0:r0 + R, S_IN + S_CONV:S_PROJ], [zxr], ['dtr'])
                _tt(P, 'dve', sm['dt'][:R, :], sm['dtr'][:R, :], sm['dtb'][:R, :], ALU.add, ['dtr', 'dtb'], ['dt'])
                _act(P, sm['dt'][:R, :], sm['dt'][:R, :], AF.Exp, ['dt'], ['dt'])
                _act(P, sm['dt'][:R, :], sm['dt'][:R, :], AF.Ln, ['dt'], ['dt'], bias=1.0)
                _tt(P, 'dve', dtaP[:R, 0:64], sm['dt'][:R, :], sm['abc'][:R, :], ALU.mult, ['dt', 'abc'], ['dta'])
                bq = P.nextbank()
                pq, pqr = C.ps[bq], 'ps%d' % bq
                _stmts = []
                _stmts.append(lambda: _mm(P, pq[:, 0:64], C.tri[:, :], dtaP[:, 0:64], True, True, ['tri', 'dta'], [pqr]))
                _stmts.append(lambda: _mm(P, pq[:, 64:64 + R], dtaP[:, :], C.tri[:, :R], True, True, ['tri', 'dta'], [pqr]))
                _stmts.append(lambda: _mm(P, pq[:, 256:320], C.onesf[:, :], dtaP[:, 0:64], True, True, ['onesf', 'dta'], [pqr]))
                _stmts.append(lambda: _mm(P, pq[:, 480:496], C.idb[:, :], C.idb[:, 0:16], True, True, ['idb'], [pqr]))
                _stmts.append(lambda: _cp(P, 'act', sm['acs'][:R, :], pq[:R, 0:64], [pqr], ['acs']))
                _stmts.append(lambda: (_ts(P, 'dve', sm['nacs'][:R, :], pq[:R, 0:64], -1.0, None, ALU.mult, None, [pqr], ['nacs']) if _os.environ.get('NACS_PSUM') else _ts(P, 'dve', sm['nacs'][:R, :], sm['acs'][:R, :], -1.0, None, ALU.mult, None, ['acs'], ['nacs'])))
                _stmts.append(lambda: _cp(P, 'act', acsT[:, :R], pq[:, 64:64 + R], [pqr], ['acsT']))
                _stmts.append(lambda: _act(P, sm['expA'][:R, :], pq[:R, 0:64], AF.Exp, [pqr], ['expA']))
                _stmts.append(lambda: _act(P, sm['decayc'][:, :], pq[:, 256:320], AF.Exp, [pqr], ['decayc']))
                _stmts.append(lambda: _cp(P, 'act', sm['ssg'][:R, :], pq[:R, 256:320], [pqr], ['ssg']))
                _stmts.append(lambda: _tt(P, 'dve', sm['tmpw'][:R, :], sm['ssg'][:R, :], sm['acs'][:R, :], ALU.subtract, ['ssg', 'acs'], ['tmpw']))
                _stmts.append(lambda: _act(P, sm['tmpw'][:R, :], sm['tmpw'][:R, :], AF.Exp, ['tmpw'], ['tmpw']))
                _stmts.append(lambda: _tt(P, 'dve', sm['wq'][:R, :], sm['tmpw'][:R, :], sm['dt'][:R, :], ALU.mult, ['tmpw', 'dt'], ['wq']))
                for _f in _stmts[:DBGX]:
                    _f()
                xs3 = xs[:R, :].rearrange("p (h d) -> p h d", d=64)
                _tt(P, 'dve', xdt[:R, :].rearrange("p (h d) -> p h d", d=64), xs3,
                    sm['dt'][:R, :].unsqueeze(2).to_broadcast([R, 64, 64]), ALU.mult, [XS, 'dt'], ['xdt'])
                _tt(P, 'dve', xw[:R, :].rearrange("p (h d) -> p h d", d=64), xs3,
                    sm['wq'][:R, :].unsqueeze(2).to_broadcast([R, 64, 64]), ALU.mult, [XS, 'wq'], ['xw'])
                _cp(P, 'act', bcb[:R, :], bcf[:R, :], [BCF], ['bcb'])
                transpose_into(C, bcb, 'bcb', R, 16, BCT, 'BCT', 0)
                def stageA(g, k):
                    ks = str(k)
                    _mm(P, C.ps[4][:R, 0:R], BCT[:, g, :R], BCT[:, 8 + g, :R], True, True, ['BCT'], ['ps4'])
                    _cp(P, 'act', CBT[k][:R, :R], C.ps[4][:R, 0:R], ['ps4'], ['CBT' + ks])
                    rg3 = rhsg[k][:, 0:8 * R].rearrange("p (r l) -> p r l", l=R)
                    _tt(P, 'pool', rg3, acsT[:, :R].unsqueeze(1).to_broadcast([128, 8, R]),
                        C.idf[:, 8 * g:8 * g + 8].unsqueeze(2).to_broadcast([128, 8, R]), ALU.mult, ['acsT', 'idf'], ['rhsg' + ks])
                    for half in range(2):
                        bb = 2 + half
                        pb_, pbr = C.ps[bb], 'ps%d' % bb
                        _mm(P, pb_[:, 0:4 * R], C.onesf[:, :], rhsg[k][:, half * 4 * R:(half + 1) * 4 * R], True, False,
                            ['onesf', 'rhsg' + ks], [pbr])
                        _mm(P, pb_[:, 0:4 * R], C.idb[:, :], mr[:, 0:4 * R], False, True, ['idb', mrr], [pbr])
                        for rr in range(4):
                            r_ = half * 4 + rr
                            hh = 8 * g + r_
                            _act(P, E[k][:R, r_ * R:(r_ + 1) * R], pb_[:R, rr * R:(rr + 1) * R], AF.Exp, [pbr, 'nacs'], ['E' + ks],
                                 bias=sm['nacs'][:R, hh:hh + 1], scale=1.0)
                    _tt(P, 'dve', MT[k][:R, 0:8 * R].rearrange("p (r l) -> p r l", l=R),
                        E[k][:R, 0:8 * R].rearrange("p (r l) -> p r l", l=R),
                        CBT[k][:R, :R].unsqueeze(1).to_broadcast([R, 8, R]), ALU.mult, ['E' + ks, 'CBT' + ks], ['MT' + ks])

                def stageB(g, k):
                    ks = str(k)
                    by, bo, bs = 5, 6, 7
                    for r_ in range(8):
                        hh = 8 * g + r_
                        _mm(P, C.ps[by][:R, r_ * 64:(r_ + 1) * 64], MT[k][:R, r_ * R:(r_ + 1) * R], xdt[:R, hh * 64:(hh + 1) * 64],
                            True, True, ['MT' + ks, 'xdt'], ['ps%d' % by])
                    _mm(P, C.ps[bo][:R, 0:512], BCT[:, 8 + g, :R], Sbf[:, g * 512:(g + 1) * 512], True, True,
                        ['BCT', 'Sbf%d' % g], ['ps%d' % bo])
                    _tt(P, 'dve', t1[k][:R, :].rearrange("p (r d) -> p r d", d=64),
                        C.ps[bo][:R, 0:512].rearrange("p (r d) -> p r d", d=64),
                        sm['expA'][:R, 8 * g:8 * g + 8].unsqueeze(2).to_broadcast([R, 8, 64]), ALU.mult,
                        ['ps%d' % bo, 'expA'], ['t1' + ks])
                    _tt(P, 'dve', ysb[:R, g * 512:(g + 1) * 512], C.ps[by][:R, 0:512], t1[k][:R, :], ALU.add,
                        ['ps%d' % by, 't1' + ks], ['ysb%d' % g])
                    _mm(P, C.ps[bs][:, 0:512], bcb[:R, g * 128:(g + 1) * 128], xw[:R, g * 512:(g + 1) * 512], True, True,
                        ['bcb', 'xw'], ['ps%d' % bs])
                    stg = ST[:, g * 512:(g + 1) * 512]
                    _tt(P, 'pool', stg.rearrange("p (r d) -> p r d", d=64), stg.rearrange("p (r d) -> p r d", d=64),
                        sm['decayc'][:, 8 * g:8 * g + 8].unsqueeze(2).to_broadcast([128, 8, 64]), ALU.mult,
                        ['ST%d' % g, 'decayc'], ['ST%d' % g])
                    _tt(P, 'dve', stg, stg, C.ps[bs][:, 0:512], ALU.add, ['ST%d' % g, 'ps%d' % bs], ['ST%d' % g])
                    _cp(P, 'pool', Sbf[:, g * 512:(g + 1) * 512], stg, ['ST%d' % g], ['Sbf%d' % g])

                kk = [(it + g) % 2 for g in range(8)]
                it += 8
                stageA(0, kk[0])
                for g in range(1, 8):
                    stageA(g, kk[g])
                    stageB(g - 1, kk[g - 1])
                stageB(7, kk[7])
                YS = ['ysb%d' % g for g in range(8)]
                _tt(P, 'dve', xs3, xs3, sm['dbc'][:R, :].unsqueeze(2).to_broadcast([R, 64, 64]), ALU.mult, [XS, 'dbc'], [XS])
                _tt(P, 'dve', ysb[:R, :], ysb[:R, :], xs[:R, :], ALU.add, YS + [XS], YS)
                _act(P, zt[:R, :], zt[:R, :], AF.Silu, [ZT], [ZT])
                _tt(P, 'dve', ysb[:R, :], ysb[:R, :], zt[:R, :], ALU.mult, YS + [ZT], YS)
                for g in range(8):
                    _act(P, ynb[:R, g * 512:(g + 1) * 512], ysb[:R, g * 512:(g + 1) * 512], AF.Square, YS, ['ynb', 'ssg'],
                         accum_out=sm['ssg'][:R, g:g + 1])
                _act(P, sm['rsg'][:R, 0:8], sm['ssg'][:R, 0:8], AF.Sqrt, ['ssg'], ['rsg'], scale=1.0 / 512, bias=C.epsb[:R, :])
                _recip(P, sm['rstdg'][:R, 0:8], sm['rsg'][:R, 0:8], ['rsg'], ['rstdg'])
                ys3 = ysb[:R, :].rearrange("p (g c) -> p g c", c=512)
                _tt(P, 'dve', ys3, ys3, sm['rstdg'][:R, 0:8].unsqueeze(2).to_broadcast([R, 8, 512]), ALU.mult, YS + ['rstdg'], YS)
                _tt(P, 'dve', ynb[:R, :], ysb[:R, :], nwbc[:R, :], ALU.mult, YS + ['nwbc'], ['ynb'])
                transpose_into(C, ynb, 'ynb', R, 32, YTt, 'YTt', 0)
                _dma(P, 'sp', D['YT'][s, :, :, r0:r0 + R], YTt[:, :, :R], ['YTt'], ['YTd_%d' % s])
        P.bankset = None
        P.flush()


def ssm_part4(C, H, j):
    nc, P, cfg, D = C.nc, C.P, C.cfg, C.D
    BC = cfg.BT * 128
    with ExitStack() as es:
        def A(name, shape, dt):
            return es.enter_context(nc.sbuf_tensor(_uname(name), shape, dt))
        yT = [A("s4_yT%d" % k, [128, 32, BC], BF16) for k in range(2)]
        wb = [A("s4_w%d" % k, [128, 8, 512], BF16) for k in range(3)]
        C.hres = [A("s4_hr%d" % k, [128, 512], F32) for k in range(3)]
        epi = residual_epi(C, H, 1.0)
        for bi, blk in enumerate(cfg.blocks()):
            k = bi % 2
            for (s, t, r0, rows, col0) in blk:
                _dma(P, 'sp', yT[k][:, :, col0:col0 + rows], D['YT'][s, :, :, r0:r0 + rows], ['YTd_%d' % s], ['yT%d' % k])
            gemm_tm(C, yT[k], 'yT%d' % k, 32, blk, D['ssm_wout'][j], cfg.DM, epi, wb, 'ow')
        P.flush()


def ssm_phase(C, H, j, i):
    ssm_part1(C, H, j, i)
    ssm_part2(C, j)
    ssm_part3(C, j)
    ssm_part4(C, H, j)


def host_ssm_layout(cfg, w_in, w_out):
    nB = w_in.shape[0]
    win_r = np.ascontiguousarray(w_in.reshape(nB, cfg.KC, 128, S_PROJ).transpose(0, 2, 1, 3))
    wout_r = np.ascontiguousarray(w_out.reshape(nB, 32, 128, cfg.DM).transpose(0, 2, 1, 3))
    return win_r, wout_r


_NC_CACHE = {}


def kernel(x, meta_tokens, norm_ffn1, ffn1_w_in, ffn1_w_out, norm_mix, norm_ffn2, ffn2_w_in, ffn2_w_out,
           mla_w_in, mla_q_norm, mla_w_uq, mla_kv_norm, mla_w_ukv, mla_w_o,
           ssm_w_in, ssm_conv_w, ssm_conv_b, ssm_dt_bias, ssm_a_log, ssm_d, ssm_norm, ssm_w_out,
           final_norm):
    from concourse.bass_utils import run_bass_kernel_spmd
    f = lambda a: np.ascontiguousarray(np.asarray(a, dtype=np.float32))
    x = f(x)
    B, SEQ, DM = x.shape
    NCORES = 8
    NSEQ = B // NCORES
    DEPTH = norm_ffn1.shape[0]
    cfg = Cfg(SEQ=SEQ, NSEQ=NSEQ, DEPTH=DEPTH, DM=DM, DFF=ffn1_w_out.shape[1])
    key = (SEQ, NSEQ, DEPTH, DM, cfg.DFF)
    if key not in _NC_CACHE:
        _NC_CACHE[key] = build(cfg)
    nc = _NC_CACHE[key]
    shared = {}
    shared['meta_tokens'] = f(meta_tokens)
    shared['norm_ffn1'] = f(norm_ffn1).reshape(DEPTH, 1, DM)
    shared['norm_mix'] = f(norm_mix).reshape(DEPTH, 1, DM)
    shared['norm_ffn2'] = f(norm_ffn2).reshape(DEPTH, 1, DM)
    shared['final_norm'] = f(final_norm).reshape(1, DM)
    shared['ffn1_win_r'], shared['ffn1_wout_r'] = host_ffn_layout(cfg, f(ffn1_w_in), f(ffn1_w_out))
    shared['ffn2_win_r'], shared['ffn2_wout_r'] = host_ffn_layout(cfg, f(ffn2_w_in), f(ffn2_w_out))
    (shared['mla_win_r'], shared['mla_wuq_r'], shared['mla_wukv_r'], shared['mla_wo_r']) = host_mla_layout(
        cfg, f(mla_w_in), f(mla_w_uq), f(mla_w_ukv), f(mla_w_o))
    nA = mla_w_in.shape[0]
    shared['mla_q_norm'] = f(mla_q_norm).reshape(nA, 1, 512)
    shared['mla_kv_norm'] = f(mla_kv_norm).reshape(nA, 1, 512)
    shared['ssm_win_r'], shared['ssm_wout_r'] = host_ssm_layout(cfg, f(ssm_w_in), f(ssm_w_out))
    nB = ssm_w_in.shape[0]
    shared['ssm_conv_w'] = f(ssm_conv_w)
    shared['ssm_conv_b'] = f(ssm_conv_b).reshape(nB, 1, S_CONV)
    shared['ssm_dt_bias'] = f(ssm_dt_bias).reshape(nB, 1, S_H)
    shared['ssm_a_log'] = f(ssm_a_log).reshape(nB, 1, S_H)
    shared['ssm_d'] = f(ssm_d).reshape(nB, 1, S_H)
    shared['ssm_norm'] = f(ssm_norm).reshape(nB, 1, S_IN)
    shared.update(host_consts(cfg))
    in_maps = []
    for c in range(NCORES):
        m = dict(shared)
        m['x'] = x[c * NSEQ:(c + 1) * NSEQ]
        in_maps.append(m)
    res = run_bass_kernel_spmd(nc, in_maps, core_ids=list(range(NCORES)))
    out = np.concatenate([res.results[c]['out'] for c in range(NCORES)], axis=0)
    return out.astype(np.float32)
```

```python
import numpy as np
from contextlib import ExitStack
import concourse.bass as bass
import concourse.mybir as mybir

F32 = mybir.dt.float32
BF16 = mybir.dt.bfloat16
AF = mybir.ActivationFunctionType
ALU = mybir.AluOpType
AX = mybir.AxisListType

ENGS = ['pe', 'act', 'dve', 'pool', 'sp']
NS_DMA = 12


class Ins:
    __slots__ = ('fn', 'waits', 'signal', 'dma', 'dsem', 'dval', 'fn_was_real')

    def __init__(self, fn, dma=False):
        self.fn = fn
        self.fn_was_real = fn is not None
        self.waits = []
        self.signal = False
        self.dma = dma
        self.dsem = None
        self.dval = 0


class Prog:
    def __init__(self, nc):
        self.nc = nc
        self.q = {e: [] for e in ENGS}
        self.lastw = {}
        self.readers = {}
        self.seen = {e: {} for e in ENGS}
        self.ndma = {e: 0 for e in ENGS}
        self.dma_tok = {e: {} for e in ENGS}

    def _need(self, eng, ins, tok):
        if tok is None:
            return
        kind = tok[0]
        if kind == 'c':
            _, f, i = tok
            if f == eng and eng == 'pe':
                return
            key = ('c', f)
            if self.seen[eng].get(key, -1) >= i:
                return
            self.seen[eng][key] = i
            self.q[f][i].signal = True
            ins.waits.append(tok)
        else:
            _, qn, j = tok
            s = j % NS_DMA
            v = 16 * (j // NS_DMA + 1)
            key = ('d', qn, s)
            if self.seen[eng].get(key, 0) >= v:
                return
            self.seen[eng][key] = v
            ins.waits.append(tok)

    def _add(self, eng, fn, r, w, dma):
        ins = Ins(fn, dma)
        idx = len(self.q[eng])
        if dma:
            j = self.ndma[eng]
            self.ndma[eng] += 1
            tok = ('d', eng, j)
            if j >= NS_DMA:
                self._need(eng, ins, ('d', eng, j - NS_DMA))
        else:
            tok = ('c', eng, idx)
        for res in r:
            self._need(eng, ins, self.lastw.get(res))
        for res in w:
            self._need(eng, ins, self.lastw.get(res))
            rd = self.readers.get(res)
            if rd is not None:
                for f, i in rd[0].items():
                    if (not dma) and f == eng:
                        continue
                    self._need(eng, ins, ('c', f, i))
                for t in rd[1]:
                    self._need(eng, ins, t)
        for res in r:
            rd = self.readers.setdefault(res, ({}, []))
            if dma:
                rd[1].append(tok)
            else:
                rd[0][eng] = idx
        for res in w:
            self.lastw[res] = tok
            self.readers.pop(res, None)
        self.q[eng].append(ins)
        return tok

    def op(self, eng, fn, r=(), w=()):
        return self._add(eng, fn, r, w, False)

    def dma(self, eng, fn, r=(), w=()):
        return self._add(eng, fn, r, w, True)

    def _setup(self):
        nc = self.nc
        self.csem = {e: nc.alloc_semaphore('c_' + e) for e in ENGS}
        self.dsem = {e: [nc.alloc_semaphore('d_%s_%d' % (e, i)) for i in range(NS_DMA)]
                     for e in ('sp', 'pool', 'act')}
        self.emitted = {e: 0 for e in ENGS}
        self.cnt = {e: [] for e in ENGS}
        self.sig = {e: 0 for e in ENGS}
        self.jd = {e: 0 for e in ENGS}
        self.engobj = {'pe': nc.tensor, 'act': nc.scalar, 'dve': nc.vector, 'pool': nc.gpsimd, 'sp': nc.sync}

    def flush(self):
        nc = self.nc
        if not hasattr(self, 'csem'):
            self._setup()
        ctoks = []
        for f in ENGS:
            n = len(self.q[f])
            if n > self.emitted[f]:
                ctoks.append(('c', f, n - 1))
        dtoks = []
        for qn in ('sp', 'pool', 'act'):
            n = self.ndma[qn]
            for j in range(max(0, n - NS_DMA), n):
                dtoks.append(('d', qn, j))
        bars = {}
        for e in ENGS:
            ins = Ins(None)
            for t in ctoks:
                if t[1] != e:
                    self._need(e, ins, t)
            for t in dtoks:
                self._need(e, ins, t)
            bars[e] = ins
        for e in ENGS:
            self.q[e].append(bars[e])
        for e in ENGS:
            c = self.sig[e]
            for ins in self.q[e][self.emitted[e]:]:
                if ins.signal and not ins.dma and ins.fn is not None:
                    c += 1
                self.cnt[e].append(c)
            self.sig[e] = c

        def do_wait(eobj, tok):
            if tok[0] == 'c':
                _, f, i = tok
                eobj.wait_ge(self.csem[f], self.cnt[f][i])
            else:
                _, qn, j = tok
                eobj.wait_ge(self.dsem[qn][j % NS_DMA], 16 * (j // NS_DMA + 1))

        def run(ename):
            lo = self.emitted[ename]
            hi = len(self.q[ename])

            def body(eobj):
                for ins in self.q[ename][lo:hi]:
                    for tok in ins.waits:
                        do_wait(eobj, tok)
                    if ins.fn is None:
                        continue
                    bi = ins.fn(eobj)
                    if ins.dma:
                        bi.then_inc(self.dsem[ename][self.jd[ename] % NS_DMA], 16)
                        self.jd[ename] += 1
                    elif ins.signal:
                        bi.then_inc(self.csem[ename], 1)
            return body

        with nc.Block() as block:
            block.tensor(run('pe'))
            block.scalar(run('act'))
            block.vector(run('dve'))
            block.gpsimd(run('pool'))
            block.sync(run('sp'))
        for e in ENGS:
            self.emitted[e] = len(self.q[e])
            for ins in self.q[e]:
                ins.fn = None if ins.fn is None else ins.fn
        self.lastw_phase_clear()

    def lastw_phase_clear(self):
        self.lastw = {k: v for k, v in self.lastw.items() if v[0] == 'd'}
        self.readers = {k: ({}, v[1]) for k, v in self.readers.items() if v[1]}

    def nextbank(self):
        bs = getattr(self, 'bankset', None)
        if bs:
            i = getattr(self, '_bsi', 0)
            self._bsi = i + 1
            return bs[i % len(bs)]
        b = getattr(self, '_bank', 0)
        self._bank = (b + 1) % 8
        return b


def _mm(P, out, lhsT, rhs, start, stop, r, w):
    P.op('pe', lambda e: e.matmul(out, lhsT=lhsT, rhs=rhs, start=start, stop=stop), r, w)


def _tr(P, out, in_, ident, r, w):
    P.op('pe', lambda e: e.transpose(out, in_, ident), r, w)


def _act(P, out, in_, func, r, w, bias=None, scale=None, accum_out=None):
    kw = {}
    if bias is not None:
        kw['bias'] = bias
    if scale is not None:
        kw['scale'] = scale
    if accum_out is not None:
        kw['accum_out'] = accum_out
    P.op('act', lambda e: e.activation(out=out, in_=in_, func=func, **kw), r, w)


def _tt(P, eng, out, in0, in1, op, r, w):
    P.op(eng, lambda e: e.tensor_tensor(out=out, in0=in0, in1=in1, op=op), r, w)


def _ts(P, eng, out, in0, s1, s2, op0, op1, r, w):
    if s2 is None:
        P.op(eng, lambda e: e.tensor_scalar(out=out, in0=in0, scalar1=s1, scalar2=None, op0=op0), r, w)
    else:
        P.op(eng, lambda e: e.tensor_scalar(out=out, in0=in0, scalar1=s1, scalar2=s2, op0=op0, op1=op1), r, w)


def _stt(P, out, in0, scalar, in1, op0, op1, r, w):
    P.op('dve', lambda e: e.scalar_tensor_tensor(out=out, in0=in0, scalar=scalar, in1=in1, op0=op0, op1=op1), r, w)


def _cp(P, eng, out, in_, r, w):
    if eng == 'act':
        P.op('act', lambda e: e.activation(out=out, in_=in_, func=AF.Copy), r, w)
    else:
        P.op(eng, lambda e: e.tensor_copy(out=out, in_=in_), r, w)


def _red(P, out, in_, op, r, w):
    P.op('dve', lambda e: e.tensor_reduce(out=out, in_=in_, axis=AX.X, op=op), r, w)


def _recip(P, out, in_, r, w):
    P.op('dve', lambda e: e.reciprocal(out=out, in_=in_), r, w)


def _dma(P, eng, out, in_, r, w):
    P.dma(eng, lambda e: e.dma_start(out=out, in_=in_), r, w)


def _memset(P, eng, ap, val, w):
    P.op(eng, lambda e: e.memset(ap, val), (), w)


EPS = 1e-6
NEG = -30000.0
N_META = 16


class Cfg:
    def __init__(self, SEQ=2048, NSEQ=2, DEPTH=4, DM=2048, DFF=5504):
        self.SEQ, self.NSEQ, self.DEPTH, self.DM, self.DFF = SEQ, NSEQ, DEPTH, DM, DFF
        self.L = SEQ + N_META
        self.NT = 1 + SEQ // 128
        self.KC = DM // 128
        self.NF = DFF // 128
        self.BT = 7

    def tile(self, t):
        if t == 0:
            return 0, N_META
        return N_META + (t - 1) * 128, 128

    def blocks(self, seqs=None):
        tl = []
        for s in (range(self.NSEQ) if seqs is None else seqs):
            for t in range(1, self.NT):
                tl.append((s, t))
            tl.append((s, 0))
        out = []
        for b0 in range(0, len(tl), self.BT):
            blk = []
            c = 0
            for (s, t) in tl[b0:b0 + self.BT]:
                r0, rows = self.tile(t)
                blk.append((s, t, r0, rows, c))
                c += rows
            out.append(blk)
        return out


def col_groups(ncols, w=512):
    return [(c, min(c + w, ncols)) for c in range(0, ncols, w)]


class Ctx:
    pass


_UID = [0]


def _uname(name):
    _UID[0] += 1
    return '%s_u%d' % (name, _UID[0])


def hres(s, t):
    return 'H_%d_%d' % (s, t)


def norm_T(C, src_ap, src_res, rows, gbc, AT, at_res, col0, i):
    P, cfg = C.P, C.cfg
    DM, KC = cfg.DM, cfg.KC
    k = i % 2
    hb, hn = C.hbuf[k], C.hn[k]
    hbr, hnr = 'hbuf%d' % k, 'hn%d' % k
    _dma(P, 'sp', hb[:rows, :], src_ap, [src_res], [hbr])
    _act(P, C.sq[:rows, :], hb[:rows, :], AF.Square, [hbr], ['sq', 'ss%d' % k], accum_out=C.ss[:rows, k:k + 1])
    _act(P, C.rs[:rows, k:k + 1], C.ss[:rows, k:k + 1], AF.Sqrt, ['ss%d' % k], ['rs%d' % k], scale=1.0 / DM, bias=C.epsb[:rows, :])
    _recip(P, C.rstd[:rows, k:k + 1], C.rs[:rows, k:k + 1], ['rs%d' % k], ['rstd%d' % k])
    _stt(P, hn[:rows, :], hb[:rows, :], C.rstd[:rows, k:k + 1], gbc[:rows, :], ALU.mult, ALU.mult,
         [hbr, 'rstd%d' % k, 'gbc'], [hnr])
    transpose_into(C, hn, hnr, rows, KC, AT, at_res, col0)


def transpose_into(C, src, src_res, rows, nch, AT, at_res, col0, chw=128):
    P = C.P
    for g4 in range(0, nch, 4):
        n4 = min(4, nch - g4)
        b = P.nextbank()
        pb = C.psb[b]
        for j in range(n4):
            ch = g4 + j
            _tr(P, pb[:chw, j * 128:j * 128 + rows], src[:rows, ch * chw:(ch + 1) * chw], C.idb[:rows, :rows],
                [src_res, 'idb'], ['ps%d' % b])
        src_v = pb[:chw, 0:n4 * 128].rearrange("p (j c) -> p j c", c=128)[:, :, :rows]
        eng = 'act' if (C.evac_ctr % 2 == 0) else 'dve'
        C.evac_ctr += 1
        _cp(P, eng, AT[:chw, g4:g4 + n4, col0:col0 + rows], src_v, ['ps%d' % b], [at_res])


def gemm_tm(C, AT, at_res, KC, blk, w_r, N, epi, wbufs, wres, kstep=8, gw=512):
    P = C.P
    for (c0, c1) in col_groups(N, gw):
        n = c0 // gw
        w = c1 - c0
        banks = [P.nextbank() for _ in blk]
        for k0 in range(0, KC, kstep):
            k1 = min(KC, k0 + kstep)
            wb = C.wctr % len(wbufs)
            C.wctr += 1
            wt, wr_ = wbufs[wb], '%s%d' % (wres, wb)
            _dma(P, 'pool', wt[:, 0:k1 - k0, 0:w], w_r[:, k0:k1, c0:c1], [], [wr_])
            for kc in range(k0, k1):
                for ti, (s, t, r0, rows, col0) in enumerate(blk):
                    _mm(P, C.ps[banks[ti]][:rows, 0:w], AT[:, kc, col0:col0 + rows], wt[:, kc - k0, 0:w],
                        kc == 0, kc == KC - 1, [at_res, wr_], ['ps%d' % banks[ti]])
        for ti, te in enumerate(blk):
            epi(n, (c0, c1), te, C.ps[banks[ti]][:te[3], 0:w], 'ps%d' % banks[ti])


def residual_epi(C, H, coef):
    P = C.P

    def epi(n, cc, te, psap, psres):
        s, t, r0, rows, col0 = te
        k = C.hres_ctr % len(C.hres)
        C.hres_ctr += 1
        hr, hrr = C.hres[k], 'hres%d' % k
        w = cc[1] - cc[0]
        _dma(P, 'sp', hr[:rows, 0:w], H[s, r0:r0 + rows, cc[0]:cc[1]], [hres(s, t)], [hrr])
        _stt(P, hr[:rows, 0:w], psap, coef, hr[:rows, 0:w], ALU.mult, ALU.add, [psres, hrr], [hrr])
        _dma(P, 'act', H[s, r0:r0 + rows, cc[0]:cc[1]], hr[:rows, 0:w], [hrr], [hres(s, t)])
    return epi


def ffn_phase(C, H, gain_d, win_r, wout_r):
    nc, P, cfg = C.nc, C.P, C.cfg
    DM, KC, NF = cfg.DM, cfg.KC, cfg.NF
    BC = cfg.BT * 128
    with ExitStack() as es:
        def A(name, shape, dt):
            return es.enter_context(nc.sbuf_tensor(_uname(name), shape, dt))
        gbc = A("f_gbc", [128, DM], F32)
        xnT = A("f_xnT", [128, KC, BC], BF16)
        actT = A("f_actT", [128, NF, BC], BF16)
        hb0 = A("f_hb0", [128, DM], F32)
        hb1 = A("f_hb1", [128, DM], F32)
        hn0 = A("f_hn0", [128, DM], BF16)
        hn1 = A("f_hn1", [128, DM], BF16)
        sq = A("f_sq", [128, DM], BF16)
        win0 = A("f_win0", [128, KC, 256], BF16)
        win1 = A("f_win1", [128, KC, 256], BF16)
        wo0 = A("f_wo0", [128, 8, 512], BF16)
        wo1 = A("f_wo1", [128, 8, 512], BF16)
        wo2 = A("f_wo2", [128, 8, 512], BF16)
        sg0 = A("f_sg0", [128, 512], F32)
        sg1 = A("f_sg1", [128, 512], F32)
        hr0 = A("f_hr0", [128, 512], F32)
        hr1 = A("f_hr1", [128, 512], F32)
        hr2 = A("f_hr2", [128, 512], F32)
        C.hbuf, C.hn, C.sq = [hb0, hb1], [hn0, hn1], sq
        C.hres = [hr0, hr1, hr2]
        wins = [win0, win1]
        sgs = [sg0, sg1]
        _dma(P, 'sp', gbc[:, :], gain_d.partition_broadcast(128), [], ['gbc'])
        epi = residual_epi(C, H, 0.5)
        for blk in cfg.blocks():
            ncols = sum(te[3] for te in blk)
            for i, (s, t, r0, rows, col0) in enumerate(blk):
                norm_T(C, H[s, r0:r0 + rows, :], hres(s, t), rows, gbc, xnT, 'xnT', col0, i)
            groups = col_groups(ncols)
            for f in range(NF):
                wb = f % 2
                wt, wr_ = wins[wb], 'win%d' % wb
                _dma(P, 'pool', wt[:, :, :], win_r[f], [], [wr_])
                for (c0, c1) in groups:
                    w = c1 - c0
                    bg, bu = P.nextbank(), P.nextbank()
                    for kc in range(KC):
                        _mm(P, C.ps[bg][:, 0:w], wt[:, kc, 0:128], xnT[:, kc, c0:c1], kc == 0, kc == KC - 1,
                            [wr_, 'xnT'], ['ps%d' % bg])
                    for kc in range(KC):
                        _mm(P, C.ps[bu][:, 0:w], wt[:, kc, 128:256], xnT[:, kc, c0:c1], kc == 0, kc == KC - 1,
                            [wr_, 'xnT'], ['ps%d' % bu])
                    k = C.sg_ctr % 2
                    C.sg_ctr += 1
                    _act(P, sgs[k][:, 0:w], C.ps[bg][:, 0:w], AF.Silu, ['ps%d' % bg], ['sg%d' % k])
                    _tt(P, 'dve', actT[:, f, c0:c1], sgs[k][:, 0:w], C.ps[bu][:, 0:w], ALU.mult,
                        ['sg%d' % k, 'ps%d' % bu], ['actT'])
            gemm_tm(C, actT, 'actT', NF, blk, wout_r, DM, epi, [wo0, wo1, wo2], 'wo')
        P.flush()


def out_phase(C, H, gain_d, out_d, raw=False):
    nc, P, cfg = C.nc, C.P, C.cfg
    DM = cfg.DM
    with ExitStack() as es:
        def A(name, shape, dt):
            return es.enter_context(nc.sbuf_tensor(_uname(name), shape, dt))
        gbc = A("o_gbc", [128, DM], F32)
        hb = [A("o_hb0", [128, DM], F32), A("o_hb1", [128, DM], F32)]
        ob = [A("o_ob0", [128, DM], F32), A("o_ob1", [128, DM], F32)]
        sq = A("o_sq", [128, DM], BF16)
        _dma(P, 'sp', gbc[:, :], gain_d.partition_broadcast(128), [], ['gbc'])
        i = 0
        for s in range(cfg.NSEQ):
            for t in range(1, cfg.NT):
                r0, rows = cfg.tile(t)
                k = i % 2
                i += 1
                hbr, obr = 'ohb%d' % k, 'oob%d' % k
                _dma(P, 'sp', hb[k][:rows, :], H[s, r0:r0 + rows, :], [hres(s, t)], [hbr])
                if raw:
                    _dma(P, 'sp', out_d[s, r0 - N_META:r0 - N_META + rows, :], hb[k][:rows, :], [hbr], ['out_%d_%d' % (s, t)])
                    continue
                _act(P, sq[:rows, :], hb[k][:rows, :], AF.Square, [hbr], ['sq', 'ss%d' % k], accum_out=C.ss[:rows, k:k + 1])
                _act(P, C.rs[:rows, k:k + 1], C.ss[:rows, k:k + 1], AF.Sqrt, ['ss%d' % k], ['rs%d' % k], scale=1.0 / DM, bias=C.epsb[:rows, :])
                _recip(P, C.rstd[:rows, k:k + 1], C.rs[:rows, k:k + 1], ['rs%d' % k], ['rstd%d' % k])
                _stt(P, ob[k][:rows, :], hb[k][:rows, :], C.rstd[:rows, k:k + 1], gbc[:rows, :], ALU.mult, ALU.mult,
                     [hbr, 'rstd%d' % k, 'gbc'], [obr])
                _dma(P, 'sp', out_d[s, r0 - N_META:r0 - N_META + rows, :], ob[k][:rows, :], [obr], ['out_%d_%d' % (s, t)])
        P.flush()


def build(cfg, plan=None):
    nc = bass.Bass("TRN2", target_bir_lowering=False)
    DM, DFF, L, NSEQ, SEQ, DEPTH = cfg.DM, cfg.DFF, cfg.L, cfg.NSEQ, cfg.SEQ, cfg.DEPTH
    KC, NF = cfg.KC, cfg.NF
    nA, nB = (DEPTH + 1) // 2, DEPTH // 2

    def din(name, shape, dt=F32):
        return nc.dram_tensor(name, list(shape), dt, kind="ExternalInput").ap()
    D = {}
    D['x'] = din('x', [NSEQ, SEQ, DM])
    D['meta'] = din('meta_tokens', [N_META, DM])
    for nm in ('norm_ffn1', 'norm_mix', 'norm_ffn2'):
        D[nm] = din(nm, [DEPTH, 1, DM])
    D['final_norm'] = din('final_norm', [1, DM])
    for nm in ('ffn1', 'ffn2'):
        D[nm + '_win'] = din(nm + '_win_r', [DEPTH, NF, 128, KC, 256])
        D[nm + '_wout'] = din(nm + '_wout_r', [DEPTH, 128, NF, DM])
    D['c_ident'] = din('c_ident', [128, 128])
    D['c_tri'] = din('c_tri', [128, 128])
    D['c_maskA'] = din('c_maskA', [128, 128])
    D['c_cos'] = din('c_cos', [L, 32])
    D['c_sin'] = din('c_sin', [L, 32])
    mixer_decl(nc, cfg, D, din, nA, nB)
    out_d = nc.dram_tensor('out', [NSEQ, SEQ, DM], F32, kind="ExternalOutput").ap()
    H = nc.dram_tensor('Hres', [NSEQ, L, DM], F32, kind="Internal").ap()

    C = Ctx()
    C.nc, C.cfg, C.D = nc, cfg, D
    C.P = P = Prog(nc)
    C.evac_ctr = C.wctr = C.hres_ctr = C.sg_ctr = 0
    C.ps = [nc.alloc_psum_tensor('psb%d' % i, [128, 512], F32) for i in range(8)]
    C.psb = [p[:, :].bitcast(BF16) for p in C.ps]
    C.ps = [p[:, :] for p in C.ps]
    C.idb = nc.alloc_sbuf_tensor('c_idb', [128, 128], BF16)
    C.idf = nc.alloc_sbuf_tensor('c_idf', [128, 128], F32)
    C.tri = nc.alloc_sbuf_tensor('c_trif', [128, 128], F32)
    C.maskA = nc.alloc_sbuf_tensor('c_maskAf', [128, 128], F32)
    C.onesf = nc.alloc_sbuf_tensor('c_onesf', [128, 128], F32)
    C.epsb = nc.alloc_sbuf_tensor('c_epsb', [128, 1], F32)
    C.ss = nc.alloc_sbuf_tensor('c_ss', [128, 2], F32)
    C.rs = nc.alloc_sbuf_tensor('c_rs', [128, 2], F32)
    C.rstd = nc.alloc_sbuf_tensor('c_rstd', [128, 2], F32)
    _dma(P, 'pool', C.idb[:, :], D['c_ident'], [], ['idb'])
    _dma(P, 'sp', C.idf[:, :], D['c_ident'], [], ['idf'])
    _dma(P, 'sp', C.tri[:, :], D['c_tri'], [], ['tri'])
    _dma(P, 'sp', C.maskA[:, :], D['c_maskA'], [], ['maskA'])
    _memset(P, 'dve', C.onesf[:, :], 1.0, ['onesf'])
    _memset(P, 'dve', C.epsb[:, :], EPS, ['epsb'])
    for s in range(NSEQ):
        _dma(P, 'sp', H[s, 0:N_META, :], D['meta'], [], [hres(s, 0)])
        for t in range(1, cfg.NT):
            r0, rows = cfg.tile(t)
            _dma(P, 'sp', H[s, r0:r0 + rows, :], D['x'][s, r0 - N_META:r0 - N_META + rows, :], [], [hres(s, t)])
    P.flush()
    if plan is None:
        plan = []
        for i in range(DEPTH):
            plan += [('ffn1', i), ('mla' if i % 2 == 0 else 'ssm', i // 2, i), ('ffn2', i)]
        plan += [('out',)]
    for st in plan:
        if st[0] in ('ffn1', 'ffn2'):
            i = st[1]
            ffn_phase(C, H, D['norm_' + st[0]][i], D[st[0] + '_win'][i], D[st[0] + '_wout'][i])
        elif st[0] == 'mla':
            mla_phase(C, H, st[1], st[2])
        elif st[0] == 'ssm':
            ssm_phase(C, H, st[1], st[2])
        elif st[0] == 'ssm1':
            ssm_part1(C, H, st[1], st[2])
        elif st[0] == 'ssm2':
            ssm_part2(C, st[1])
        elif st[0] == 'ssm3':
            ssm_part3(C, st[1])
        elif st[0] == 'ssm4':
            ssm_part4(C, H, st[1])
        elif st[0] == 'out':
            out_phase(C, H, D['final_norm'], out_d)
        elif st[0] == 'raw':
            out_phase(C, H, D['final_norm'], out_d, raw=True)
    return nc


def mixer_decl(nc, cfg, D, din, nA, nB):
    mla_decl(nc, cfg, D, din, nA)
    ssm_decl(nc, cfg, D, din, nB)


def host_consts(cfg):
    L = cfg.L
    ident = np.eye(128, dtype=np.float32)
    k = np.arange(128)
    tri = (k[:, None] <= k[None, :]).astype(np.float32)
    maskA = np.where(k[None, :] <= k[:, None], 0.0, NEG).astype(np.float32)
    inv_freq = (1.0 / (10000.0 ** (np.arange(0, 64, 2, dtype=np.float32) / np.float32(64)))).astype(np.float32)
    ang = np.arange(L, dtype=np.float32)[:, None] * inv_freq[None, :]
    return dict(c_ident=ident, c_tri=tri, c_maskA=maskA, c_maskS=np.ascontiguousarray(maskA.T),
                c_cos=np.cos(ang).astype(np.float32), c_sin=np.sin(ang).astype(np.float32))


def host_ffn_layout(cfg, w_in, w_out):
    Dp = w_in.shape[0]
    KC, NF, DFF, DM = cfg.KC, cfg.NF, cfg.DFF, cfg.DM
    g = w_in[:, :, :DFF].reshape(Dp, KC, 128, NF, 128)
    u = w_in[:, :, DFF:].reshape(Dp, KC, 128, NF, 128)
    gu = np.concatenate([g, u], axis=-1)
    win_r = np.ascontiguousarray(gu.transpose(0, 3, 2, 1, 4))
    wout_r = np.ascontiguousarray(w_out.reshape(Dp, NF, 128, DM).transpose(0, 2, 1, 3))
    return win_r, wout_r


MLA_H = 16
ATT_SCALE = float(192 ** -0.5)


def rope(C, src, cs, sn, dst, rows, nh, rres, wres):
    P = C.P
    a, b = C.ropeA[:rows, 0:nh, :], C.ropeB[:rows, 0:nh, :]
    csb = cs[:rows, :].unsqueeze(1).to_broadcast([rows, nh, 32])
    snb = sn[:rows, :].unsqueeze(1).to_broadcast([rows, nh, 32])
    t1, t2 = src[:, :, 0:32], src[:, :, 32:64]
    _tt(P, 'dve', a, t1, csb, ALU.mult, rres, ['ropeA'])
    _tt(P, 'dve', b, t2, snb, ALU.mult, rres, ['ropeB'])
    _tt(P, 'dve', dst[:, :, 0:32], a, b, ALU.subtract, ['ropeA', 'ropeB'], wres)
    _tt(P, 'dve', a, t2, csb, ALU.mult, rres, ['ropeA'])
    _tt(P, 'dve', b, t1, snb, ALU.mult, rres, ['ropeB'])
    _tt(P, 'dve', dst[:, :, 32:64], a, b, ALU.add, ['ropeA', 'ropeB'], wres)


def mla_decl(nc, cfg, D, din, nA):
    L, NSEQ = cfg.L, cfg.NSEQ
    D['mla_win'] = din('mla_win_r', [nA, 128, cfg.KC, 1088])
    D['mla_wuq'] = din('mla_wuq_r', [nA, 128, 4, 3072])
    D['mla_wukv'] = din('mla_wukv_r', [nA, 128, 4, 4096])
    D['mla_wo'] = din('mla_wo_r', [nA, 128, 16, cfg.DM])
    D['mla_qn'] = din('mla_q_norm', [nA, 1, 512])
    D['mla_kvn'] = din('mla_kv_norm', [nA, 1, 512])

    def scr(name, shape):
        return nc.dram_tensor(name, shape, BF16, kind="Internal").ap()
    D['QN'] = scr('s_QN', [NSEQ, L, 2048])
    D['QR'] = scr('s_QR', [NSEQ, L, 1024])
    D['KN'] = scr('s_KN', [NSEQ, L, 2048])
    D['V'] = scr('s_V', [NSEQ, L, 2048])
    D['KR'] = scr('s_KR', [NSEQ, L, 64])
    D['OT'] = scr('s_OT', [NSEQ, 128, 16, L])


def mla_part1(C, H, j, i):
    nc, P, cfg, D = C.nc, C.P, C.cfg, C.D
    DM, KC = cfg.DM, cfg.KC
    BC = cfg.BT * 128
    with ExitStack() as es:
        def A(name, shape, dt):
            return es.enter_context(nc.sbuf_tensor(_uname(name), shape, dt))
        gbc = A("m_gbc", [128, DM], F32)
        uT = A("m_uT", [128, KC, BC], BF16)
        C.hbuf = [A("m_hb0", [128, DM], F32), A("m_hb1", [128, DM], F32)]
        C.hn = [A("m_hn0", [128, DM], BF16), A("m_hn1", [128, DM], BF16)]
        C.sq = A("m_sq", [128, DM], BF16)
        cqT = A("m_cqT", [128, 4, BC], BF16)
        ckvT = A("m_ckvT", [128, 4, BC], BF16)
        wb = [A("m_w%d" % k, [128, 8, 512], BF16) for k in range(3)]
        gq = A("m_gq", [128, 512], F32)
        gkv = A("m_gkv", [128, 512], F32)
        cn = [A("m_cn%d" % k, [128, 512], BF16) for k in range(2)]
        st = [A("m_st%d" % k, [128, 512], BF16) for k in range(3)]
        fr = [A("m_fr%d" % k, [128, 512], F32) for k in range(2)]
        cs = [A("m_cs%d" % k, [128, 32], F32) for k in range(2)]
        sn = [A("m_sn%d" % k, [128, 32], F32) for k in range(2)]
        C.ropeA = A("m_ropeA", [128, 8, 32], F32)
        C.ropeB = A("m_ropeB", [128, 8, 32], F32)
        ctr = {'cn': 0, 'st': 0, 'fr': 0, 'cs': 0}
        _dma(P, 'sp', gbc[:, :], D['norm_mix'][i].partition_broadcast(128), [], ['gbc'])
        _dma(P, 'sp', gq[:, :], D['mla_qn'][j].partition_broadcast(128), [], ['gq'])
        _dma(P, 'sp', gkv[:, :], D['mla_kvn'][j].partition_broadcast(128), [], ['gkv'])

        def load_cs(r0, rows):
            k = ctr['cs'] % 2
            ctr['cs'] += 1
            _dma(P, 'sp', cs[k][:rows, :], D['c_cos'][r0:r0 + rows, :], [], ['cs%d' % k])
            _dma(P, 'sp', sn[k][:rows, :], D['c_sin'][r0:r0 + rows, :], [], ['sn%d' % k])
            return k

        def store_bf(psap, psres, rows, w, dst_ap, dst_res):
            k = ctr['st'] % 3
            ctr['st'] += 1
            eng = 'act' if k % 2 == 0 else 'dve'
            _cp(P, eng, st[k][:rows, 0:w], psap, [psres], ['st%d' % k])
            _dma(P, 'sp', dst_ap, st[k][:rows, 0:w], ['st%d' % k], [dst_res])

        def rope_store(psap, psres, te, nh, dst_ap, dst_res):
            s, t, r0, rows, col0 = te
            kf = ctr['fr'] % 2
            ctr['fr'] += 1
            _cp(P, 'act', fr[kf][:rows, 0:nh * 64], psap, [psres], ['fr%d' % kf])
            kc_ = load_cs(r0, rows)
            k = ctr['st'] % 3
            ctr['st'] += 1
            src = fr[kf][:rows, 0:nh * 64].rearrange("p (h d) -> p h d", d=64)
            dst = st[k][:rows, 0:nh * 64].rearrange("p (h d) -> p h d", d=64)
            rope(C, src, cs[kc_], sn[kc_], dst, rows, nh, ['fr%d' % kf, 'cs%d' % kc_, 'sn%d' % kc_], ['st%d' % k])
            _dma(P, 'sp', dst_ap, st[k][:rows, 0:nh * 64], ['st%d' % k], [dst_res])

        for blk in cfg.blocks():
            for ii, (s, t, r0, rows, col0) in enumerate(blk):
                norm_T(C, H[s, r0:r0 + rows, :], hres(s, t), rows, gbc, uT, 'uT', col0, ii)

            def epi_c(n, cc, te, psap, psres):
                s, t, r0, rows, col0 = te
                if n < 2:
                    k = ctr['cn'] % 2
                    ctr['cn'] += 1
                    _act(P, C.sq[:rows, 0:512], psap, AF.Square, [psres], ['sq', 'ss%d' % k], accum_out=C.ss[:rows, k:k + 1])
                    _act(P, C.rs[:rows, k:k + 1], C.ss[:rows, k:k + 1], AF.Sqrt, ['ss%d' % k], ['rs%d' % k], scale=1.0 / 512, bias=C.epsb[:rows, :])
                    _recip(P, C.rstd[:rows, k:k + 1], C.rs[:rows, k:k + 1], ['rs%d' % k], ['rstd%d' % k])
                    g_, gr_ = (gq, 'gq') if n == 0 else (gkv, 'gkv')
                    _stt(P, cn[k][:rows, :], psap, C.rstd[:rows, k:k + 1], g_[:rows, :], ALU.mult, ALU.mult,
                         [psres, 'rstd%d' % k, gr_], ['cn%d' % k])
                    if n == 0:
                        transpose_into(C, cn[k], 'cn%d' % k, rows, 4, cqT, 'cqT', col0)
                    else:
                        transpose_into(C, cn[k], 'cn%d' % k, rows, 4, ckvT, 'ckvT', col0)
                else:
                    rope_store(psap, psres, te, 1, D['KR'][s, r0:r0 + rows, :], 'KR_%d_%d' % (s, t))
            gemm_tm(C, uT, 'uT', KC, blk, D['mla_win'][j], 1088, epi_c, wb, 'mw')

            def epi_q(n, cc, te, psap, psres):
                s, t, r0, rows, col0 = te
                if n < 4:
                    store_bf(psap, psres, rows, 512, D['QN'][s, r0:r0 + rows, cc[0]:cc[1]], 'QN_%d_%d' % (s, t))
                else:
                    rope_store(psap, psres, te, 8, D['QR'][s, r0:r0 + rows, cc[0] - 2048:cc[1] - 2048], 'QR_%d_%d' % (s, t))
            gemm_tm(C, cqT, 'cqT', 4, blk, D['mla_wuq'][j], 3072, epi_q, wb, 'mw', kstep=4)

            def epi_kv(n, cc, te, psap, psres):
                s, t, r0, rows, col0 = te
                if n < 4:
                    store_bf(psap, psres, rows, 512, D['KN'][s, r0:r0 + rows, cc[0]:cc[1]], 'KN_%d_%d' % (s, t))
                else:
                    store_bf(psap, psres, rows, 512, D['V'][s, r0:r0 + rows, cc[0] - 2048:cc[1] - 2048], 'V_%d_%d' % (s, t))
            gemm_tm(C, ckvT, 'ckvT', 4, blk, D['mla_wukv'][j], 4096, epi_kv, wb, 'mw', kstep=4)
        P.flush()


def load_tok(C, dst, dst_res, src, s, c0, c1, rres_fn):
    P, cfg = C.P, C.cfg
    rr = [rres_fn(s, t) for t in range(cfg.NT)]
    _dma(P, 'sp', dst[:N_META, 0, :], src[s, 0:N_META, c0:c1], rr, [dst_res])
    _dma(P, 'sp', dst[:, 1:cfg.NT, :], src[s, N_META:, c0:c1].rearrange("(t p) d -> p t d", p=128), rr, [dst_res])


def transpose_tok(C, tok, tok_res, nd, dstT, dst_res):
    P, cfg = C.P, C.cfg
    groups = [[0]] + [list(range(t, min(t + 4, cfg.NT))) for t in range(1, cfg.NT, 4)]
    for g in groups:
        b = P.nextbank()
        pb = C.psb[b]
        for jj, t in enumerate(g):
            r0, rows = cfg.tile(t)
            _tr(P, pb[:nd, jj * 128:jj * 128 + rows], tok[:rows, t, :], C.idb[:rows, :rows], [tok_res, 'idb'], ['ps%d' % b])
        r0, rows = cfg.tile(g[0])
        ncol = sum(cfg.tile(t)[1] for t in g)
        eng = 'act' if (C.evac_ctr % 2 == 0) else 'dve'
        C.evac_ctr += 1
        _cp(P, eng, dstT[:nd, r0:r0 + ncol], pb[:nd, 0:ncol], ['ps%d' % b], [dst_res])


def mla_part2(C, j):
    nc, P, cfg, D = C.nc, C.P, C.cfg, C.D
    L, NT = cfg.L, cfg.NT
    bounds = [0] + [cfg.tile(t)[0] for t in range(4, NT, 4)] + [L]
    with ExitStack() as es:
        def A(name, shape, dt):
            return es.enter_context(nc.sbuf_tensor(_uname(name), shape, dt))
        krtok = A("a_krtok", [128, NT, 64], BF16)
        KRT = A("a_KRT", [64, L], BF16)
        S2 = []
        for k in range(2):
            S2.append(dict(
                ktok=A("a_ktok%d" % k, [128, NT, 128], BF16), vtok=A("a_vtok%d" % k, [128, NT, 128], BF16),
                qntok=A("a_qntok%d" % k, [128, NT, 128], BF16), qrtok=A("a_qrtok%d" % k, [128, NT, 64], BF16),
                KT=A("a_KT%d" % k, [128, L], BF16), QNT=A("a_QNT%d" % k, [128, L], BF16),
                QRT=A("a_QRT%d" % k, [64, L], BF16), OT=A("a_OT%d" % k, [128, L], BF16)))
        Pb = [A("a_Pb%d" % k, [128, L], BF16) for k in range(2)]
        PT = [A("a_PT%d" % k, [128, NT, 128], BF16) for k in range(2)]
        mxs = [A("a_mxs%d" % k, [128, 8], F32) for k in range(2)]
        mx = [A("a_mx%d" % k, [128, 1], F32) for k in range(2)]
        nb = [A("a_nb%d" % k, [128, 1], F32) for k in range(2)]
        sums = [A("a_sums%d" % k, [128, 8], F32) for k in range(2)]
        rsum = [A("a_rsum%d" % k, [128, 1], F32) for k in range(2)]
        rinv = [A("a_rinv%d" % k, [128, 1], F32) for k in range(2)]
        KRT2 = [KRT, A("a_KRT1", [64, L], BF16)]
        krtok2 = [krtok, A("a_krtok1", [128, NT, 64], BF16)]
        P.bankset = [5, 6]
        units = [(s, h) for s in range(cfg.NSEQ) for h in range(MLA_H)]
        itc = [0]

        def prologue(u):
            s, h = units[u]
            B = S2[u % 2]
            sfx = str(u % 2)
            if h == 0:
                ss_ = str(s % 2)
                load_tok(C, krtok2[s % 2], 'krtok' + ss_, D['KR'], s, 0, 64, lambda s_, t_: 'KR_%d_%d' % (s_, t_))
                transpose_tok(C, krtok2[s % 2], 'krtok' + ss_, 64, KRT2[s % 2], 'KRT' + ss_)
            load_tok(C, B['ktok'], 'ktok' + sfx, D['KN'], s, h * 128, (h + 1) * 128, lambda s_, t_: 'KN_%d_%d' % (s_, t_))
            load_tok(C, B['vtok'], 'vtok' + sfx, D['V'], s, h * 128, (h + 1) * 128, lambda s_, t_: 'V_%d_%d' % (s_, t_))
            load_tok(C, B['qntok'], 'qntok' + sfx, D['QN'], s, h * 128, (h + 1) * 128, lambda s_, t_: 'QN_%d_%d' % (s_, t_))
            load_tok(C, B['qrtok'], 'qrtok' + sfx, D['QR'], s, h * 64, (h + 1) * 64, lambda s_, t_: 'QR_%d_%d' % (s_, t_))
            transpose_tok(C, B['ktok'], 'ktok' + sfx, 128, B['KT'], 'KT' + sfx)
            transpose_tok(C, B['qntok'], 'qntok' + sfx, 128, B['QNT'], 'QNT' + sfx)
            transpose_tok(C, B['qrtok'], 'qrtok' + sfx, 64, B['QRT'], 'QRT' + sfx)

        def stageA(u, i, k):
            s, h = units[u]
            B = S2[u % 2]
            sfx = str(u % 2)
            KRTs, krr = KRT2[s % 2], 'KRT' + str(s % 2)
            r0, rows = cfg.tile(i)
            nkeys = r0 + rows
            ks = str(k)
            grp = [(bounds[g], min(bounds[g + 1], nkeys)) for g in range(len(bounds) - 1) if bounds[g] < nkeys]
            for gi, (g0, g1) in enumerate(grp):
                w = g1 - g0
                pr = 'ps%d' % gi
                psg = C.ps[gi][:rows, 0:w]
                _mm(P, psg, B['QNT'][:, r0:r0 + rows], B['KT'][:, g0:g1], True, False, ['QNT' + sfx, 'KT' + sfx], [pr])
                _mm(P, psg, B['QRT'][:64, r0:r0 + rows], KRTs[:64, g0:g1], False, True, ['QRT' + sfx, krr], [pr])
                if g1 == nkeys:
                    lo = r0 - g0
                    dg = C.ps[gi][:rows, lo:lo + rows]
                    _tt(P, 'dve', dg, dg, C.maskA[:rows, :rows], ALU.add, [pr, 'maskA'], [pr])
                _red(P, mxs[k][:rows, gi:gi + 1], psg, ALU.max, [pr], ['mxs' + ks])
            ng = len(grp)
            _red(P, mx[k][:rows, :], mxs[k][:rows, 0:ng], ALU.max, ['mxs' + ks], ['mx' + ks])
            _ts(P, 'dve', nb[k][:rows, :], mx[k][:rows, :], -ATT_SCALE, None, ALU.mult, None, ['mx' + ks], ['nb' + ks])
            for gi, (g0, g1) in enumerate(grp):
                w = g1 - g0
                pr = 'ps%d' % gi
                _act(P, Pb[k][:rows, g0:g1], C.ps[gi][:rows, 0:w], AF.Exp, [pr, 'nb' + ks], ['Pb' + ks, 'sums' + ks],
                     bias=nb[k][:rows, :], scale=ATT_SCALE, accum_out=sums[k][:rows, gi:gi + 1])
            _red(P, rsum[k][:rows, :], sums[k][:rows, 0:ng], ALU.add, ['sums' + ks], ['rsum' + ks])
            _recip(P, rinv[k][:rows, :], rsum[k][:rows, :], ['rsum' + ks], ['rinv' + ks])
            _ts(P, 'dve', Pb[k][:rows, 0:nkeys], Pb[k][:rows, 0:nkeys], rinv[k][:rows, :], None, ALU.mult, None,
                ['Pb' + ks, 'rinv' + ks], ['Pb' + ks])

        def stageB(u, i, k):
            s, h = units[u]
            B = S2[u % 2]
            sfx = str(u % 2)
            r0, rows = cfg.tile(i)
            ks = str(k)
            kt = list(range(i + 1))
            for q0 in range(0, len(kt), 4):
                g = kt[q0:q0 + 4]
                b = P.nextbank()
                for jj, jt in enumerate(g):
                    kr0, kn = cfg.tile(jt)
                    _tr(P, C.psb[b][:kn, jj * 128:jj * 128 + rows], Pb[k][:rows, kr0:kr0 + kn], C.idb[:rows, :rows],
                        ['Pb' + ks, 'idb'], ['ps%d' % b])
                srcv = C.psb[b][:, 0:len(g) * 128].rearrange("p (j c) -> p j c", c=128)[:, :, :rows]
                eng = 'act' if (C.evac_ctr % 2 == 0) else 'dve'
                C.evac_ctr += 1
                _cp(P, eng, PT[k][:, g[0]:g[0] + len(g), :rows], srcv, ['ps%d' % b], ['PT' + ks])
            bo = 7
            for jt in kt:
                kr0, kn = cfg.tile(jt)
                _mm(P, C.ps[bo][:, 0:rows], B['vtok'][:kn, jt, :], PT[k][:kn, jt, :rows], jt == 0, jt == i,
                    ['vtok' + sfx, 'PT' + ks], ['ps%d' % bo])
            _cp(P, 'act', B['OT'][:, r0:r0 + rows], C.ps[bo][:, 0:rows], ['ps%d' % bo], ['OT' + sfx])

        prologue(0)
        for u, (s, h) in enumerate(units):
            if u + 1 < len(units):
                prologue(u + 1)
            kk = [(itc[0] + i) % 2 for i in range(NT)]
            itc[0] += NT
            stageA(u, 0, kk[0])
            for i in range(1, NT):
                stageA(u, i, kk[i])
                stageB(u, i - 1, kk[i - 1])
            stageB(u, NT - 1, kk[NT - 1])
            _dma(P, 'sp', D['OT'][s, :, h, :], S2[u % 2]['OT'][:, :], ['OT' + str(u % 2)], ['OTd_%d' % s])
        P.bankset = None
        P.flush()


def mla_part3(C, H, j):
    nc, P, cfg, D = C.nc, C.P, C.cfg, C.D
    BC = cfg.BT * 128
    with ExitStack() as es:
        def A(name, shape, dt):
            return es.enter_context(nc.sbuf_tensor(_uname(name), shape, dt))
        oT = [A("p_oT%d" % k, [128, 16, BC], BF16) for k in range(2)]
        wb = [A("p_w%d" % k, [128, 8, 512], BF16) for k in range(3)]
        C.hres = [A("p_hr%d" % k, [128, 512], F32) for k in range(3)]
        epi = residual_epi(C, H, 1.0)
        for bi, blk in enumerate(cfg.blocks()):
            k = bi % 2
            for (s, t, r0, rows, col0) in blk:
                _dma(P, 'sp', oT[k][:, :, col0:col0 + rows], D['OT'][s, :, :, r0:r0 + rows], ['OTd_%d' % s], ['oT%d' % k])
            gemm_tm(C, oT[k], 'oT%d' % k, 16, blk, D['mla_wo'][j], cfg.DM, epi, wb, 'pw')
        P.flush()


def mla_phase(C, H, j, i):
    mla_part1(C, H, j, i)
    mla_part2(C, j)
    mla_part3(C, H, j)


def host_mla_layout(cfg, w_in, w_uq, w_ukv, w_o):
    nA = w_in.shape[0]
    KC = cfg.KC
    win_r = np.ascontiguousarray(w_in.reshape(nA, KC, 128, 1088).transpose(0, 2, 1, 3))
    q = w_uq.reshape(nA, 512, 16, 192)
    wuq = np.concatenate([q[..., :128].reshape(nA, 512, 2048), q[..., 128:].reshape(nA, 512, 1024)], axis=-1)
    wuq_r = np.ascontiguousarray(wuq.reshape(nA, 4, 128, 3072).transpose(0, 2, 1, 3))
    kv = w_ukv.reshape(nA, 512, 16, 256)
    wukv = np.concatenate([kv[..., :128].reshape(nA, 512, 2048), kv[..., 128:].reshape(nA, 512, 2048)], axis=-1)
    wukv_r = np.ascontiguousarray(wukv.reshape(nA, 4, 128, 4096).transpose(0, 2, 1, 3))
    wo_r = np.ascontiguousarray(w_o.reshape(nA, 16, 128, cfg.DM).transpose(0, 2, 1, 3))
    return win_r, wuq_r, wukv_r, wo_r


S_IN = 4096
S_CONV = 6144
S_PROJ = 10304
S_H = 64
import os as _os
DBG3 = int(_os.environ.get('DBG3', '9'))
DBGX = int(_os.environ.get('DBGX', '99'))


def ssm_decl(nc, cfg, D, din, nB):
    L, NSEQ = cfg.L, cfg.NSEQ
    D['ssm_win'] = din('ssm_win_r', [nB, 128, cfg.KC, S_PROJ])
    D['ssm_wout'] = din('ssm_wout_r', [nB, 128, 32, cfg.DM])
    D['ssm_cw'] = din('ssm_conv_w', [nB, 4, S_CONV])
    D['ssm_cb'] = din('ssm_conv_b', [nB, 1, S_CONV])
    D['ssm_dtb'] = din('ssm_dt_bias', [nB, 1, S_H])
    D['ssm_alog'] = din('ssm_a_log', [nB, 1, S_H])
    D['ssm_d'] = din('ssm_d', [nB, 1, S_H])
    D['ssm_nw'] = din('ssm_norm', [nB, 1, S_IN])
    D['c_maskS'] = din('c_maskS', [128, 128])
    D['ZX'] = nc.dram_tensor('s_ZX', [NSEQ, L, S_PROJ], F32, kind="Internal").ap()
    D['XC'] = nc.dram_tensor('s_XC', [NSEQ, L, S_CONV], F32, kind="Internal").ap()
    D['YT'] = nc.dram_tensor('s_YT', [NSEQ, 128, 32, L], BF16, kind="Internal").ap()


def ssm_part1(C, H, j, i):
    nc, P, cfg, D = C.nc, C.P, C.cfg, C.D
    DM, KC = cfg.DM, cfg.KC
    BC = cfg.BT * 128
    with ExitStack() as es:
        def A(name, shape, dt):
            return es.enter_context(nc.sbuf_tensor(_uname(name), shape, dt))
        gbc = A("s1_gbc", [128, DM], F32)
        uT = A("s1_uT", [128, KC, BC], BF16)
        C.hbuf = [A("s1_hb0", [128, DM], F32), A("s1_hb1", [128, DM], F32)]
        C.hn = [A("s1_hn0", [128, DM], BF16), A("s1_hn1", [128, DM], BF16)]
        C.sq = A("s1_sq", [128, DM], BF16)
        wb = [A("s1_w%d" % k, [128, 8, 512], BF16) for k in range(3)]
        fr = [A("s1_fr%d" % k, [128, 512], F32) for k in range(4)]
        ctr = [0]
        _dma(P, 'sp', gbc[:, :], D['norm_mix'][i].partition_broadcast(128), [], ['gbc'])

        def epi(n, cc, te, psap, psres):
            s, t, r0, rows, col0 = te
            k = ctr[0] % 4
            ctr[0] += 1
            w = cc[1] - cc[0]
            _cp(P, 'act' if k % 2 == 0 else 'dve', fr[k][:rows, 0:w], psap, [psres], ['fr%d' % k])
            _dma(P, 'sp', D['ZX'][s, r0:r0 + rows, cc[0]:cc[1]], fr[k][:rows, 0:w], ['fr%d' % k], ['ZX_%d_%d' % (s, t)])
        for blk in cfg.blocks():
            for ii, (s, t, r0, rows, col0) in enumerate(blk):
                norm_T(C, H[s, r0:r0 + rows, :], hres(s, t), rows, gbc, uT, 'uT', col0, ii)
            gemm_tm(C, uT, 'uT', KC, blk, D['ssm_win'][j], S_PROJ, epi, wb, 'sw')
        P.flush()


def ssm_part2(C, j):
    nc, P, cfg, D = C.nc, C.P, C.cfg, C.D
    CW = 1024
    with ExitStack() as es:
        def A(name, shape, dt):
            return es.enter_context(nc.sbuf_tensor(_uname(name), shape, dt))
        wbc = A("s2_wbc", [128, 4, CW], F32)
        bbc = A("s2_bbc", [128, CW], F32)
        xk = [[A("s2_x%d_%d" % (b, k), [128, CW], F32) for k in range(4)] for b in range(2)]
        pk = [[A("s2_p%d_%d" % (b, k), [128, CW], F32) for k in range(4)] for b in range(2)]
        oc = [A("s2_o%d" % b, [128, CW], F32) for b in range(2)]
        it = 0
        for c0 in range(0, S_CONV, CW):
            c1 = c0 + CW
            for k in range(4):
                _dma(P, 'sp', wbc[:, k, :], D['ssm_cw'][j, k:k + 1, c0:c1].partition_broadcast(128), [], ['wbc'])
            _dma(P, 'sp', bbc[:, :], D['ssm_cb'][j, :, c0:c1].partition_broadcast(128), [], ['bbc'])
            for s in range(cfg.NSEQ):
                for t in range(cfg.NT):
                    r0, rows = cfg.tile(t)
                    b = it % 2
                    it += 1
                    zr = ['ZX_%d_%d' % (s, t)] + (['ZX_%d_%d' % (s, t - 1)] if t >= 2 else []) + (['ZX_%d_0' % s] if t == 1 else [])
                    for k in range(4):
                        sh = 3 - k
                        xr = 'x%d_%d' % (b, k)
                        if r0 - sh < 0:
                            _memset(P, 'pool', xk[b][k][:rows, :], 0.0, [xr])
                            _dma(P, 'sp', xk[b][k][sh:rows, :], D['ZX'][s, 0:rows - sh, S_IN + c0:S_IN + c1], zr, [xr])
                        else:
                            _dma(P, 'sp', xk[b][k][:rows, :], D['ZX'][s, r0 - sh:r0 - sh + rows, S_IN + c0:S_IN + c1], zr, [xr])
                        _tt(P, 'pool' if k == 1 else 'dve', pk[b][k][:rows, :], xk[b][k][:rows, :], wbc[:rows, k, :], ALU.mult, [xr, 'wbc'], ['p%d_%d' % (b, k)])
                    _tt(P, 'dve', pk[b][0][:rows, :], pk[b][0][:rows, :], pk[b][1][:rows, :], ALU.add, ['p%d_0' % b, 'p%d_1' % b], ['p%d_0' % b])
                    _tt(P, 'dve', pk[b][2][:rows, :], pk[b][2][:rows, :], pk[b][3][:rows, :], ALU.add, ['p%d_2' % b, 'p%d_3' % b], ['p%d_2' % b])
                    _tt(P, 'dve', pk[b][0][:rows, :], pk[b][0][:rows, :], pk[b][2][:rows, :], ALU.add, ['p%d_0' % b, 'p%d_2' % b], ['p%d_0' % b])
                    _tt(P, 'dve', pk[b][0][:rows, :], pk[b][0][:rows, :], bbc[:rows, :], ALU.add, ['p%d_0' % b, 'bbc'], ['p%d_0' % b])
                    _act(P, oc[b][:rows, :], pk[b][0][:rows, :], AF.Silu, ['p%d_0' % b], ['oc%d' % b])
                    _dma(P, 'act', D['XC'][s, r0:r0 + rows, c0:c1], oc[b][:rows, :], ['oc%d' % b], ['XC_%d_%d' % (s, t)])
        P.flush()


def ssm_part3(C, j):
    nc, P, cfg, D = C.nc, C.P, C.cfg, C.D
    with ExitStack() as es:
        def A(name, shape, dt):
            return es.enter_context(nc.sbuf_tensor(_uname(name), shape, dt))
        ST = A("s3_ST", [128, S_IN], F32)
        Sbf = A("s3_Sbf", [128, S_IN], BF16)
        nwbc = A("s3_nw", [128, S_IN], F32)
        xs_2 = [A("s3_xs0", [128, S_IN], F32), A("s3_xs1", [128, S_IN], F32)]
        zt_2 = [A("s3_zt0", [128, S_IN], F32), A("s3_zt1", [128, S_IN], F32)]
        bcf_2 = [A("s3_bcf0", [128, 2048], F32), A("s3_bcf1", [128, 2048], F32)]
        bcb = A("s3_bcb", [128, 2048], BF16)
        BCT = A("s3_BCT", [128, 16, 128], BF16)
        xdt = A("s3_xdt", [128, S_IN], BF16)
        xw = A("s3_xw", [128, S_IN], BF16)
        ysb = A("s3_ysb", [128, S_IN], F32)
        ynb = A("s3_ynb", [128, S_IN], BF16)
        YTt = A("s3_YTt", [128, 32, 128], BF16)
        E = [A("s3_E%d" % k, [128, 8 * 128], BF16) for k in range(2)]
        MT = [A("s3_MT%d" % k, [128, 8 * 128], BF16) for k in range(2)]
        rhsg = [A("s3_rg%d" % k, [128, 8 * 128], F32) for k in range(2)]
        CBT = [A("s3_CBT%d" % k, [128, 128], BF16) for k in range(2)]
        t1 = [A("s3_t1%d" % k, [128, 512], F32) for k in range(2)]
        mrep = A("s3_mrep", [128, 8 * 128], BF16)
        mrep16 = A("s3_mrep16", [128, 8 * 16], BF16)
        maskS = A("s3_maskS", [128, 128], F32)
        sm = {}
        for nm in ('dtb', 'abc', 'dbc', 'dtr', 'dt', 'dta', 'acs', 'nacs', 'expA', 'decayc', 'wq', 'tmpw', 'ssg', 'rsg', 'rstdg'):
            sm[nm] = A("s3_" + nm, [128, 64], F32)
        acsT = A("s3_acsT", [128, 128], F32)
        dtaP = A("s3_dtaP", [128, 128], F32)
        _dma(P, 'sp', nwbc[:, :], D['ssm_nw'][j].partition_broadcast(128), [], ['nwbc'])
        _dma(P, 'sp', sm['dtb'][:, :], D['ssm_dtb'][j].partition_broadcast(128), [], ['dtb'])
        _dma(P, 'sp', sm['abc'][:, :], D['ssm_alog'][j].partition_broadcast(128), [], ['abc'])
        _dma(P, 'sp', sm['dbc'][:, :], D['ssm_d'][j].partition_broadcast(128), [], ['dbc'])
        _dma(P, 'sp', maskS[:, :], D['c_maskS'], [], ['maskS'])
        _act(P, sm['abc'][:, :], sm['abc'][:, :], AF.Exp, ['abc'], ['abc'])
        _ts(P, 'dve', sm['abc'][:, :], sm['abc'][:, :], -1.0, None, ALU.mult, None, ['abc'], ['abc'])
        _cp(P, 'dve', mrep[:, :].rearrange("p (r l) -> p r l", l=128), maskS[:, :].unsqueeze(1).to_broadcast([128, 8, 128]), ['maskS'], ['mrep'])
        _memset(P, 'dve', mrep16[:, :], 0.0, ['mrep16'])
        _cp(P, 'dve', mrep16[:16, :].rearrange("p (r l) -> p r l", l=16), maskS[:16, :16].unsqueeze(1).to_broadcast([16, 8, 16]), ['maskS'], ['mrep16'])
        it = 0
        P.bankset = [0, 1]
        for s in range(cfg.NSEQ):
            _memset(P, 'pool', ST[:, :], 0.0, ['ST%d' % g for g in range(8)])
            _memset(P, 'pool', Sbf[:, :], 0.0, ['Sbf%d' % g for g in range(8)])
            _memset(P, 'pool', dtaP[:, :], 0.0, ['dta'])
            for t in range(cfg.NT):
                r0, rows = cfg.tile(t)
                R = rows
                mr = mrep if R == 128 else mrep16
                mrr = 'mrep' if R == 128 else 'mrep16'
                tp = (s * cfg.NT + t) % 2
                xs, zt, bcf = xs_2[tp], zt_2[tp], bcf_2[tp]
                XS, ZT, BCF = 'xs%d' % tp, 'zt%d' % tp, 'bcf%d' % tp
                xcr, zxr = 'XC_%d_%d' % (s, t), 'ZX_%d_%d' % (s, t)
                _dma(P, 'sp', xs[:R, :], D['XC'][s, r0:r0 + R, 0:S_IN], [xcr], [XS])
                _dma(P, 'sp', bcf[:R, :], D['XC'][s, r0:r0 + R, S_IN:S_CONV], [xcr], [BCF])
                _dma(P, 'sp', zt[:R, :], D['ZX'][s, r0:r0 + R, 0:S_IN], [zxr], [ZT])
                _dma(P, 'sp', sm['dtr'][:R, :], D['ZX'][s, r0:r0 + R, S_IN + S_CONV:S_PROJ], [zxr], ['dtr'])
                _tt(P, 'dve', sm['dt'][:R, :], sm['dtr'][:R, :], sm['dtb'][:R, :], ALU.add, ['dtr', 'dtb'], ['dt'])
                _act(P, sm['dt'][:R, :], sm['dt'][:R, :], AF.Exp, ['dt'], ['dt'])
                _act(P, sm['dt'][:R, :], sm['dt'][:R, :], AF.Ln, ['dt'], ['dt'], bias=1.0)
                _tt(P, 'dve', dtaP[:R, 0:64], sm['dt'][:R, :], sm['abc'][:R, :], ALU.mult, ['dt', 'abc'], ['dta'])
                bq = P.nextbank()
                pq, pqr = C.ps[bq], 'ps%d' % bq
                _stmts = []
                _stmts.append(lambda: _mm(P, pq[:, 0:64], C.tri[:, :], dtaP[:, 0:64], True, True, ['tri', 'dta'], [pqr]))
                _stmts.append(lambda: _mm(P, pq[:, 64:64 + R], dtaP[:, :], C.tri[:, :R], True, True, ['tri', 'dta'], [pqr]))
                _stmts.append(lambda: _mm(P, pq[:, 256:320], C.onesf[:, :], dtaP[:, 0:64], True, True, ['onesf', 'dta'], [pqr]))
                _stmts.append(lambda: _mm(P, pq[:, 480:496], C.idb[:, :], C.idb[:, 0:16], True, True, ['idb'], [pqr]))
                _stmts.append(lambda: _cp(P, 'act', sm['acs'][:R, :], pq[:R, 0:64], [pqr], ['acs']))
                _stmts.append(lambda: (_ts(P, 'dve', sm['nacs'][:R, :], pq[:R, 0:64], -1.0, None, ALU.mult, None, [pqr], ['nacs']) if _os.environ.get('NACS_PSUM') else _ts(P, 'dve', sm['nacs'][:R, :], sm['acs'][:R, :], -1.0, None, ALU.mult, None, ['acs'], ['nacs'])))
                _stmts.append(lambda: _cp(P, 'act', acsT[:, :R], pq[:, 64:64 + R], [pqr], ['acsT']))
                _stmts.append(lambda: _act(P, sm['expA'][:R, :], pq[:R, 0:64], AF.Exp, [pqr], ['expA']))
                _stmts.append(lambda: _act(P, sm['decayc'][:, :], pq[:, 256:320], AF.Exp, [pqr], ['decayc']))
                _stmts.append(lambda: _cp(P, 'act', sm['ssg'][:R, :], pq[:R, 256:320], [pqr], ['ssg']))
                _stmts.append(lambda: _tt(P, 'dve', sm['tmpw'][:R, :], sm['ssg'][:R, :], sm['acs'][:R, :], ALU.subtract, ['ssg', 'acs'], ['tmpw']))
                _stmts.append(lambda: _act(P, sm['tmpw'][:R, :], sm['tmpw'][:R, :], AF.Exp, ['tmpw'], ['tmpw']))
                _stmts.append(lambda: _tt(P, 'dve', sm['wq'][:R, :], sm['tmpw'][:R, :], sm['dt'][:R, :], ALU.mult, ['tmpw', 'dt'], ['wq']))
                for _f in _stmts[:DBGX]:
                    _f()
                xs3 = xs[:R, :].rearrange("p (h d) -> p h d", d=64)
                _tt(P, 'dve', xdt[:R, :].rearrange("p (h d) -> p h d", d=64), xs3,
                    sm['dt'][:R, :].unsqueeze(2).to_broadcast([R, 64, 64]), ALU.mult, [XS, 'dt'], ['xdt'])
                _tt(P, 'dve', xw[:R, :].rearrange("p (h d) -> p h d", d=64), xs3,
                    sm['wq'][:R, :].unsqueeze(2).to_broadcast([R, 64, 64]), ALU.mult, [XS, 'wq'], ['xw'])
                _cp(P, 'act', bcb[:R, :], bcf[:R, :], [BCF], ['bcb'])
                transpose_into(C, bcb, 'bcb', R, 16, BCT, 'BCT', 0)
                def stageA(g, k):
                    ks = str(k)
                    _mm(P, C.ps[4][:R, 0:R], BCT[:, g, :R], BCT[:, 8 + g, :R], True, True, ['BCT'], ['ps4'])
                    _cp(P, 'act', CBT[k][:R, :R], C.ps[4][:R, 0:R], ['ps4'], ['CBT' + ks])
                    rg3 = rhsg[k][:, 0:8 * R].rearrange("p (r l) -> p r l", l=R)
                    _tt(P, 'pool', rg3, acsT[:, :R].unsqueeze(1).to_broadcast([128, 8, R]),
                        C.idf[:, 8 * g:8 * g + 8].unsqueeze(2).to_broadcast([128, 8, R]), ALU.mult, ['acsT', 'idf'], ['rhsg' + ks])
                    for half in range(2):
                        bb = 2 + half
                        pb_, pbr = C.ps[bb], 'ps%d' % bb
                        _mm(P, pb_[:, 0:4 * R], C.onesf[:, :], rhsg[k][:, half * 4 * R:(half + 1) * 4 * R], True, False,
                            ['onesf', 'rhsg' + ks], [pbr])
                        _mm(P, pb_[:, 0:4 * R], C.idb[:, :], mr[:, 0:4 * R], False, True, ['idb', mrr], [pbr])
                        for rr in range(4):
                            r_ = half * 4 + rr
                            hh = 8 * g + r_
                            _act(P, E[k][:R, r_ * R:(r_ + 1) * R], pb_[:R, rr * R:(rr + 1) * R], AF.Exp, [pbr, 'nacs'], ['E' + ks],
                                 bias=sm['nacs'][:R, hh:hh + 1], scale=1.0)
                    _tt(P, 'dve', MT[k][:R, 0:8 * R].rearrange("p (r l) -> p r l", l=R),
                        E[k][:R, 0:8 * R].rearrange("p (r l) -> p r l", l=R),
                        CBT[k][:R, :R].unsqueeze(1).to_broadcast([R, 8, R]), ALU.mult, ['E' + ks, 'CBT' + ks], ['MT' + ks])

                def stageB(g, k):
                    ks = str(k)
                    by, bo, bs = 5, 6, 7
                    for r_ in range(8):
                        hh = 8 * g + r_
                        _mm(P, C.ps[by][:R, r_ * 64:(r_ + 1) * 64], MT[k][:R, r_ * R:(r_ + 1) * R], xdt[:R, hh * 64:(hh + 1) * 64],
                            True, True, ['MT' + ks, 'xdt'], ['ps%d' % by])
                    _mm(P, C.ps[bo][:R, 0:512], BCT[:, 8 + g, :R], Sbf[:, g * 512:(g + 1) * 512], True, True,
                        ['BCT', 'Sbf%d' % g], ['ps%d' % bo])
                    _tt(P, 'dve', t1[k][:R, :].rearrange("p (r d) -> p r d", d=64),
                        C.ps[bo][:R, 0:512].rearrange("p (r d) -> p r d", d=64),
                        sm['expA'][:R, 8 * g:8 * g + 8].unsqueeze(2).to_broadcast([R, 8, 64]), ALU.mult,
                        ['ps%d' % bo, 'expA'], ['t1' + ks])
                    _tt(P, 'dve', ysb[:R, g * 512:(g + 1) * 512], C.ps[by][:R, 0:512], t1[k][:R, :], ALU.add,
                        ['ps%d' % by, 't1' + ks], ['ysb%d' % g])
                    _mm(P, C.ps[bs][:, 0:512], bcb[:R, g * 128:(g + 1) * 128], xw[:R, g * 512:(g + 1) * 512], True, True,
                        ['bcb', 'xw'], ['ps%d' % bs])
                    stg = ST[:, g * 512:(g + 1) * 512]
                    _tt(P, 'pool', stg.rearrange("p (r d) -> p r d", d=64), stg.rearrange("p (r d) -> p r d", d=64),
                        sm['decayc'][:, 8 * g:8 * g + 8].unsqueeze(2).to_broadcast([128, 8, 64]), ALU.mult,
                        ['ST%d' % g, 'decayc'], ['ST%d' % g])
                    _tt(P, 'dve', stg, stg, C.ps[bs][:, 0:512], ALU.add, ['ST%d' % g, 'ps%d' % bs], ['ST%d' % g])
                    _cp(P, 'pool', Sbf[:, g * 512:(g + 1) * 512], stg, ['ST%d' % g], ['Sbf%d' % g])

                kk = [(it + g) % 2 for g in range(8)]
                it += 8
                stageA(0, kk[0])
                for g in range(1, 8):
                    stageA(g, kk[g])
                    stageB(g - 1, kk[g - 1])
                stageB(7, kk[7])
                YS = ['ysb%d' % g for g in range(8)]
                _tt(P, 'dve', xs3, xs3, sm['dbc'][:R, :].unsqueeze(2).to_broadcast([R, 64, 64]), ALU.mult, [XS, 'dbc'], [XS])
                _tt(P, 'dve', ysb[:R, :], ysb[:R, :], xs[:R, :], ALU.add, YS + [XS], YS)
                _act(P, zt[:R, :], zt[:R, :], AF.Silu, [ZT], [ZT])
                _tt(P, 'dve', ysb[:R, :], ysb[:R, :], zt[:R, :], ALU.mult, YS + [ZT], YS)
                for g in range(8):
                    _act(P, ynb[:R, g * 512:(g + 1) * 512], ysb[:R, g * 512:(g + 1) * 512], AF.Square, YS, ['ynb', 'ssg'],
                         accum_out=sm['ssg'][:R, g:g + 1])
                _act(P, sm['rsg'][:R, 0:8], sm['ssg'][:R, 0:8], AF.Sqrt, ['ssg'], ['rsg'], scale=1.0 / 512, bias=C.epsb[:R, :])
                _recip(P, sm['rstdg'][:R, 0:8], sm['rsg'][:R, 0:8], ['rsg'], ['rstdg'])
                ys3 = ysb[:R, :].rearrange("p (g c) -> p g c", c=512)
                _tt(P, 'dve', ys3, ys3, sm['rstdg'][:R, 0:8].unsqueeze(2).to_broadcast([R, 8, 512]), ALU.mult, YS + ['rstdg'], YS)
                _tt(P, 'dve', ynb[:R, :], ysb[:R, :], nwbc[:R, :], ALU.mult, YS + ['nwbc'], ['ynb'])
                transpose_into(C, ynb, 'ynb', R, 32, YTt, 'YTt', 0)
                _dma(P, 'sp', D['YT'][s, :, :, r0:r0 + R], YTt[:, :, :R], ['YTt'], ['YTd_%d' % s])
        P.bankset = None
        P.flush()


def ssm_part4(C, H, j):
    nc, P, cfg, D = C.nc, C.P, C.cfg, C.D
    BC = cfg.BT * 128
    with ExitStack() as es:
        def A(name, shape, dt):
            return es.enter_context(nc.sbuf_tensor(_uname(name), shape, dt))
        yT = [A("s4_yT%d" % k, [128, 32, BC], BF16) for k in range(2)]
        wb = [A("s4_w%d" % k, [128, 8, 512], BF16) for k in range(3)]
        C.hres = [A("s4_hr%d" % k, [128, 512], F32) for k in range(3)]
        epi = residual_epi(C, H, 1.0)
        for bi, blk in enumerate(cfg.blocks()):
            k = bi % 2
            for (s, t, r0, rows, col0) in blk:
                _dma(P, 'sp', yT[k][:, :, col0:col0 + rows], D['YT'][s, :, :, r0:r0 + rows], ['YTd_%d' % s], ['yT%d' % k])
            gemm_tm(C, yT[k], 'yT%d' % k, 32, blk, D['ssm_wout'][j], cfg.DM, epi, wb, 'ow')
        P.flush()


def ssm_phase(C, H, j, i):
    ssm_part1(C, H, j, i)
    ssm_part2(C, j)
    ssm_part3(C, j)
    ssm_part4(C, H, j)


def host_ssm_layout(cfg, w_in, w_out):
    nB = w_in.shape[0]
    win_r = np.ascontiguousarray(w_in.reshape(nB, cfg.KC, 128, S_PROJ).transpose(0, 2, 1, 3))
    wout_r = np.ascontiguousarray(w_out.reshape(nB, 32, 128, cfg.DM).transpose(0, 2, 1, 3))
    return win_r, wout_r


_NC_CACHE = {}


def kernel(x, meta_tokens, norm_ffn1, ffn1_w_in, ffn1_w_out, norm_mix, norm_ffn2, ffn2_w_in, ffn2_w_out,
           mla_w_in, mla_q_norm, mla_w_uq, mla_kv_norm, mla_w_ukv, mla_w_o,
           ssm_w_in, ssm_conv_w, ssm_conv_b, ssm_dt_bias, ssm_a_log, ssm_d, ssm_norm, ssm_w_out,
           final_norm):
    from concourse.bass_utils import run_bass_kernel_spmd
    f = lambda a: np.ascontiguousarray(np.asarray(a, dtype=np.float32))
    x = f(x)
    B, SEQ, DM = x.shape
    NCORES = 8
    NSEQ = B // NCORES
    DEPTH = norm_ffn1.shape[0]
    cfg = Cfg(SEQ=SEQ, NSEQ=NSEQ, DEPTH=DEPTH, DM=DM, DFF=ffn1_w_out.shape[1])
    key = (SEQ, NSEQ, DEPTH, DM, cfg.DFF)
    if key not in _NC_CACHE:
        _NC_CACHE[key] = build(cfg)
    nc = _NC_CACHE[key]
    shared = {}
    shared['meta_tokens'] = f(meta_tokens)
    shared['norm_ffn1'] = f(norm_ffn1).reshape(DEPTH, 1, DM)
    shared['norm_mix'] = f(norm_mix).reshape(DEPTH, 1, DM)
    shared['norm_ffn2'] = f(norm_ffn2).reshape(DEPTH, 1, DM)
    shared['final_norm'] = f(final_norm).reshape(1, DM)
    shared['ffn1_win_r'], shared['ffn1_wout_r'] = host_ffn_layout(cfg, f(ffn1_w_in), f(ffn1_w_out))
    shared['ffn2_win_r'], shared['ffn2_wout_r'] = host_ffn_layout(cfg, f(ffn2_w_in), f(ffn2_w_out))
    (shared['mla_win_r'], shared['mla_wuq_r'], shared['mla_wukv_r'], shared['mla_wo_r']) = host_mla_layout(
        cfg, f(mla_w_in), f(mla_w_uq), f(mla_w_ukv), f(mla_w_o))
    nA = mla_w_in.shape[0]
    shared['mla_q_norm'] = f(mla_q_norm).reshape(nA, 1, 512)
    shared['mla_kv_norm'] = f(mla_kv_norm).reshape(nA, 1, 512)
    shared['ssm_win_r'], shared['ssm_wout_r'] = host_ssm_layout(cfg, f(ssm_w_in), f(ssm_w_out))
    nB = ssm_w_in.shape[0]
    shared['ssm_conv_w'] = f(ssm_conv_w)
    shared['ssm_conv_b'] = f(ssm_conv_b).reshape(nB, 1, S_CONV)
    shared['ssm_dt_bias'] = f(ssm_dt_bias).reshape(nB, 1, S_H)
    shared['ssm_a_log'] = f(ssm_a_log).reshape(nB, 1, S_H)
    shared['ssm_d'] = f(ssm_d).reshape(nB, 1, S_H)
    shared['ssm_norm'] = f(ssm_norm).reshape(nB, 1, S_IN)
    shared.update(host_consts(cfg))
    in_maps = []
    for c in range(NCORES):
        m = dict(shared)
        m['x'] = x[c * NSEQ:(c + 1) * NSEQ]
        in_maps.append(m)
    res = run_bass_kernel_spmd(nc, in_maps, core_ids=list(range(NCORES)))
    out = np.concatenate([res.results[c]['out'] for c in range(NCORES)], axis=0)
    return out.astype(np.float32)
```
